# Optimizing a Trainium2 kernel written in Bass

```python
import jax
import jax.numpy as jnp
from jax import lax
import numpy as np

D_MODEL = 4096
BATCH = 2
SEQ = 4096
DEPTH = 2

CHUNK = 64
RET_HEADS = 4
RET_DK = 128
RET_DV = 256
GLA_HEADS = 4
GLA_DK = 128
GLA_DV = 256
GLA_RANK = 16
GLA_TAU = 16.0
DN_HEADS = 16
DN_DK = 128
DN_DV = 128
CONV_K = 4

ROPE_BASE = 10000.0
EPS = 1e-6
LN_EPS = 1e-5
DEEPNORM_ALPHA = float((2 * DEPTH) ** 0.25)
DEEPNORM_BETA = float((8 * DEPTH) ** -0.25)

D_MIX = RET_HEADS * RET_DV + GLA_HEADS * GLA_DV + DN_HEADS * DN_DV
DN_QKV = 2 * DN_HEADS * DN_DK + DN_HEADS * DN_DV
SPLITS = (RET_HEADS * RET_DK, RET_HEADS * RET_DK, RET_HEADS * RET_DV, RET_HEADS * RET_DV,
          GLA_HEADS * GLA_DK, GLA_HEADS * GLA_DK, GLA_HEADS * GLA_DV, GLA_HEADS * GLA_DV, GLA_RANK,
          DN_QKV, DN_HEADS * DN_DV, DN_HEADS, DN_HEADS)
SPLIT_POINTS = tuple(int(v) for v in np.cumsum(SPLITS)[:-1])
D_IN = int(sum(SPLITS))

kernel_name = 'hybrid_retention_gla_gdn_deepnorm'


def to_chunks(t):
    b, s, h, d = t.shape
    return t.reshape(b, s // CHUNK, CHUNK, h, d).transpose(1, 0, 3, 2, 4)


def from_chunks(t):
    n, b, h, c, d = t.shape
    return t.transpose(1, 0, 3, 2, 4).reshape(b, n * c, h, d)


def rotary(t, positions):
    half = t.shape[-1] // 2
    inv = ROPE_BASE ** (-jnp.arange(half, dtype=jnp.float32) / half)
    ang = positions.astype(jnp.float32)[..., None] * inv
    cos = jnp.cos(ang)[:, :, None, :]
    sin = jnp.sin(ang)[:, :, None, :]
    t1, t2 = t[..., :half], t[..., half:]
    return jnp.concatenate([t1 * cos - t2 * sin, t1 * sin + t2 * cos], axis=-1)


def head_layernorm(t):
    mu = jnp.mean(t, axis=-1, keepdims=True)
    var = jnp.mean(jnp.square(t - mu), axis=-1, keepdims=True)
    return (t - mu) * lax.rsqrt(var + EPS)


def head_rmsnorm(t):
    return t * lax.rsqrt(jnp.mean(jnp.square(t), axis=-1, keepdims=True) + EPS)


def l2norm(t):
    return t * lax.rsqrt(jnp.sum(jnp.square(t), axis=-1, keepdims=True) + EPS)


def layernorm(t, g, b):
    t = t.astype(jnp.float32)
    mu = jnp.mean(t, axis=-1, keepdims=True)
    var = jnp.mean(jnp.square(t - mu), axis=-1, keepdims=True)
    return (t - mu) * lax.rsqrt(var + LN_EPS) * g + b


def causal_depthwise_conv(u, w):
    taps, t = w.shape[0], u.shape[1]
    up = jnp.pad(u, ((0, 0), (taps - 1, 0), (0, 0)))
    out = up[:, 0:t] * w[0]
    for i in range(1, taps):
        out = out + up[:, i:i + t] * w[i]
    return out


def retention_chunkwise(q, k, v):
    b, _, h, dk = q.shape
    dv = v.shape[-1]
    log_gamma = jnp.log(1.0 - jnp.power(2.0, -5.0 - jnp.arange(h, dtype=jnp.float32)))
    i = jnp.arange(CHUNK, dtype=jnp.float32)
    diff = i[:, None] - i[None, :]
    causal = diff >= 0
    dmask = jnp.where(causal, jnp.exp(jnp.where(causal, diff, 0.0)[None] * log_gamma[:, None, None]), 0.0)
    xi = jnp.exp((i[None, :] + 1.0) * log_gamma[:, None])[..., None]
    zeta = jnp.exp((CHUNK - 1.0 - i[None, :]) * log_gamma[:, None])[..., None]
    g_chunk = jnp.exp(CHUNK * log_gamma)[:, None, None]

    def step(S, inp):
        q_c, k_c, v_c = inp
        scores = jnp.einsum('bhik,bhjk->bhij', q_c, k_c) * dmask
        o = jnp.einsum('bhij,bhjv->bhiv', scores, v_c) + jnp.einsum('bhik,bhkv->bhiv', q_c, S) * xi
        S = g_chunk * S + jnp.einsum('bhjk,bhjv->bhkv', k_c * zeta, v_c)
        return S, o

    S0 = jnp.zeros((b, h, dk, dv), jnp.float32)
    _, o = lax.scan(step, S0, (to_chunks(q), to_chunks(k), to_chunks(v)))
    return from_chunks(o)


def gla_chunkwise(q, k, v, log_a):
    b, _, h, dk = q.shape
    dv = v.shape[-1]
    causal = jnp.tril(jnp.ones((CHUNK, CHUNK), dtype=bool))

    def step(S, inp):
        q_c, k_c, v_c, a_c = inp
        b_c = jnp.cumsum(a_c, axis=2)
        b_last = b_c[:, :, -1:, :]
        diff = b_c[:, :, :, None, :] - b_c[:, :, None, :, :]
        decay = jnp.exp(jnp.where(causal[:, :, None], diff, -jnp.inf))
        scores = jnp.einsum('bhik,bhjk,bhijk->bhij', q_c, k_c, decay)
        o = jnp.einsum('bhij,bhjv->bhiv', scores, v_c) + jnp.einsum('bhik,bhkv->bhiv', q_c * jnp.exp(b_c), S)
        S = jnp.exp(b_last[:, :, 0, :])[..., None] * S + jnp.einsum('bhjk,bhjv->bhkv', k_c * jnp.exp(b_last - b_c), v_c)
        return S, o

    S0 = jnp.zeros((b, h, dk, dv), jnp.float32)
    _, o = lax.scan(step, S0, (to_chunks(q), to_chunks(k), to_chunks(v), to_chunks(log_a)))
    return from_chunks(o)


def gated_delta_chunkwise(q, k, v, beta, g):
    b, _, h, dk = q.shape
    dv = v.shape[-1]
    qc, kc, vc = to_chunks(q), to_chunks(k), to_chunks(v)
    bc = to_chunks(beta[..., None])[..., 0]
    gc = jnp.cumsum(to_chunks(g[..., None])[..., 0], axis=-1)
    lower = jnp.tril(jnp.ones((CHUNK, CHUNK), dtype=bool))
    strict = jnp.tril(jnp.ones((CHUNK, CHUNK), dtype=bool), k=-1)
    decay = jnp.exp(jnp.where(lower, gc[..., :, None] - gc[..., None, :], -jnp.inf))
    kk = jnp.einsum('nbhik,nbhjk->nbhij', kc, kc)
    a_strict = jnp.where(strict, bc[..., :, None] * kk * decay, 0.0)
    m = a_strict + jnp.eye(CHUNK, dtype=jnp.float32)
    u = lax.linalg.triangular_solve(m, vc * bc[..., None], left_side=True, lower=True, unit_diagonal=True)
    w = lax.linalg.triangular_solve(m, kc * (bc * jnp.exp(gc))[..., None], left_side=True, lower=True, unit_diagonal=True)
    qk = jnp.einsum('nbhik,nbhjk->nbhij', qc, kc) * decay

    def step(S, inp):
        q_c, k_c, u_c, w_c, g_c, qk_c = inp
        v_new = u_c - jnp.einsum('bhik,bhkv->bhiv', w_c, S)
        o = jnp.einsum('bhik,bhkv->bhiv', q_c * jnp.exp(g_c)[..., None], S) + jnp.einsum('bhij,bhjv->bhiv', qk_c, v_new)
        g_last = g_c[..., -1:]
        S = jnp.exp(g_last)[..., None] * S + jnp.einsum('bhjk,bhjv->bhkv', k_c * jnp.exp(g_last - g_c)[..., None], v_new)
        return S, o

    S0 = jnp.zeros((b, h, dk, dv), jnp.float32)
    _, o = lax.scan(step, S0, (qc, kc, u, w, gc, qk))
    return from_chunks(o)


def hybrid_layer(x, positions, w_in, gla_w_a2, gla_b_a, dn_conv_w, dn_a_log, dn_dt_bias,
                 ret_norm_g, gla_norm_g, dn_norm_g, w_out, ln_g, ln_b):
    b, s, _ = x.shape
    proj = jnp.einsum('btd,de->bte', x, w_in).astype(jnp.float32)
    (rq, rk, rv, rg, gq, gk, gv, gg, ga1, dqkv, dg, dbeta, da) = jnp.split(proj, SPLIT_POINTS, axis=-1)

    rq = rotary(rq.reshape(b, s, RET_HEADS, RET_DK), positions) * RET_DK ** -0.5
    rk = rotary(rk.reshape(b, s, RET_HEADS, RET_DK), positions)
    ro = retention_chunkwise(rq, rk, rv.reshape(b, s, RET_HEADS, RET_DV))
    ro = head_layernorm(ro) * ret_norm_g.reshape(RET_HEADS, RET_DV)
    ro = ro.reshape(b, s, -1) * jax.nn.silu(rg)

    log_a = jax.nn.log_sigmoid(jnp.einsum('btr,rk->btk', ga1, gla_w_a2) + gla_b_a) / GLA_TAU
    go = gla_chunkwise(gq.reshape(b, s, GLA_HEADS, GLA_DK) * GLA_DK ** -0.5,
                       gk.reshape(b, s, GLA_HEADS, GLA_DK),
                       gv.reshape(b, s, GLA_HEADS, GLA_DV),
                       log_a.reshape(b, s, GLA_HEADS, GLA_DK))
    go = head_rmsnorm(go) * gla_norm_g.reshape(GLA_HEADS, GLA_DV)
    go = go.reshape(b, s, -1) * jax.nn.silu(gg)

    dqkv = jax.nn.silu(causal_depthwise_conv(dqkv, dn_conv_w))
    dq, dk, dv = jnp.split(dqkv, (DN_HEADS * DN_DK, 2 * DN_HEADS * DN_DK), axis=-1)
    dq = l2norm(dq.reshape(b, s, DN_HEADS, DN_DK)) * DN_DK ** -0.5
    dk = l2norm(dk.reshape(b, s, DN_HEADS, DN_DK))
    beta = jax.nn.sigmoid(dbeta)
    g = -jnp.exp(dn_a_log.astype(jnp.float32)) * jax.nn.softplus(da + dn_dt_bias)
    do = gated_delta_chunkwise(dq, dk, dv.reshape(b, s, DN_HEADS, DN_DV), beta, g)
    do = head_rmsnorm(do) * dn_norm_g.reshape(DN_HEADS, DN_DV)
    do = do.reshape(b, s, -1) * jax.nn.silu(dg)

    mix = jnp.concatenate([ro, go, do], axis=-1).astype(x.dtype)
    y = jnp.einsum('bte,ed->btd', mix, w_out)
    return layernorm(DEEPNORM_ALPHA * x + y, ln_g, ln_b).astype(x.dtype)


def setup_inputs(seed: int = 0) -> dict:
    key = jax.random.key(seed)
    ks = jax.random.split(key, 16)
    x = jax.random.normal(ks[0], (BATCH, SEQ, D_MODEL), jnp.float32)
    positions = (jnp.arange(SEQ, dtype=jnp.int32)[None, :]
                 + jax.random.randint(ks[1], (BATCH, 1), 0, 1024, dtype=jnp.int32))
    starts = np.concatenate([[0], np.cumsum(SPLITS)]).astype(np.int64)
    col_scale = np.ones((D_IN,), np.float32)
    col_scale[starts[2]:starts[3]] = DEEPNORM_BETA
    col_scale[starts[6]:starts[7]] = DEEPNORM_BETA
    col_scale[starts[10] - DN_HEADS * DN_DV:starts[10]] = DEEPNORM_BETA
    w_in = (jax.random.normal(ks[2], (DEPTH, D_MODEL, D_IN), jnp.float32)
            * (D_MODEL ** -0.5) * jnp.asarray(col_scale))
    gla_w_a2 = jax.random.normal(ks[3], (DEPTH, GLA_RANK, GLA_HEADS * GLA_DK), jnp.float32) * GLA_RANK ** -0.5
    gla_b_a = 0.1 * jax.random.normal(ks[4], (DEPTH, GLA_HEADS * GLA_DK), jnp.float32)
    dn_conv_w = jax.random.normal(ks[5], (DEPTH, CONV_K, DN_QKV), jnp.float32) * CONV_K ** -0.5
    dn_a_log = jnp.log(jax.random.uniform(ks[6], (DEPTH, DN_HEADS), jnp.float32, 1.0, 16.0))
    dt = jnp.exp(jax.random.uniform(ks[7], (DEPTH, DN_HEADS), jnp.float32,
                                    float(np.log(1e-3)), float(np.log(1e-1))))
    dn_dt_bias = dt + jnp.log(-jnp.expm1(-dt))
    ret_norm_g = 1.0 + 0.02 * jax.random.normal(ks[8], (DEPTH, RET_HEADS * RET_DV), jnp.float32)
    gla_norm_g = 1.0 + 0.02 * jax.random.normal(ks[9], (DEPTH, GLA_HEADS * GLA_DV), jnp.float32)
    dn_norm_g = 1.0 + 0.02 * jax.random.normal(ks[10], (DEPTH, DN_HEADS * DN_DV), jnp.float32)
    w_out = (jax.random.normal(ks[11], (DEPTH, D_MIX, D_MODEL), jnp.float32)
             * (D_MIX ** -0.5) * DEEPNORM_BETA)
    ln_g = 1.0 + 0.02 * jax.random.normal(ks[12], (DEPTH, D_MODEL), jnp.float32)
    ln_b = 0.02 * jax.random.normal(ks[13], (DEPTH, D_MODEL), jnp.float32)
    return {'x': x, 'positions': positions, 'w_in': w_in, 'gla_w_a2': gla_w_a2, 'gla_b_a': gla_b_a,
            'dn_conv_w': dn_conv_w, 'dn_a_log': dn_a_log, 'dn_dt_bias': dn_dt_bias,
            'ret_norm_g': ret_norm_g, 'gla_norm_g': gla_norm_g, 'dn_norm_g': dn_norm_g,
            'w_out': w_out, 'ln_g': ln_g, 'ln_b': ln_b}


def reference(x, positions, w_in, gla_w_a2, gla_b_a, dn_conv_w, dn_a_log, dn_dt_bias,
              ret_norm_g, gla_norm_g, dn_norm_g, w_out, ln_g, ln_b):
    h = x
    for l in range(DEPTH):
        h = hybrid_layer(h, positions, w_in[l], gla_w_a2[l], gla_b_a[l], dn_conv_w[l], dn_a_log[l],
                         dn_dt_bias[l], ret_norm_g[l], gla_norm_g[l], dn_norm_g[l], w_out[l],
                         ln_g[l], ln_b[l])
    return h
```

```python
import contextlib
import numpy as np
import ml_dtypes
import concourse.bass as bass
import concourse.mybir as mybir
from concourse.bass_utils import run_bass_kernel_spmd

F32 = mybir.dt.float32
BF16 = mybir.dt.bfloat16
I32 = mybir.dt.int32
AF = mybir.ActivationFunctionType
ALU = mybir.AluOpType

D = 4096
SEQ = 4096
DEPTH = 2
DIN = 14384
EPS = 1e-6
LN_EPS = 1e-5
ALPHA = float((2 * DEPTH) ** 0.25)
NCH = 32
DBG = 0

ENGS = ("pe", "dve", "act", "pool", "sp")
NDSEM = 6


class _Op:
    __slots__ = ("eng", "fn", "idx", "is_dma", "dsem", "dval", "waits", "mark")


class Prog:
    def __init__(self):
        self.ops = {e: [] for e in ENGS}
        self.lastw = {}
        self.readers = {}
        self.ndma = {e: 0 for e in ENGS}
        self.known = {e: {} for e in ENGS}
        self.snap = {e: [] for e in ENGS}

    def _need(self, eng, dep, waits):
        kn = self.known[eng]
        if dep[0] == "c":
            _, e2, idx = dep
            if e2 == eng:
                if e2 == "pe":
                    return
                if idx < len(self.ops[eng]) - 3:
                    return
            key = ("c", e2)
            if kn.get(key, -1) >= idx:
                return
            waits.append(dep)
            kn[key] = idx
            for k, v in self.snap[e2][idx].items():
                if kn.get(k, -1) < v:
                    kn[k] = v
        else:
            _, e2, slot, val = dep
            key = ("d", e2, slot)
            if kn.get(key, -1) >= val:
                return
            waits.append(dep)
            kn[key] = val

    def _add(self, eng, fn, reads, writes, is_dma):
        op = _Op()
        op.eng = eng
        op.fn = fn
        op.idx = len(self.ops[eng])
        op.is_dma = is_dma
        op.mark = False
        op.waits = []
        if is_dma:
            j = self.ndma[eng]
            self.ndma[eng] += 1
            op.dsem = j % NDSEM
            op.dval = 16 * (j // NDSEM + 1)
            if j >= NDSEM:
                self._need(eng, ("d", eng, op.dsem, op.dval - 16), op.waits)
            me = ("d", eng, op.dsem, op.dval)
        else:
            me = ("c", eng, op.idx)
        for r in reads:
            w = self.lastw.get(r)
            if w is not None:
                self._need(eng, w, op.waits)
        for w_ in writes:
            w = self.lastw.get(w_)
            if w is not None:
                self._need(eng, w, op.waits)
            for rd in self.readers.get(w_, ()):
                if rd != me:
                    self._need(eng, rd, op.waits)
        for r in reads:
            self.readers.setdefault(r, []).append(me)
        for w_ in writes:
            self.lastw[w_] = me
            self.readers[w_] = []
        self.ops[eng].append(op)
        self.snap[eng].append(dict(self.known[eng]))
        return op

    def op(self, eng, fn, reads=(), writes=()):
        return self._add(eng, fn, reads, writes, False)

    def dma(self, eng, out, in_, reads=(), writes=()):
        return self._add(eng, lambda e: e.dma_start(out=out, in_=in_), reads, writes, True)

    def emit(self, nc):
        for e in ENGS:
            for op in self.ops[e]:
                for d in op.waits:
                    if d[0] == "c":
                        self.ops[d[1]][d[2]].mark = True
        cnt = {}
        for e in ENGS:
            c = 0
            arr = []
            for op in self.ops[e]:
                if op.mark:
                    c += 1
                arr.append(c)
            cnt[e] = arr
        with contextlib.ExitStack() as st:
            csem = {e: st.enter_context(nc.semaphore("c_" + e)) for e in ENGS}
            dsem = {e: [st.enter_context(nc.semaphore("d_%s%d" % (e, i))) for i in range(NDSEM)]
                    for e in ENGS if self.ndma[e] > 0}
            block = st.enter_context(nc.Block())
            handles = {"pe": block.tensor, "dve": block.vector, "act": block.scalar,
                       "pool": block.gpsimd, "sp": block.sync}

            def mk(e):
                def body(h):
                    for op in self.ops[e]:
                        for d in op.waits:
                            if d[0] == "c":
                                h.wait_ge(csem[d[1]], cnt[d[1]][d[2]])
                            else:
                                h.wait_ge(dsem[d[1]][d[2]], d[3])
                        ins = op.fn(h)
                        if op.is_dma:
                            ins.then_inc(dsem[e][op.dsem], 16)
                        elif op.mark:
                            ins.then_inc(csem[e], 1)
                    if self.ndma[e] > 0:
                        n = self.ndma[e]
                        for s in range(min(NDSEM, n)):
                            last = ((n - 1 - s) // NDSEM) * NDSEM + s
                            h.wait_ge(dsem[e][s], 16 * (last // NDSEM + 1))
                return body

            for e in ENGS:
                if self.ops[e]:
                    handles[e](mk(e))


def _consts():
    i = np.arange(128)
    c = {}
    c["ident"] = np.eye(128, dtype=np.float32)
    c["ones"] = np.ones((128, 128), np.float32)
    perm = np.zeros((128, 128), np.float32)
    perm[(i + 64) % 128, i] = 1.0
    c["perm"] = perm
    c["mincl"] = (i[None, :] >= i[:, None]).astype(np.float32)
    c["mstrict"] = (i[None, :] > i[:, None]).astype(np.float32)
    c["mlow"] = (i[:, None] > i[None, :]).astype(np.float32)
    blk = lambda s: ((i[:, None] // s) == (i[None, :] // s)).astype(np.float32)
    c["bd16"] = blk(16)
    c["off32"] = blk(32) - blk(16)
    c["off64"] = blk(64) - blk(32)
    c["off128"] = 1.0 - blk(64)
    t = np.arange(512)
    c["reset"] = np.broadcast_to(((t % 128) != 0).astype(np.float32)[None, :], (128, 512)).copy()
    names = ["ident", "ones", "perm", "mincl", "mstrict", "mlow", "bd16", "off32", "off64", "off128", "reset"]
    off = {}
    o = 0
    for n in names:
        off[n] = o
        o += c[n].shape[1]
    half = 64
    inv = (10000.0 ** (-np.arange(half, dtype=np.float32) / half)).astype(np.float32)
    invf = np.concatenate([inv, inv]).astype(np.float32)
    sgn = np.concatenate([-np.ones(64), np.ones(64)]).astype(np.float32)
    cols = np.stack([invf, sgn], axis=1).astype(np.float32)
    off["invf"] = o
    off["sgn"] = o + 1
    o += 2
    arr = np.concatenate([c[n] for n in names] + [cols], axis=1).astype(np.float32)
    return arr, off


CST, COFF = _consts()
NCST = CST.shape[1]

LA_COLS = 784
DN_COLS = 514
PASSES = [("ret", LA_COLS), ("dn", DN_COLS), ("gla", LA_COLS), ("dn", DN_COLS), ("dn", DN_COLS), ("dn", DN_COLS)]
PASS_ROWS = [0, 512, 256, 640, 768, 896]
NTAB = 640


def build_k1(T, passes=None, single=False):
    if passes is None:
        passes = list(range(6))
    NB = T // 512
    nc = bass.Bass("TRN2", target_bir_lowering=False)
    xT = nc.dram_tensor("xT", [D, T], F32, kind="ExternalInput").ap()
    pos = nc.dram_tensor("pos", [1, T], I32, kind="ExternalInput").ap()
    cst = nc.dram_tensor("cst", [128, NCST], F32, kind="ExternalInput").ap()
    if single:
        p0 = passes[0]
        Wd = {p0: nc.dram_tensor("w", [D, PASSES[p0][1]], F32, kind="ExternalInput").ap()}
        tabd = {p0: nc.dram_tensor("tab", [128, NTAB], F32, kind="ExternalInput").ap()}
        nrows = 128 if PASSES[p0][0] == "dn" else 256
        mixT = nc.dram_tensor("mixT", [nrows, T], BF16, kind="ExternalOutput").ap()
    else:
        Wd = {p: nc.dram_tensor("w%d" % p, [D, PASSES[p][1]], F32, kind="ExternalInput").ap() for p in passes}
        tabd = {p: nc.dram_tensor("tab%d" % p, [128, NTAB], F32, kind="ExternalInput").ap() for p in passes}
        mixT = nc.dram_tensor("mixT", [1024, T], BF16, kind="ExternalOutput").ap()
    xb = nc.dram_tensor("xb", [D, T], BF16).ap()

    P = Prog()
    with contextlib.ExitStack() as st:
        def sb(name, shape, dt=F32):
            return st.enter_context(nc.sbuf_tensor(name, shape, dt))

        def psum(name, shape, dt=F32):
            return st.enter_context(nc.psum_tensor(name, shape, dt))

        WA = sb("WA", [128, NCH, LA_COLS], BF16)
        xt = [sb("xt%d" % i, [128, NCH, 512], BF16) for i in range(2)]
        cs = sb("cs", [128, NCST])
        cb = sb("cb", [128, 128 * 10], BF16)
        tab = [sb("tabA", [128, NTAB]), sb("tabB", [128, NTAB])]
        PS = [psum("ps%d" % i, [128, 512]) for i in range(7)]
        PSB = psum("psb", [128, 1024], BF16)

        def C(name, w=128):
            o = COFF[name]
            return cs[:, o:o + w]

        def CB(name):
            o = COFF[name]
            return cb[:, o:o + 128]

        f512 = [sb("f512_%d" % i, [128, 512]) for i in range(7)]
        b512 = [sb("b512_%d" % i, [128, 512], BF16) for i in range(3)]
        i512 = sb("i512", [128, 512], I32)
        S = sb("S", [128, 256])
        Sb = sb("Sb", [128, 256], BF16)
        vbf = sb("vbf", [128, 256], BF16)
        gs = sb("gs", [128, 256])
        ktok = sb("ktok", [128, 128], BF16)
        PT = sb("PT", [128, 128], BF16)
        tmpS = sb("tmpS", [128, 256])
        onrm = sb("onrm", [128, 256])
        mixb = sb("mixb", [128, 256], BF16)
        mixTs = [sb("mixTs%d" % i, [128, 1024], BF16) for i in range(2)]
        st6 = sb("st6", [128, 6])
        mv = sb("mv", [128, 2])
        col = sb("col", [128, 64])
        a1f = sb("a1f", [16, 512])
        U = [sb("U%d" % i, [128, 515]) for i in range(3)]
        f128 = [sb("f128_%d" % i, [128, 128]) for i in range(8)]
        dnb = {}
        for nm in ["a", "n", "L32", "N32", "L64", "N64", "L128", "Z", "Y", "p2a", "p2n", "p4a", "p4n",
                   "W1", "V1", "QKm", "qg", "kb", "kd", "vb", "nwT", "vnew"]:
            dnb[nm] = [sb("dn_%s%d" % (nm, t), [128, 128], BF16) for t in range(4)]
        dgs = [sb("dgs%d" % t, [128, 128]) for t in range(4)]
        Af = [sb("Af%d" % t, [128, 128]) for t in range(4)]
        Nf = [sb("Nf%d" % t, [128, 128]) for t in range(4)]
        Zf = [sb("Zf%d" % t, [128, 128]) for t in range(4)]
        Yf = [sb("Yf%d" % t, [128, 128]) for t in range(4)]
        c4 = {nm: sb("c4_" + nm, [128, 4]) for nm in ["beta", "y", "ay", "e", "l", "g", "gc", "eg", "bg", "kdc", "gl", "egl"]}

        P.dma("sp", cs[:], cst[:, :], writes=["cs"])
        P.op("dve", lambda e: e.tensor_copy(cb[:], cs[:, 0:1280]), reads=["cs"], writes=["cb"])
        xb_keys = []
        CW = min(T, 2048)
        for r in range(D // 128):
            for cc in range(T // CW):
                k_ = "xb_%d_%d" % (r, cc)
                P.dma("pool", xb[r * 128:(r + 1) * 128, cc * CW:(cc + 1) * CW], xT[r * 128:(r + 1) * 128, cc * CW:(cc + 1) * CW], writes=[k_])
                xb_keys.append(k_)

        xcount = [0]

        def load_x(blk):
            slot = xcount[0] % 2
            xcount[0] += 1
            src = xb[:, blk * 512:(blk + 1) * 512].rearrange("(c p) t -> p c t", p=128)
            for hh in range(2):
                P.dma("sp", xt[slot][:, hh * 16:(hh + 1) * 16, :], src[:, hh * 16:(hh + 1) * 16, :],
                      reads=xb_keys, writes=["xt%d_%d" % (slot, hh)])
            return slot

        def xkeys(slot):
            return ["xt%d_0" % slot, "xt%d_1" % slot]

        def load_w(p):
            kind, ncols = PASSES[p]
            Wt, key = WA, "WA"
            src = Wd[p].rearrange("(c p) n -> p c n", p=128)
            for q in range(4):
                P.dma("pool", Wt[:, q * 8:(q + 1) * 8, 0:ncols], src[:, q * 8:(q + 1) * 8, :], writes=["%s_%d" % (key, q)])
            tslot = 0 if single else p % 2
            P.dma("sp", tab[tslot][:], tabd[p][:, :], writes=["tab%d" % tslot])
            return Wt, ["%s_%d" % (key, q) for q in range(4)], tab[tslot], "tab%d" % tslot

        def proj_fm(ps_ap, Wt, wkeys, c0, ncol, slot, pskey, M=128):
            for c in range(NCH):
                P.op("pe", lambda e, c=c: e.matmul(ps_ap, Wt[:, c, c0:c0 + ncol], xt[slot][:, c, :],
                                                  start=(c == 0), stop=(c == NCH - 1)),
                     reads=wkeys + xkeys(slot), writes=pskey if isinstance(pskey, list) else [pskey])

        def proj_tm(ps_ap, Wt, wkeys, c0, ncol, slot, t, pskey):
            for c in range(NCH):
                P.op("pe", lambda e, c=c: e.matmul(ps_ap, xt[slot][:, c, t * 128:(t + 1) * 128], Wt[:, c, c0:c0 + ncol],
                                                  start=(c == 0), stop=(c == NCH - 1)),
                     reads=wkeys + xkeys(slot), writes=[pskey])

        def finish_tile(p, blk, t, ps_o, okey, ncol, gate_ap, gkey, normg_ap, tkey, mslot, use_ln):
            if use_ln and DBG != 6:
                P.op("dve", lambda e: e.bn_stats(st6[:], ps_o), reads=[okey], writes=["st6"])
                P.op("dve", lambda e: e.bn_aggr(mv[:], st6[:]), reads=["st6"], writes=["mv"])
                P.op("act", lambda e: e.activation(col[:, 0:1], mv[:, 1:2], AF.Sqrt, bias=EPS), reads=["mv"], writes=["col0"])
                P.op("dve", lambda e: e.reciprocal(col[:, 1:2], col[:, 0:1]), reads=["col0"], writes=["col1"])
                P.op("dve", lambda e: e.tensor_scalar(onrm[:, 0:ncol], ps_o, mv[:, 0:1], col[:, 1:2], ALU.subtract, ALU.mult),
                     reads=[okey, "mv", "col1"], writes=["onrm"])
            else:
                P.op("act", lambda e: e.activation(tmpS[:, 0:ncol], ps_o, AF.Square, accum_out=col[:, 2:3]),
                     reads=[okey], writes=["tmpS", "col2"])
                P.op("act", lambda e: e.activation(col[:, 0:1], col[:, 2:3], AF.Sqrt, bias=EPS, scale=1.0 / ncol),
                     reads=["col2"], writes=["col0"])
                P.op("dve", lambda e: e.reciprocal(col[:, 1:2], col[:, 0:1]), reads=["col0"], writes=["col1"])
                P.op("dve", lambda e: e.tensor_scalar(onrm[:, 0:ncol], ps_o, col[:, 1:2], None, ALU.mult),
                     reads=[okey, "col1"], writes=["onrm"])
            if DBG == 7:
                return
            P.op("pool", lambda e: e.tensor_tensor(onrm[:, 0:ncol], onrm[:, 0:ncol], normg_ap, ALU.mult),
                 reads=["onrm", tkey], writes=["onrm"])
            if DBG == 8:
                return
            P.op("dve", lambda e: e.tensor_tensor(mixb[:, 0:ncol], onrm[:, 0:ncol], gate_ap, ALU.mult),
                 reads=["onrm", gkey], writes=["mixb"])
            if DBG == 9:
                return
            for c in range(ncol // 128):
                P.op("pe", lambda e, c=c: e.transpose(PSB[:, 128 + c * 128:128 + (c + 1) * 128], mixb[:, c * 128:(c + 1) * 128], CB("ident")),
                     reads=["mixb", "cb"], writes=["psb_m%d" % c])
                if DBG == 10:
                    continue
                P.op("dve", lambda e, c=c: e.tensor_copy(mixTs[mslot][:, c * 512 + t * 128:c * 512 + (t + 1) * 128], PSB[:, 128 + c * 128:128 + (c + 1) * 128]),
                     reads=["psb_m%d" % c], writes=["mixTs%d" % mslot])

        def store_mix(p, blk, mslot, nchunk):
            r0 = 0 if single else PASS_ROWS[p]
            dst = mixT[r0:r0 + nchunk * 128, blk * 512:(blk + 1) * 512].rearrange("(c p) t -> p c t", p=128)
            P.dma("sp", dst, mixTs[mslot][:, 0:nchunk * 512].rearrange("p (c t) -> p c t", c=nchunk), reads=["mixTs%d" % mslot])

        mcount = [0]

        def la_pass(p):
            kind = PASSES[p][0]
            Wt, wkeys, tb, tkey = load_w(p)
            normg = tb[:, 384:640]
            P.op("dve", lambda e: e.memset(S[:], 0.0), writes=["S"])
            P.op("dve", lambda e: e.memset(Sb[:], 0.0), writes=["Sb"])
            for blk in range(NB):
                slot = load_x(blk)
                mslot = mcount[0] % 2
                mcount[0] += 1
                qps, kps, cps, dps = PS[0], PS[1], PS[2], PS[6]
                proj_fm(qps[:, :], Wt, wkeys, 0, 128, slot, "ps0")
                proj_fm(kps[:, :], Wt, wkeys, 128, 128, slot, "ps1")
                qt, kt = b512[0], b512[1]
                if DBG == 1:
                    continue
                if kind == "ret":
                    ang, t1, t2, C1, S1, qf, kf_ = f512[0], f512[1], f512[2], f512[3], f512[4], f512[5], f512[6]
                    P.dma("sp", i512[:], pos[0:1, blk * 512:(blk + 1) * 512].partition_broadcast(128), writes=["i512"])
                    P.op("dve", lambda e: e.tensor_copy(ang[:], i512[:]), reads=["i512"], writes=["f0"])
                    P.op("dve", lambda e: e.tensor_scalar(ang[:], ang[:], C("invf", 1), None, ALU.mult), reads=["f0", "cs"], writes=["f0"])
                    for (dst, dkey, shift) in ((S1, "f4", 0.0), (C1, "f3", float(np.pi / 2))):
                        P.op("dve", lambda e, shift=shift: e.tensor_scalar(t1[:], ang[:], shift, float(1.0 / (2 * np.pi)), ALU.add, ALU.mult),
                             reads=["f0"], writes=["f1"])
                        P.op("dve", lambda e: e.tensor_copy(i512[:], t1[:]), reads=["f1"], writes=["i512"])
                        P.op("dve", lambda e: e.tensor_copy(t1[:], i512[:]), reads=["i512"], writes=["f1"])
                        P.op("dve", lambda e: e.scalar_tensor_tensor(t1[:], t1[:], float(-2 * np.pi), ang[:], ALU.mult, ALU.add),
                             reads=["f1", "f0"], writes=["f1"])
                        if shift != 0.0:
                            P.op("dve", lambda e, shift=shift: e.tensor_scalar(t1[:], t1[:], shift, None, ALU.add), reads=["f1"], writes=["f1"])
                        P.op("act", lambda e, dst=dst: e.activation(dst[:], t1[:], AF.Sin), reads=["f1"], writes=[dkey])
                    P.op("dve", lambda e: e.tensor_scalar(S1[:], S1[:], C("sgn", 1), None, ALU.mult), reads=["f4", "cs"], writes=["f4"])
                    for (ps_, pk, xf, xk, swp, swk, tabo, outb, ok) in (
                            (qps, "ps0", qf, "f5", cps, "ps2", 0, qt, "b0"), (kps, "ps1", kf_, "f6", dps, "ps6", 128, kt, "b1")):
                        P.op("act", lambda e, xf=xf, ps_=ps_: e.activation(xf[:], ps_[:, :], AF.Identity), reads=[pk], writes=[xk])
                        P.op("pe", lambda e, swp=swp, xf=xf: e.matmul(swp[:, :], C("perm"), xf[:], start=True, stop=True),
                             reads=[xk, "cs"], writes=[swk])
                        P.op("dve", lambda e, xf=xf: e.tensor_tensor(xf[:], xf[:], C1[:], ALU.mult), reads=[xk, "f3"], writes=[xk])
                        P.op("dve", lambda e, swp=swp: e.tensor_tensor(t2[:], swp[:, :], S1[:], ALU.mult), reads=[swk, "f4"], writes=["f2"])
                        P.op("pool", lambda e, xf=xf: e.tensor_tensor(xf[:], xf[:], t2[:], ALU.add), reads=[xk, "f2"], writes=[xk])
                        for t in range(4):
                            P.op("dve" if t % 2 == 0 else "pool", lambda e, xf=xf, outb=outb, tabo=tabo, t=t: e.tensor_tensor(
                                outb[:, t * 128:(t + 1) * 128], xf[:, t * 128:(t + 1) * 128], tb[:, tabo:tabo + 128], ALU.mult),
                                reads=[xk, tkey], writes=[ok])
                    elast = lambda t: tb[:, 256:257]
                    ekey = tkey
                else:
                    y, ay, ee, ll, bcum, eb, enb = f512[0], f512[1], f512[2], f512[3], f512[4], f512[5], f512[6]
                    proj_fm(cps[0:16, :], Wt, wkeys, 768, 16, slot, "ps2")
                    P.op("act", lambda e: e.activation(a1f[:], cps[0:16, :], AF.Identity), reads=["ps2"], writes=["a1f"])
                    P.op("pe", lambda e: e.matmul(dps[:, :], tb[0:16, 0:128], a1f[:], start=True, stop=True),
                         reads=["a1f", tkey], writes=["ps6"])
                    P.op("dve", lambda e: e.tensor_scalar(y[:], dps[:, :], tb[:, 256:257], -1.0, ALU.add, ALU.mult),
                         reads=["ps6", tkey], writes=["f0"])
                    P.op("dve", lambda e: e.scalar_tensor_tensor(ay[:], y[:], -1.0, y[:], ALU.mult, ALU.max), reads=["f0"], writes=["f1"])
                    P.op("act", lambda e: e.activation(ee[:], ay[:], AF.Exp, scale=-1.0), reads=["f1"], writes=["f2"])
                    P.op("act", lambda e: e.activation(ll[:], ee[:], AF.Ln, bias=1.0), reads=["f2"], writes=["f3"])
                    P.op("dve", lambda e: e.scalar_tensor_tensor(ll[:], y[:], 0.0, ll[:], ALU.max, ALU.add), reads=["f0", "f3"], writes=["f3"])
                    P.op("dve", lambda e: e.tensor_scalar(ll[:], ll[:], -1.0 / 16.0, None, ALU.mult), reads=["f3"], writes=["f3"])
                    P.op("dve", lambda e: e.tensor_tensor_scan(bcum[:], C("reset", 512), ll[:], 0.0, ALU.mult, ALU.add),
                         reads=["f3", "cs"], writes=["f4"])
                    P.op("act", lambda e: e.activation(eb[:], bcum[:], AF.Exp), reads=["f4"], writes=["f5"])
                    P.op("act", lambda e: e.activation(enb[:], bcum[:], AF.Exp, scale=-1.0), reads=["f4"], writes=["f6"])
                    P.op("dve", lambda e: e.scalar_tensor_tensor(qt[:], qps[:, :], float(128 ** -0.5), eb[:], ALU.mult, ALU.mult),
                         reads=["ps0", "f5"], writes=["b0"])
                    P.op("dve", lambda e: e.tensor_tensor(kt[:], kps[:, :], enb[:], ALU.mult), reads=["ps1", "f6"], writes=["b1"])
                    elast = lambda t: eb[:, t * 128 + 127:t * 128 + 128]
                    ekey = "f5"
                if DBG == 2:
                    continue
                for t in range(4):
                    ts = slice(t * 128, (t + 1) * 128)
                    P.op("pe", lambda e, ts=ts: e.transpose(PSB[:, 0:128], kt[:, ts], CB("ident")), reads=["b1", "cb"], writes=["psb_k"])
                    P.op("act", lambda e: e.activation(ktok[:], PSB[:, 0:128], AF.Identity), reads=["psb_k"], writes=["ktok"])
                    proj_tm(PS[3][:, :], Wt, wkeys, 256, 512, slot, t, "ps3")
                    P.op("act", lambda e: e.activation(vbf[:], PS[3][:, 0:256], AF.Identity), reads=["ps3"], writes=["vbf"])
                    P.op("act", lambda e: e.activation(gs[:], PS[3][:, 256:512], AF.Silu), reads=["ps3"], writes=["gs"])
                    P.op("pe", lambda e, ts=ts: e.matmul(PS[4][:, 0:128], kt[:, ts], qt[:, ts], start=True, stop=True),
                         reads=["b0", "b1"], writes=["ps4s"])
                    P.op("dve", lambda e: e.tensor_tensor(PT[:], PS[4][:, 0:128], C("mincl"), ALU.mult), reads=["ps4s", "cs"], writes=["PT"])
                    if DBG == 3:
                        continue
                    P.op("pe", lambda e: e.matmul(PS[5][:, 0:256], PT[:], vbf[:], start=True, stop=False), reads=["PT", "vbf"], writes=["ps5"])
                    P.op("pe", lambda e, ts=ts: e.matmul(PS[5][:, 0:256], qt[:, ts], Sb[:], start=False, stop=True),
                         reads=["b0", "Sb"], writes=["ps5"])
                    P.op("pe", lambda e: e.matmul(PS[4][:, 128:384], ktok[:], vbf[:], start=True, stop=True),
                         reads=["ktok", "vbf"], writes=["ps4d"])
                    P.op("dve", lambda e: e.tensor_tensor(tmpS[:], PS[4][:, 128:384], S[:], ALU.add), reads=["ps4d", "S"], writes=["tmpS"])
                    P.op("act", lambda e, t=t: e.activation(S[:], tmpS[:], AF.Identity, scale=elast(t)), reads=["tmpS", ekey], writes=["S"])
                    P.op("pool", lambda e: e.tensor_copy(Sb[:], S[:]), reads=["S"], writes=["Sb"])
                    if DBG == 4:
                        continue
                    finish_tile(p, blk, t, PS[5][:, 0:256], "ps5", 256, gs[:], "gs", normg, tkey, mslot, kind == "ret")
                if DBG == 5:
                    continue
                store_mix(p, blk, mslot, 2)

        def dn_pass(p):
            Wt, wkeys, tb, tkey = load_w(p)
            normg = tb[:, 384:512]
            P.op("dve", lambda e: e.memset(S[:, 0:128], 0.0), writes=["S"])
            P.op("dve", lambda e: e.memset(Sb[:, 0:128], 0.0), writes=["Sb"])
            for s_ in range(3):
                P.op("dve", lambda e, s_=s_: e.memset(U[s_][:, 512:515], 0.0), writes=["U%d" % s_])
            P.op("act", lambda e: e.activation(col[:, 8:9], tb[:, 16:17], AF.Exp), reads=[tkey], writes=["col8"])
            P.op("dve", lambda e: e.tensor_scalar(col[:, 9:10], col[:, 8:9], -1.0, None, ALU.mult), reads=["col8"], writes=["col9"])
            for blk in range(NB):
                slot = load_x(blk)
                mslot = mcount[0] % 2
                mcount[0] += 1
                cvs = [f512[0], f512[1], f512[2]]
                for s_ in range(3):
                    ps_ = PS[s_ % 2]
                    pk = "ps%d" % (s_ % 2)
                    proj_fm(ps_[:, :], Wt, wkeys, s_ * 128, 128, slot, [pk] + ["pc%d_%d_1" % (s_ % 2, tt) for tt in range(4)])
                    Us, uk = U[s_], "U%d" % s_
                    P.op("pool", lambda e, Us=Us: e.tensor_copy(Us[:, 0:3], Us[:, 512:515]), reads=[uk], writes=[uk + "h"])
                    P.op("act", lambda e, Us=Us, ps_=ps_: e.activation(Us[:, 3:515], ps_[:, :], AF.Identity), reads=[pk, uk + "h"], writes=[uk])
                    acc, ak = f512[3 + s_ % 2], "f%d" % (3 + s_ % 2)
                    P.op("dve", lambda e, Us=Us, acc=acc, s_=s_: e.tensor_scalar(acc[:], Us[:, 0:512], tb[:, s_ * 4:s_ * 4 + 1], None, ALU.mult),
                         reads=[uk, uk + "h", tkey], writes=[ak])
                    for i in range(1, 4):
                        P.op("dve", lambda e, Us=Us, acc=acc, s_=s_, i=i: e.scalar_tensor_tensor(
                            acc[:], Us[:, i:i + 512], tb[:, s_ * 4 + i:s_ * 4 + i + 1], acc[:], ALU.mult, ALU.add),
                            reads=[uk, uk + "h", ak, tkey], writes=[ak])
                    P.op("act", lambda e, acc=acc, s_=s_: e.activation(cvs[s_][:], acc[:], AF.Silu), reads=[ak], writes=["f%d" % s_])
                qn, kn = f512[5], f512[6]
                qb, kb_ = b512[0], b512[1]
                for (s_, dst, dk_, scl, bdst, bk) in ((0, qn, "f5", float(128 ** -0.5), qb, "b0"), (1, kn, "f6", 1.0, kb_, "b1")):
                    sq, sqk = f512[3], "f3"
                    P.op("act", lambda e, s_=s_: e.activation(sq[:], cvs[s_][:], AF.Square), reads=["f%d" % s_], writes=[sqk])
                    P.op("pe", lambda e: e.matmul(PS[2][:, :], C("ones"), sq[:], start=True, stop=True), reads=[sqk, "cs"], writes=["ps2"])
                    P.op("act", lambda e: e.activation(f512[4][:], PS[2][:, :], AF.Sqrt, bias=EPS), reads=["ps2"], writes=["f4"])
                    P.op("dve", lambda e: e.reciprocal(f512[4][:], f512[4][:]), reads=["f4"], writes=["f4"])
                    P.op("dve", lambda e, s_=s_, dst=dst, scl=scl: e.scalar_tensor_tensor(dst[:], cvs[s_][:], scl, f512[4][:], ALU.mult, ALU.mult),
                         reads=["f%d" % s_, "f4"], writes=[dk_])
                    P.op("pool", lambda e, dst=dst, bdst=bdst: e.tensor_copy(bdst[:], dst[:]), reads=[dk_], writes=[bk])
                vs = cvs[2]
                if DBG == 11:
                    continue
                for t in range(4):
                    proj_tm(PS[2][:, 0:130], Wt, wkeys, 384, 130, slot, t, "ps2")
                    P.op("act", lambda e, t=t: e.activation(dgs[t][:], PS[2][:, 0:128], AF.Silu), reads=["ps2"], writes=["dgs%d" % t])
                    P.op("act", lambda e, t=t: e.activation(c4["beta"][:, t:t + 1], PS[2][:, 128:129], AF.Sigmoid), reads=["ps2"], writes=["c4beta"])
                    P.op("dve", lambda e, t=t: e.tensor_scalar(c4["y"][:, t:t + 1], PS[2][:, 129:130], tb[:, 17:18], None, ALU.add),
                         reads=["ps2", tkey], writes=["c4y"])
                P.op("dve", lambda e: e.scalar_tensor_tensor(c4["ay"][:], c4["y"][:], -1.0, c4["y"][:], ALU.mult, ALU.max), reads=["c4y"], writes=["c4ay"])
                P.op("act", lambda e: e.activation(c4["e"][:], c4["ay"][:], AF.Exp, scale=-1.0), reads=["c4ay"], writes=["c4e"])
                P.op("act", lambda e: e.activation(c4["l"][:], c4["e"][:], AF.Ln, bias=1.0), reads=["c4e"], writes=["c4l"])
                P.op("dve", lambda e: e.scalar_tensor_tensor(c4["l"][:], c4["y"][:], 0.0, c4["l"][:], ALU.max, ALU.add), reads=["c4y", "c4l"], writes=["c4l"])
                P.op("dve", lambda e: e.tensor_scalar(c4["g"][:], c4["l"][:], col[:, 9:10], None, ALU.mult), reads=["c4l", "col9"], writes=["c4g"])
                if DBG == 12:
                    continue
                for t in range(4):
                    ts = slice(t * 128, (t + 1) * 128)
                    tk = lambda nm: "%s%d" % (nm, t)
                    gb, gbk = f512[3], "f3"
                    P.op("dve", lambda e, t=t: e.tensor_scalar(gb[:, 0:128], C("mincl"), c4["g"][:, t:t + 1], None, ALU.mult),
                         reads=["cs", "c4g"], writes=[gbk])
                    P.op("dve", lambda e, t=t: e.tensor_scalar(gb[:, 128:256], C("ident"), c4["beta"][:, t:t + 1], None, ALU.mult),
                         reads=["cs", "c4beta"], writes=[gbk])
                    P.op("pe", lambda e: e.matmul(PS[3][:, 0:256], C("ones"), gb[:, 0:256], start=True, stop=True), reads=[gbk, "cs"], writes=["ps3"])
                    P.op("pe", lambda e, t=t: e.matmul(PS[3][:, 256:257], C("mincl"), c4["g"][:, t:t + 1], start=True, stop=True),
                         reads=["c4g", "cs"], writes=["ps3c"])
                    gcc = c4["gc"][:, t:t + 1]
                    P.op("act", lambda e, gcc=gcc: e.activation(gcc, PS[3][:, 256:257], AF.Identity), reads=["ps3c"], writes=["c4gc"])
                    P.op("act", lambda e, t=t: e.activation(c4["gl"][:, t:t + 1], PS[3][:, 127:128], AF.Identity), reads=["ps3"], writes=["c4gl"])
                    Dji, Dij = f128[0], f128[1]
                    P.op("dve", lambda e, gcc=gcc: e.tensor_scalar(Dji[:], PS[3][:, 0:128], gcc, 0.0, ALU.subtract, ALU.min),
                         reads=["ps3", "c4gc"], writes=["g0"])
                    P.op("act", lambda e: e.activation(Dji[:], Dji[:], AF.Exp), reads=["g0"], writes=["g0"])
                    P.op("dve", lambda e, gcc=gcc: e.tensor_scalar(Dij[:], PS[3][:, 0:128], gcc, 0.0, ALU.subtract, ALU.max),
                         reads=["ps3", "c4gc"], writes=["g1"])
                    P.op("act", lambda e: e.activation(Dij[:], Dij[:], AF.Exp, scale=-1.0), reads=["g1"], writes=["g1"])
                    P.op("pe", lambda e, ts=ts: e.matmul(PS[4][:, 0:128], kb_[:, ts], kb_[:, ts], start=True, stop=True), reads=["b1"], writes=["ps4a"])
                    P.op("pe", lambda e, ts=ts: e.matmul(PS[4][:, 128:256], kb_[:, ts], qb[:, ts], start=True, stop=True), reads=["b0", "b1"], writes=["ps4b"])
                    t1_, t2_, t3_, t4_ = f128[2], f128[3], f128[4], f128[5]
                    P.op("dve", lambda e: e.tensor_tensor(t1_[:], PS[4][:, 0:128], Dji[:], ALU.mult), reads=["ps4a", "g0"], writes=["g2"])
                    P.op("dve", lambda e: e.tensor_tensor(t2_[:], PS[3][:, 128:256], C("mstrict"), ALU.mult), reads=["ps3", "cs"], writes=["g3"])
                    P.op("pool", lambda e, t=t: e.tensor_tensor(Nf[t][:], t1_[:], t2_[:], ALU.mult), reads=["g2", "g3"], writes=[tk("Nf")])
                    P.op("dve", lambda e: e.tensor_tensor(t3_[:], PS[4][:, 0:128], Dij[:], ALU.mult), reads=["ps4a", "g1"], writes=["g4"])
                    P.op("dve", lambda e, t=t: e.scalar_tensor_tensor(Af[t][:], C("mlow"), c4["beta"][:, t:t + 1], t3_[:], ALU.mult, ALU.mult),
                         reads=["g4", "cs", "c4beta"], writes=[tk("Af")])
                    P.op("dve", lambda e: e.tensor_tensor(t4_[:], PS[4][:, 128:256], Dji[:], ALU.mult), reads=["ps4b", "g0"], writes=["g5"])
                    P.op("pool", lambda e, t=t: e.tensor_tensor(dnb["QKm"][t][:], t4_[:], C("mincl"), ALU.mult), reads=["g5", "cs"], writes=[tk("QKm")])
                    Eg = f128[6]
                    P.op("act", lambda e: e.activation(Eg[:], PS[3][:, 0:128], AF.Exp), reads=["ps3"], writes=["g6"])
                    P.op("dve", lambda e, t=t, ts=ts: e.tensor_tensor(dnb["qg"][t][:], qn[:, ts], Eg[:], ALU.mult), reads=["f5", "g6"], writes=[tk("qg")])
                    P.op("act", lambda e, t=t: e.activation(c4["eg"][:, t:t + 1], c4["gc"][:, t:t + 1], AF.Exp), reads=["c4gc"], writes=["c4eg"])
                    P.op("dve", lambda e, t=t: e.tensor_tensor(c4["bg"][:, t:t + 1], c4["eg"][:, t:t + 1], c4["beta"][:, t:t + 1], ALU.mult),
                         reads=["c4eg", "c4beta"], writes=["c4bg"])
                    P.op("act", lambda e, t=t: e.activation(c4["kdc"][:, t:t + 1], c4["gc"][:, t:t + 1], AF.Exp, scale=-1.0, bias=c4["gl"][:, t:t + 1]),
                         reads=["c4gc", "c4gl"], writes=["c4kdc"])
                    P.op("act", lambda e, t=t: e.activation(c4["egl"][:, t:t + 1], c4["gl"][:, t:t + 1], AF.Exp), reads=["c4gl"], writes=["c4egl"])
                    P.op("pe", lambda e, ts=ts: e.transpose(PS[4][:, 256:384], kn[:, ts], C("ident")), reads=["f6", "cs"], writes=["ps4c"])
                    P.op("pe", lambda e, ts=ts: e.transpose(PS[4][:, 384:512], vs[:, ts], C("ident")), reads=["f2", "cs"], writes=["ps4d"])
                    P.op("dve", lambda e, t=t: e.tensor_scalar(dnb["kb"][t][:], PS[4][:, 256:384], c4["bg"][:, t:t + 1], None, ALU.mult),
                         reads=["ps4c", "c4bg"], writes=[tk("kb")])
                    P.op("act", lambda e, t=t: e.activation(dnb["kd"][t][:], PS[4][:, 256:384], AF.Identity, scale=c4["kdc"][:, t:t + 1]),
                         reads=["ps4c", "c4kdc"], writes=[tk("kd")])
                    P.op("act", lambda e, t=t: e.activation(dnb["vb"][t][:], PS[4][:, 384:512], AF.Identity, scale=c4["beta"][:, t:t + 1]),
                         reads=["ps4d", "c4beta"], writes=[tk("vb")])
                    for (nm, src, m) in (("a", Af, "bd16"), ("n", Nf, "bd16"), ("L32", Af, "off32"), ("N32", Nf, "off32"),
                                         ("L64", Af, "off64"), ("N64", Nf, "off64"), ("L128", Af, "off128")):
                        P.op("pool", lambda e, nm=nm, src=src, m=m, t=t: e.tensor_tensor(dnb[nm][t][:], src[t][:], C(m), ALU.mult),
                             reads=[tk("Af") if src is Af else tk("Nf"), "cs"], writes=[tk(nm)])
                    for (nm, src, Mf) in (("Z", Af, Zf), ("Y", Nf, Yf)):
                        sk = tk("Af") if src is Af else tk("Nf")
                        P.op("pool", lambda e, src=src, Mf=Mf, t=t: e.tensor_tensor(Mf[t][:], src[t][:], C("bd16"), ALU.mult),
                             reads=[sk, "cs"], writes=["%sf%d" % (nm, t)])
                        P.op("pool", lambda e, Mf=Mf, t=t: e.tensor_tensor(Mf[t][:], C("ident"), Mf[t][:], ALU.subtract),
                             reads=["%sf%d" % (nm, t), "cs"], writes=["%sf%d" % (nm, t)])
                        P.op("pool", lambda e, Mf=Mf, nm=nm, t=t: e.tensor_copy(dnb[nm][t][:], Mf[t][:]), reads=["%sf%d" % (nm, t)], writes=[tk(nm)])

                if DBG == 13:
                    continue
                rnd = [0]

                def slotp(t, h, r=None):
                    r = rnd[0] if r is None else r
                    if r % 2 == 0:
                        return PS[5][:, t * 128:(t + 1) * 128] if h == 0 else PS[6][:, t * 128:(t + 1) * 128]
                    return PS[0][:, t * 128:(t + 1) * 128] if h == 0 else PS[1][:, t * 128:(t + 1) * 128]

                def pkey(t, h, r=None):
                    r = rnd[0] if r is None else r
                    return "pc%d_%d_%d" % (h, t, r % 2)

                def mm(t, h, lhs, rhs, lk, rk):
                    if DBG == 21 and lk[0] in "YZ":
                        return
                    sp_, pk_ = slotp(t, h), pkey(t, h)
                    wk_ = [pk_] if t != 0 else [pkey(tt, h) for tt in range(4)]
                    P.op("pe", lambda e: e.matmul(sp_, lhs, rhs, start=True, stop=True), reads=[lk, rk], writes=wk_)

                def ev_copy(eng, t, h, dst, dk_):
                    sp_, pk_ = slotp(t, h), pkey(t, h)
                    if eng == "act" and DBG != 30:
                        P.op("act", lambda e: e.activation(dst, sp_, AF.Identity), reads=[pk_], writes=[dk_])
                    else:
                        P.op("dve", lambda e: e.tensor_copy(dst, sp_), reads=[pk_], writes=[dk_])

                def ev_comb(t, h, nm, op):
                    if DBG == 20:
                        return
                    if DBG in (19, 21):
                        ev_copy("dve", t, h, dnb[nm][t][:], "%s%d" % (nm, t))
                        return
                    sc = 1.0 if op == "add" else -1.0
                    Mf = Zf[t] if nm == "Z" else Yf[t]
                    fk = "%sf%d" % (nm, t)
                    sp_, pk_ = slotp(t, h), pkey(t, h)
                    P.op("dve", lambda e: e.tensor_tensor(Mf[:], Mf[:], sp_, ALU.add if op == "add" else ALU.subtract),
                         reads=[pk_, fk], writes=[fk])
                    P.op("dve" if DBG >= 18 else "pool", lambda e: e.tensor_copy(dnb[nm][t][:], Mf[:]), reads=[fk], writes=["%s%d" % (nm, t)])

                def chain_for(tl):
                    pa, pn = ["a"] * 4, ["n"] * 4
                    for lvl in range(3):
                        na, nn = ("p2a", "p2n") if lvl % 2 == 0 else ("p4a", "p4n")
                        rnd[0] += 1
                        for t in tl:
                            tk = lambda nm: "%s%d" % (nm, t)
                            mm(t, 0, dnb[pn[t]][t][:], dnb[pa[t]][t][:], tk(pn[t]), tk(pa[t]))
                            mm(t, 1, dnb[pa[t]][t][:], dnb[pn[t]][t][:], tk(pa[t]), tk(pn[t]))
                            ev_copy("act", t, 0, dnb[na][t][:], tk(na))
                            ev_copy("dve", t, 1, dnb[nn][t][:], tk(nn))
                        if DBG == 15:
                            break
                        rnd[0] += 1
                        for t in tl:
                            tk = lambda nm: "%s%d" % (nm, t)
                            if DBG == 22:
                                mm(t, 0, dnb["n"][t][:], dnb["a"][t][:], tk("n"), tk("a"))
                                mm(t, 1, dnb["a"][t][:], dnb["n"][t][:], tk("a"), tk("n"))
                                continue
                            if DBG == 23:
                                mm(t, 0, dnb["n"][t][:], dnb[na][t][:], tk("n"), tk(na))
                                mm(t, 1, dnb["a"][t][:], dnb[nn][t][:], tk("a"), tk(nn))
                                continue
                            mm(t, 0, dnb["Y"][t][:], dnb[na][t][:], tk("Y"), tk(na))
                            mm(t, 1, dnb["Z"][t][:], dnb[nn][t][:], tk("Z"), tk(nn))
                            ev_comb(t, 0, "Z", "add")
                            ev_comb(t, 1, "Y", "add")
                            pa[t], pn[t] = na, nn
                        if DBG in (18, 19, 20, 21, 22, 23):
                            break
                    if DBG in (15, 16, 18, 19, 20, 21, 22, 23, 30, 31):
                        return
                    for (Ln, Nn) in (("L32", "N32"), ("L64", "N64")):
                        rnd[0] += 1
                        for t in tl:
                            tk = lambda nm: "%s%d" % (nm, t)
                            mm(t, 0, dnb[Nn][t][:], dnb["Z"][t][:], tk(Nn), tk("Z"))
                            mm(t, 1, dnb[Ln][t][:], dnb["Y"][t][:], tk(Ln), tk("Y"))
                            ev_copy("act", t, 0, dnb["W1"][t][:], tk("W1"))
                            ev_copy("dve", t, 1, dnb["V1"][t][:], tk("V1"))
                        rnd[0] += 1
                        for t in tl:
                            tk = lambda nm: "%s%d" % (nm, t)
                            mm(t, 0, dnb["Y"][t][:], dnb["W1"][t][:], tk("Y"), tk("W1"))
                            mm(t, 1, dnb["Z"][t][:], dnb["V1"][t][:], tk("Z"), tk("V1"))
                            ev_comb(t, 0, "Z", "sub")
                            ev_comb(t, 1, "Y", "sub")
                    rnd[0] += 1
                    for t in tl:
                        tk = lambda nm: "%s%d" % (nm, t)
                        mm(t, 1, dnb["L128"][t][:], dnb["Y"][t][:], tk("L128"), tk("Y"))
                        ev_copy("act", t, 1, dnb["V1"][t][:], tk("V1"))
                    rnd[0] += 1
                    for t in tl:
                        tk = lambda nm: "%s%d" % (nm, t)
                        mm(t, 1, dnb["Z"][t][:], dnb["V1"][t][:], tk("Z"), tk("V1"))
                        ev_comb(t, 1, "Y", "sub")
                        mm(t, 0, dnb["kb"][t][:], dnb["Y"][t][:], tk("kb"), tk("Y"))
                        P.op("act", lambda e, t=t, sp_=slotp(t, 0): e.activation(dnb["nwT"][t][:], sp_, AF.Identity, scale=-1.0),
                             reads=[pkey(t, 0)], writes=[tk("nwT")])

                Yc = ["Y"] * 4
                if DBG == 33:
                    chain_for([0, 1, 2, 3])
                else:
                    for tsel in range(4):
                        chain_for([tsel])
                if DBG == 14:
                    continue
                for t in range(4):
                    tk = lambda nm: "%s%d" % (nm, t)
                    Yt = dnb[Yc[t]][t]
                    P.op("pe", lambda e, Yt=Yt, t=t: e.matmul(PS[3][:, 0:128], Yt[:], dnb["vb"][t][:], start=True, stop=False),
                         reads=[tk(Yc[t]), tk("vb")], writes=["ps3"])
                    P.op("pe", lambda e, t=t: e.matmul(PS[3][:, 0:128], dnb["nwT"][t][:], Sb[:, 0:128], start=False, stop=True),
                         reads=[tk("nwT"), "Sb"], writes=["ps3"])
                    P.op("act", lambda e, t=t: e.activation(dnb["vnew"][t][:], PS[3][:, 0:128], AF.Identity), reads=["ps3"], writes=[tk("vnew")])
                    P.op("pe", lambda e, t=t: e.matmul(PS[3][:, 128:256], dnb["qg"][t][:], Sb[:, 0:128], start=True, stop=False),
                         reads=[tk("qg"), "Sb"], writes=["ps3o"])
                    P.op("pe", lambda e, t=t: e.matmul(PS[3][:, 128:256], dnb["QKm"][t][:], dnb["vnew"][t][:], start=False, stop=True),
                         reads=[tk("QKm"), tk("vnew")], writes=["ps3o"])
                    P.op("pe", lambda e, t=t: e.matmul(PS[3][:, 256:384], dnb["kd"][t][:], dnb["vnew"][t][:], start=True, stop=True),
                         reads=[tk("kd"), tk("vnew")], writes=["ps3d"])
                    P.op("dve", lambda e, t=t: e.scalar_tensor_tensor(S[:, 0:128], S[:, 0:128], c4["egl"][:, t:t + 1], PS[3][:, 256:384], ALU.mult, ALU.add),
                         reads=["S", "c4egl", "ps3d"], writes=["S"])
                    P.op("pool", lambda e: e.tensor_copy(Sb[:, 0:128], S[:, 0:128]), reads=["S"], writes=["Sb"])
                    finish_tile(p, blk, t, PS[3][:, 128:256], "ps3o", 128, dgs[t][:], "dgs%d" % t, normg, tkey, mslot, False)
                store_mix(p, blk, mslot, 1)

        for p in passes:
            if PASSES[p][0] == "dn":
                dn_pass(p)
            else:
                la_pass(p)
        P.emit(nc)
    return nc


_ST = np.concatenate([[0], np.cumsum([512, 512, 1024, 1024, 512, 512, 1024, 1024, 16, 6144, 2048, 16, 16])]).astype(np.int64)


def k1_inputs(xT_b, pos_b, l, hg, inp, passes=None, single=False):
    w_in = inp["w_in"][l]
    m = {"xT": np.ascontiguousarray(xT_b), "pos": np.ascontiguousarray(pos_b), "cst": CST}
    rep = lambda v, n=128: np.broadcast_to(np.asarray(v, np.float32)[None, :], (n, len(v)))
    for p, (kind, ncols) in enumerate(PASSES):
        if passes is not None and p not in passes:
            continue
        tab = np.zeros((128, NTAB), np.float32)
        if kind == "ret":
            h = hg
            cols = np.concatenate([np.arange(_ST[0] + h * 128, _ST[0] + (h + 1) * 128), np.arange(_ST[1] + h * 128, _ST[1] + (h + 1) * 128),
                                   np.arange(_ST[2] + h * 256, _ST[2] + (h + 1) * 256), np.arange(_ST[3] + h * 256, _ST[3] + (h + 1) * 256),
                                   np.arange(_ST[8], _ST[8] + 16)])
            lg = np.log(np.float32(1.0) - np.power(np.float32(2.0), np.float32(-5.0 - h)))
            i = np.arange(128, dtype=np.float32)
            rq = np.exp((i + 1.0) * lg) * (128 ** -0.5)
            rk = np.exp(-(i + 1.0) * lg)
            tab[:, 0:128] = rep(rq)
            tab[:, 128:256] = rep(rk)
            tab[:, 256] = np.exp(128.0 * lg)
            tab[:, 384:640] = rep(inp["ret_norm_g"][l][h * 256:(h + 1) * 256])
        elif kind == "gla":
            h = hg
            cols = np.concatenate([np.arange(_ST[4] + h * 128, _ST[4] + (h + 1) * 128), np.arange(_ST[5] + h * 128, _ST[5] + (h + 1) * 128),
                                   np.arange(_ST[6] + h * 256, _ST[6] + (h + 1) * 256), np.arange(_ST[7] + h * 256, _ST[7] + (h + 1) * 256),
                                   np.arange(_ST[8], _ST[8] + 16)])
            tab[0:16, 0:128] = inp["gla_w_a2"][l][:, h * 128:(h + 1) * 128]
            tab[:, 256] = inp["gla_b_a"][l][h * 128:(h + 1) * 128]
            tab[:, 384:640] = rep(inp["gla_norm_g"][l][h * 256:(h + 1) * 256])
        else:
            idx = [1, 3, 4, 5].index(p)
            h = hg * 4 + idx
            b0 = _ST[9]
            cols = np.concatenate([np.arange(b0 + h * 128, b0 + (h + 1) * 128), np.arange(b0 + 2048 + h * 128, b0 + 2048 + (h + 1) * 128),
                                   np.arange(b0 + 4096 + h * 128, b0 + 4096 + (h + 1) * 128),
                                   np.arange(_ST[10] + h * 128, _ST[10] + (h + 1) * 128), [_ST[11] + h], [_ST[12] + h]])
            cw = inp["dn_conv_w"][l]
            for s_ in range(3):
                tab[:, s_ * 4:(s_ + 1) * 4] = cw[:, s_ * 2048 + h * 128:s_ * 2048 + (h + 1) * 128].T
            tab[:, 16] = inp["dn_a_log"][l][h]
            tab[:, 17] = inp["dn_dt_bias"][l][h]
            tab[:, 384:512] = rep(inp["dn_norm_g"][l][h * 128:(h + 1) * 128])
        m["w" if single else "w%d" % p] = np.ascontiguousarray(w_in[:, cols])
        m["tab" if single else "tab%d" % p] = tab
    return m


def mix_row_perm():
    perm = []
    for hg in range(4):
        rows = np.zeros(1024, np.int64)
        rows[0:256] = np.arange(hg * 256, (hg + 1) * 256)
        rows[256:512] = 1024 + np.arange(hg * 256, (hg + 1) * 256)
        for i in range(4):
            h = hg * 4 + i
            rows[512 + i * 128:512 + (i + 1) * 128] = 2048 + np.arange(h * 128, (h + 1) * 128)
        perm.append(rows)
    return np.concatenate(perm)


def build_k2(TT=1024):
    nc = bass.Bass("TRN2", target_bir_lowering=False)
    mixc = nc.dram_tensor("mixc", [D, TT], BF16, kind="ExternalInput").ap()
    xres = nc.dram_tensor("xres", [D, TT], F32, kind="ExternalInput").ap()
    wo = nc.dram_tensor("wo", [NCH, 128, NCH, 128], F32, kind="ExternalInput").ap()
    lgb = nc.dram_tensor("lgb", [128, 2 * NCH], F32, kind="ExternalInput").ap()
    cst = nc.dram_tensor("cst", [128, NCST], F32, kind="ExternalInput").ap()
    xo = nc.dram_tensor("xo", [D, TT], F32, kind="ExternalOutput").ap()
    P = Prog()
    with contextlib.ExitStack() as st:
        def sb(name, shape, dt=F32):
            return st.enter_context(nc.sbuf_tensor(name, shape, dt))
        cs = sb("cs", [128, NCST])
        gb = sb("gb", [128, 2 * NCH])
        mx = sb("mx", [128, NCH, 512], BF16)
        z = sb("z", [128, NCH, 512])
        wt = [sb("wt%d" % i, [128, NCH, 128], BF16) for i in range(2)]
        xr = [sb("xr%d" % i, [128, 512]) for i in range(2)]
        sq = sb("sq", [128, 512])
        s1 = sb("s1", [128, 512])
        s2 = sb("s2", [128, 512])
        mean = sb("mean", [128, 512])
        rstd = sb("rstd", [128, 512])
        ot = [sb("ot%d" % i, [128, 512]) for i in range(2)]
        PS = [st.enter_context(nc.psum_tensor("ps%d" % i, [128, 512], F32)) for i in range(4)]
        ones = cs[:, COFF["ones"]:COFF["ones"] + 128]
        P.dma("sp", cs[:], cst[:, :], writes=["cs"])
        P.dma("sp", gb[:], lgb[:, :], writes=["gb"])
        for half in range(TT // 512):
            hs = slice(half * 512, (half + 1) * 512)
            src = mixc[:, hs].rearrange("(c p) t -> p c t", p=128)
            for q in range(2):
                P.dma("sp", mx[:, q * 16:(q + 1) * 16, :], src[:, q * 16:(q + 1) * 16, :], writes=["mx%d" % q])
            for dc in range(NCH):
                w = wt[dc % 2]
                wk = "wt%d" % (dc % 2)
                P.dma("pool", w[:], wo[dc], writes=[wk])
                xt_ = xr[dc % 2]
                xk = "xr%d" % (dc % 2)
                P.dma("sp", xt_[:], xres[dc * 128:(dc + 1) * 128, hs], writes=[xk])
                ps = PS[dc % 2]
                pk = "ps%d" % (dc % 2)
                for ec in range(NCH):
                    P.op("pe", lambda e, ec=ec, w=w, ps=ps: e.matmul(ps[:, :], w[:, ec, :], mx[:, ec, :], start=(ec == 0), stop=(ec == NCH - 1)),
                         reads=[wk, "mx0", "mx1"], writes=[pk])
                zk = "z%d" % dc
                P.op("dve", lambda e, dc=dc, xt_=xt_, ps=ps: e.scalar_tensor_tensor(z[:, dc, :], xt_[:], ALPHA, ps[:, :], ALU.mult, ALU.add),
                     reads=[xk, pk], writes=[zk])
                P.op("act", lambda e, dc=dc: e.activation(sq[:], z[:, dc, :], AF.Square), reads=[zk], writes=["sq"])
                P.op("pe", lambda e, dc=dc: e.matmul(PS[2][:, :], ones, z[:, dc, :], start=True, stop=True), reads=[zk, "cs"], writes=["ps2"])
                P.op("pe", lambda e: e.matmul(PS[3][:, :], ones, sq[:], start=True, stop=True), reads=["sq", "cs"], writes=["ps3"])
                if dc == 0:
                    P.op("dve", lambda e: e.tensor_copy(s1[:], PS[2][:, :]), reads=["ps2"], writes=["s1"])
                    P.op("dve", lambda e: e.tensor_copy(s2[:], PS[3][:, :]), reads=["ps3"], writes=["s2"])
                else:
                    P.op("dve", lambda e: e.tensor_tensor(s1[:], s1[:], PS[2][:, :], ALU.add), reads=["ps2", "s1"], writes=["s1"])
                    P.op("dve", lambda e: e.tensor_tensor(s2[:], s2[:], PS[3][:, :], ALU.add), reads=["ps3", "s2"], writes=["s2"])
            P.op("dve", lambda e: e.tensor_scalar(mean[:], s1[:], 1.0 / D, None, ALU.mult), reads=["s1"], writes=["mean"])
            P.op("dve", lambda e: e.tensor_tensor(sq[:], mean[:], mean[:], ALU.mult), reads=["mean"], writes=["sq"])
            P.op("dve", lambda e: e.scalar_tensor_tensor(rstd[:], s2[:], 1.0 / D, sq[:], ALU.mult, ALU.subtract), reads=["s2", "sq"], writes=["rstd"])
            P.op("act", lambda e: e.activation(rstd[:], rstd[:], AF.Sqrt, bias=LN_EPS), reads=["rstd"], writes=["rstd"])
            P.op("dve", lambda e: e.reciprocal(rstd[:], rstd[:]), reads=["rstd"], writes=["rstd"])
            for dc in range(NCH):
                o_ = ot[dc % 2]
                ok = "ot%d" % (dc % 2)
                zk = "z%d" % dc
                P.op("dve", lambda e, dc=dc, o_=o_: e.tensor_tensor(o_[:], z[:, dc, :], mean[:], ALU.subtract), reads=[zk, "mean"], writes=[ok])
                P.op("pool", lambda e, o_=o_: e.tensor_tensor(o_[:], o_[:], rstd[:], ALU.mult), reads=[ok, "rstd"], writes=[ok])
                P.op("dve", lambda e, dc=dc, o_=o_: e.tensor_scalar(o_[:], o_[:], gb[:, dc:dc + 1], gb[:, NCH + dc:NCH + dc + 1], ALU.mult, ALU.add),
                     reads=[ok, "gb"], writes=[ok])
                P.dma("sp", xo[dc * 128:(dc + 1) * 128, hs], o_[:], reads=[ok])
        P.emit(nc)
    return nc


def kernel(**inp):
    inp = {k: np.asarray(v) for k, v in inp.items()}
    x = inp["x"]
    B = x.shape[0]
    pos = inp["positions"].astype(np.int32)
    xT = [np.ascontiguousarray(x[b].T) for b in range(B)]
    perm = mix_row_perm()
    progs = {"ret": build_k1(SEQ, [0], single=True), "gla": build_k1(SEQ, [2], single=True), "dn": build_k1(SEQ, [1], single=True)}
    nc2 = build_k2(1024)
    for l in range(DEPTH):
        mixs = [np.zeros((1024, SEQ), ml_dtypes.bfloat16) for _ in range(8)]
        for p in range(6):
            kind = PASSES[p][0]
            in_maps = [k1_inputs(xT[c // 4], pos[c // 4:c // 4 + 1], l, c % 4, inp, passes=[p], single=True) for c in range(8)]
            r1 = run_bass_kernel_spmd(progs[kind], in_maps, core_ids=list(range(8)))
            r0 = PASS_ROWS[p]
            n = 128 if kind == "dn" else 256
            for c in range(8):
                mixs[c][r0:r0 + n] = np.asarray(r1.results[c]["mixT"])
            del in_maps, r1
        w = inp["w_out"][l][perm]
        wo = np.ascontiguousarray(w.reshape(NCH, 128, NCH, 128).transpose(2, 1, 0, 3))
        lgb = np.ascontiguousarray(np.concatenate([inp["ln_g"][l].reshape(NCH, 128).T, inp["ln_b"][l].reshape(NCH, 128).T], axis=1).astype(np.float32))
        in2 = []
        for c in range(8):
            b, sq_ = c // 4, c % 4
            ts = slice(sq_ * 1024, (sq_ + 1) * 1024)
            mixc = np.ascontiguousarray(np.concatenate([mixs[b * 4 + hg][:, ts] for hg in range(4)], axis=0))
            in2.append({"mixc": mixc, "xres": np.ascontiguousarray(xT[b][:, ts]), "wo": wo, "lgb": lgb, "cst": CST})
        r2 = run_bass_kernel_spmd(nc2, in2, core_ids=list(range(8)))
        xT = [np.ascontiguousarray(np.concatenate([np.asarray(r2.results[b * 4 + s]["xo"]) for s in range(4)], axis=1)) for b in range(B)]
        del in2, r2, mixs
    out = np.stack([xT[b].T for b in range(B)], axis=0).astype(np.float32)
    return out
```

```python
import contextlib
import numpy as np
import ml_dtypes
import concourse.bass as bass
import concourse.mybir as mybir
from concourse.bass_utils import run_bass_kernel_spmd

F32 = mybir.dt.float32
BF16 = mybir.dt.bfloat16
I32 = mybir.dt.int32
AF = mybir.ActivationFunctionType
ALU = mybir.AluOpType

D = 4096
SEQ = 4096
DEPTH = 2
DIN = 14384
EPS = 1e-6
LN_EPS = 1e-5
ALPHA = float((2 * DEPTH) ** 0.25)
NCH = 32
DBG = 0

ENGS = ("pe", "dve", "act", "pool", "sp")
NDSEM = 6


class _Op:
    __slots__ = ("eng", "fn", "idx", "is_dma", "dsem", "dval", "waits", "mark")


class Prog:
    def __init__(self):
        self.ops = {e: [] for e in ENGS}
        self.lastw = {}
        self.readers = {}
        self.ndma = {e: 0 for e in ENGS}
        self.known = {e: {} for e in ENGS}
        self.snap = {e: [] for e in ENGS}

    def _need(self, eng, dep, waits):
        kn = self.known[eng]
        if dep[0] == "c":
            _, e2, idx = dep
            if e2 == eng:
                if e2 == "pe":
                    return
                if idx < len(self.ops[eng]) - 3:
                    return
            key = ("c", e2)
            if kn.get(key, -1) >= idx:
                return
            waits.append(dep)
            kn[key] = idx
            for k, v in self.snap[e2][idx].items():
                if kn.get(k, -1) < v:
                    kn[k] = v
        else:
            _, e2, slot, val = dep
            key = ("d", e2, slot)
            if kn.get(key, -1) >= val:
                return
            waits.append(dep)
            kn[key] = val

    def _add(self, eng, fn, reads, writes, is_dma):
        op = _Op()
        op.eng = eng
        op.fn = fn
        op.idx = len(self.ops[eng])
        op.is_dma = is_dma
        op.mark = False
        op.waits = []
        if is_dma:
            j = self.ndma[eng]
            self.ndma[eng] += 1
            op.dsem = j % NDSEM
            op.dval = 16 * (j // NDSEM + 1)
            if j >= NDSEM:
                self._need(eng, ("d", eng, op.dsem, op.dval - 16), op.waits)
            me = ("d", eng, op.dsem, op.dval)
        else:
            me = ("c", eng, op.idx)
        for r in reads:
            w = self.lastw.get(r)
            if w is not None:
                self._need(eng, w, op.waits)
        for w_ in writes:
            w = self.lastw.get(w_)
            if w is not None:
                self._need(eng, w, op.waits)
            for rd in self.readers.get(w_, ()):
                if rd != me:
                    self._need(eng, rd, op.waits)
        for r in reads:
            self.readers.setdefault(r, []).append(me)
        for w_ in writes:
            self.lastw[w_] = me
            self.readers[w_] = []
        self.ops[eng].append(op)
        self.snap[eng].append(dict(self.known[eng]))
        return op

    def op(self, eng, fn, reads=(), writes=()):
        return self._add(eng, fn, reads, writes, False)

    def dma(self, eng, out, in_, reads=(), writes=()):
        return self._add(eng, lambda e: e.dma_start(out=out, in_=in_), reads, writes, True)

    def emit(self, nc):
        for e in ENGS:
            for op in self.ops[e]:
                for d in op.waits:
                    if d[0] == "c":
                        self.ops[d[1]][d[2]].mark = True
        cnt = {}
        for e in ENGS:
            c = 0
            arr = []
            for op in self.ops[e]:
                if op.mark:
                    c += 1
                arr.append(c)
            cnt[e] = arr
        with contextlib.ExitStack() as st:
            csem = {e: st.enter_context(nc.semaphore("c_" + e)) for e in ENGS}
            dsem = {e: [st.enter_context(nc.semaphore("d_%s%d" % (e, i))) for i in range(NDSEM)]
                    for e in ENGS if self.ndma[e] > 0}
            block = st.enter_context(nc.Block())
            handles = {"pe": block.tensor, "dve": block.vector, "act": block.scalar,
                       "pool": block.gpsimd, "sp": block.sync}

            def mk(e):
                def body(h):
                    for op in self.ops[e]:
                        for d in op.waits:
                            if d[0] == "c":
                                h.wait_ge(csem[d[1]], cnt[d[1]][d[2]])
                            else:
                                h.wait_ge(dsem[d[1]][d[2]], d[3])
                        ins = op.fn(h)
                        if op.is_dma:
                            ins.then_inc(dsem[e][op.dsem], 16)
                        elif op.mark:
                            ins.then_inc(csem[e], 1)
                    if self.ndma[e] > 0:
                        n = self.ndma[e]
                        for s in range(min(NDSEM, n)):
                            last = ((n - 1 - s) // NDSEM) * NDSEM + s
                            h.wait_ge(dsem[e][s], 16 * (last // NDSEM + 1))
                return body

            for e in ENGS:
                if self.ops[e]:
                    handles[e](mk(e))


def _consts():
    i = np.arange(128)
    c = {}
    c["ident"] = np.eye(128, dtype=np.float32)
    c["ones"] = np.ones((128, 128), np.float32)
    perm = np.zeros((128, 128), np.float32)
    perm[(i + 64) % 128, i] = 1.0
    c["perm"] = perm
    c["mincl"] = (i[None, :] >= i[:, None]).astype(np.float32)
    c["mstrict"] = (i[None, :] > i[:, None]).astype(np.float32)
    c["mlow"] = (i[:, None] > i[None, :]).astype(np.float32)
    blk = lambda s: ((i[:, None] // s) == (i[None, :] // s)).astype(np.float32)
    c["bd16"] = blk(16)
    c["off32"] = blk(32) - blk(16)
    c["off64"] = blk(64) - blk(32)
    c["off128"] = 1.0 - blk(64)
    t = np.arange(512)
    c["reset"] = np.broadcast_to(((t % 128) != 0).astype(np.float32)[None, :], (128, 512)).copy()
    names = ["ident", "ones", "perm", "mincl", "mstrict", "mlow", "bd16", "off32", "off64", "off128", "reset"]
    off = {}
    o = 0
    for n in names:
        off[n] = o
        o += c[n].shape[1]
    half = 64
    inv = (10000.0 ** (-np.arange(half, dtype=np.float32) / half)).astype(np.float32)
    invf = np.concatenate([inv, inv]).astype(np.float32)
    sgn = np.concatenate([-np.ones(64), np.ones(64)]).astype(np.float32)
    cols = np.stack([invf, sgn], axis=1).astype(np.float32)
    off["invf"] = o
    off["sgn"] = o + 1
    o += 2
    arr = np.concatenate([c[n] for n in names] + [cols], axis=1).astype(np.float32)
    return arr, off


CST, COFF = _consts()
NCST = CST.shape[1]

LA_COLS = 784
DN_COLS = 514
PASSES = [("ret", LA_COLS), ("dn", DN_COLS), ("gla", LA_COLS), ("dn", DN_COLS), ("dn", DN_COLS), ("dn", DN_COLS)]
PASS_ROWS = [0, 512, 256, 640, 768, 896]
NTAB = 640


def build_k1(T, passes=None, single=False):
    if passes is None:
        passes = list(range(6))
    NB = T // 512
    nc = bass.Bass("TRN2", target_bir_lowering=False)
    xT = nc.dram_tensor("xT", [NB, 128, NCH * 512], F32, kind="ExternalInput").ap()
    pos = nc.dram_tensor("pos", [1, T], I32, kind="ExternalInput").ap()
    cst = nc.dram_tensor("cst", [128, NCST], F32, kind="ExternalInput").ap()
    if single:
        p0 = passes[0]
        Wd = {p0: nc.dram_tensor("w", [128, NCH * PASSES[p0][1]], F32, kind="ExternalInput").ap()}
        tabd = {p0: nc.dram_tensor("tab", [128, NTAB], F32, kind="ExternalInput").ap()}
        nrows = 128 if PASSES[p0][0] == "dn" else 256
        mixT = nc.dram_tensor("mixT", [NB, 128, (nrows // 128) * 512], BF16, kind="ExternalOutput").ap()
    else:
        Wd = {p: nc.dram_tensor("w%d" % p, [128, NCH * PASSES[p][1]], F32, kind="ExternalInput").ap() for p in sorted(set(passes))}
        tabd = {p: nc.dram_tensor("tab%d" % p, [128, NTAB], F32, kind="ExternalInput").ap() for p in sorted(set(passes))}
        mixT = nc.dram_tensor("mixT", [NB, 128, 8 * 512], BF16, kind="ExternalOutput").ap()
    xb = nc.dram_tensor("xb", [NB, 128, NCH * 512], BF16).ap()

    P = Prog()
    with contextlib.ExitStack() as st:
        def sb(name, shape, dt=F32):
            return st.enter_context(nc.sbuf_tensor(name, shape, dt))

        def psum(name, shape, dt=F32):
            return st.enter_context(nc.psum_tensor(name, shape, dt))

        WA = sb("WA", [128, NCH, LA_COLS], BF16)
        xt = [sb("xt%d" % i, [128, NCH, 512], BF16) for i in range(2)]
        cs = sb("cs", [128, NCST])
        cb = sb("cb", [128, 128 * 10], BF16)
        tab = [sb("tabA", [128, NTAB]), sb("tabB", [128, NTAB])]
        PS = [psum("ps%d" % i, [128, 512]) for i in range(7)]
        PSB = psum("psb", [128, 1024], BF16)

        def C(name, w=128):
            o = COFF[name]
            return cs[:, o:o + w]

        def CB(name):
            o = COFF[name]
            return cb[:, o:o + 128]

        f512 = [sb("f512_%d" % i, [128, 512]) for i in range(7)]
        b512 = [sb("b512_%d" % i, [128, 512], BF16) for i in range(3)]
        i512 = sb("i512", [128, 512], I32)
        S = sb("S", [128, 256])
        Sb = sb("Sb", [128, 256], BF16)
        vbf = sb("vbf", [128, 256], BF16)
        gs = sb("gs", [128, 256])
        ktok = sb("ktok", [128, 128], BF16)
        PT = sb("PT", [128, 128], BF16)
        tmpS = sb("tmpS", [128, 256])
        onrm = sb("onrm", [128, 256])
        mixb = sb("mixb", [128, 256], BF16)
        mixTs = [sb("mixTs%d" % i, [128, 1024], BF16) for i in range(2)]
        st6 = sb("st6", [128, 6])
        mv = sb("mv", [128, 2])
        col = sb("col", [128, 64])
        a1f = sb("a1f", [16, 512])
        U = [sb("U%d" % i, [128, 515]) for i in range(3)]
        f128 = [sb("f128_%d" % i, [128, 128]) for i in range(8)]
        dnb = {}
        for nm in ["a", "n", "L32", "N32", "L64", "N64", "L128", "Z", "Y", "p2a", "p2n", "p4a", "p4n",
                   "W1", "V1", "QKm", "qg", "kb", "kd", "vb", "nwT", "vnew"]:
            dnb[nm] = [sb("dn_%s%d" % (nm, t), [128, 128], BF16) for t in range(4)]
        dgs = [sb("dgs%d" % t, [128, 128]) for t in range(4)]
        Af = [sb("Af%d" % t, [128, 128]) for t in range(4)]
        Nf = [sb("Nf%d" % t, [128, 128]) for t in range(4)]
        Zf = [sb("Zf%d" % t, [128, 128]) for t in range(4)]
        Yf = [sb("Yf%d" % t, [128, 128]) for t in range(4)]
        c4 = {nm: sb("c4_" + nm, [128, 4]) for nm in ["beta", "y", "ay", "e", "l", "g", "gc", "eg", "bg", "kdc", "gl", "egl"]}

        P.dma("sp", cs[:], cst[:, :], writes=["cs"])
        P.op("dve", lambda e: e.tensor_copy(cb[:], cs[:, 0:1280]), reads=["cs"], writes=["cb"])
        for blk in range(NB):
            for hh in range(2):
                P.dma("pool", xb[blk][:, hh * 8192:(hh + 1) * 8192], xT[blk][:, hh * 8192:(hh + 1) * 8192], writes=["xb_%d_%d" % (blk, hh)])

        xcount = [0]

        def load_x(blk):
            slot = xcount[0] % 2
            xcount[0] += 1
            for hh in range(2):
                src = xb[blk][:, hh * 8192:(hh + 1) * 8192].rearrange("p (c t) -> p c t", t=512)
                P.dma("sp", xt[slot][:, hh * 16:(hh + 1) * 16, :], src,
                      reads=["xb_%d_%d" % (blk, hh)], writes=["xt%d_%d" % (slot, hh)])
            return slot

        pref = {}

        def get_x(blk):
            slot = pref.pop(blk) if blk in pref else load_x(blk)
            if blk + 1 < NB:
                pref[blk + 1] = load_x(blk + 1)
            return slot

        def xkeys(slot):
            return ["xt%d_0" % slot, "xt%d_1" % slot]

        def load_w(p):
            kind, ncols = PASSES[p]
            Wt, key = WA, "WA"
            for q in range(4):
                src = Wd[p][:, q * 8 * ncols:(q + 1) * 8 * ncols].rearrange("p (c n) -> p c n", n=ncols)
                P.dma("pool", Wt[:, q * 8:(q + 1) * 8, 0:ncols], src, writes=["%s_%d" % (key, q)])
            tslot = 0 if single else p % 2
            P.dma("sp", tab[tslot][:], tabd[p][:, :], writes=["tab%d" % tslot])
            return Wt, ["%s_%d" % (key, q) for q in range(4)], tab[tslot], "tab%d" % tslot

        def proj_fm(ps_ap, Wt, wkeys, c0, ncol, slot, pskey, M=128):
            for c in range(NCH):
                P.op("pe", lambda e, c=c: e.matmul(ps_ap, Wt[:, c, c0:c0 + ncol], xt[slot][:, c, :],
                                                  start=(c == 0), stop=(c == NCH - 1)),
                     reads=wkeys + xkeys(slot), writes=pskey if isinstance(pskey, list) else [pskey])

        def proj_tm(ps_ap, Wt, wkeys, c0, ncol, slot, t, pskey):
            for c in range(NCH):
                P.op("pe", lambda e, c=c: e.matmul(ps_ap, xt[slot][:, c, t * 128:(t + 1) * 128], Wt[:, c, c0:c0 + ncol],
                                                  start=(c == 0), stop=(c == NCH - 1)),
                     reads=wkeys + xkeys(slot), writes=[pskey])

        def finish_tile(p, blk, t, ps_o, okey, ncol, gate_ap, gkey, normg_ap, tkey, mslot, use_ln):
            if use_ln and DBG != 6:
                P.op("dve", lambda e: e.bn_stats(st6[:], ps_o), reads=[okey], writes=["st6"])
                P.op("dve", lambda e: e.bn_aggr(mv[:], st6[:]), reads=["st6"], writes=["mv"])
                P.op("act", lambda e: e.activation(col[:, 0:1], mv[:, 1:2], AF.Sqrt, bias=EPS), reads=["mv"], writes=["col0"])
                P.op("dve", lambda e: e.reciprocal(col[:, 1:2], col[:, 0:1]), reads=["col0"], writes=["col1"])
                P.op("dve", lambda e: e.tensor_scalar(onrm[:, 0:ncol], ps_o, mv[:, 0:1], col[:, 1:2], ALU.subtract, ALU.mult),
                     reads=[okey, "mv", "col1"], writes=["onrm"])
            else:
                P.op("act", lambda e: e.activation(tmpS[:, 0:ncol], ps_o, AF.Square, accum_out=col[:, 2:3]),
                     reads=[okey], writes=["tmpS", "col2"])
                P.op("act", lambda e: e.activation(col[:, 0:1], col[:, 2:3], AF.Sqrt, bias=EPS, scale=1.0 / ncol),
                     reads=["col2"], writes=["col0"])
                P.op("dve", lambda e: e.reciprocal(col[:, 1:2], col[:, 0:1]), reads=["col0"], writes=["col1"])
                P.op("dve", lambda e: e.tensor_scalar(onrm[:, 0:ncol], ps_o, col[:, 1:2], None, ALU.mult),
                     reads=[okey, "col1"], writes=["onrm"])
            if DBG == 7:
                return
            P.op("pool", lambda e: e.tensor_tensor(onrm[:, 0:ncol], onrm[:, 0:ncol], normg_ap, ALU.mult),
                 reads=["onrm", tkey], writes=["onrm"])
            if DBG == 8:
                return
            P.op("dve", lambda e: e.tensor_tensor(mixb[:, 0:ncol], onrm[:, 0:ncol], gate_ap, ALU.mult),
                 reads=["onrm", gkey], writes=["mixb"])
            if DBG == 9:
                return
            for c in range(ncol // 128):
                P.op("pe", lambda e, c=c: e.transpose(PSB[:, 128 + c * 128:128 + (c + 1) * 128], mixb[:, c * 128:(c + 1) * 128], CB("ident")),
                     reads=["mixb", "cb"], writes=["psb_m%d" % c])
                if DBG == 10:
                    continue
                P.op("dve", lambda e, c=c: e.tensor_copy(mixTs[mslot][:, c * 512 + t * 128:c * 512 + (t + 1) * 128], PSB[:, 128 + c * 128:128 + (c + 1) * 128]),
                     reads=["psb_m%d" % c], writes=["mixTs%d" % mslot])

        def store_mix(p, blk, mslot, nchunk):
            r0 = 0 if single else PASS_ROWS[p]
            rc0 = r0 // 128
            P.dma("sp", mixT[blk][:, rc0 * 512:(rc0 + nchunk) * 512], mixTs[mslot][:, 0:nchunk * 512], reads=["mixTs%d" % mslot])

        mcount = [0]

        def la_pass(p):
            kind = PASSES[p][0]
            Wt, wkeys, tb, tkey = load_w(p)
            normg = tb[:, 384:640]
            P.op("dve", lambda e: e.memset(S[:], 0.0), writes=["S"])
            P.op("dve", lambda e: e.memset(Sb[:], 0.0), writes=["Sb"])
            for blk in range(NB):
                slot = get_x(blk)
                mslot = mcount[0] % 2
                mcount[0] += 1
                qps, kps, cps, dps = PS[0], PS[1], PS[2], PS[6]
                proj_fm(qps[:, :], Wt, wkeys, 0, 128, slot, "ps0")
                proj_fm(kps[:, :], Wt, wkeys, 128, 128, slot, "ps1")
                qt, kt = b512[0], b512[1]
                if DBG == 1:
                    continue
                if kind == "ret":
                    ang, t1, t2, C1, S1, qf, kf_ = f512[0], f512[1], f512[2], f512[3], f512[4], f512[5], f512[6]
                    P.dma("sp", i512[:], pos[0:1, blk * 512:(blk + 1) * 512].partition_broadcast(128), writes=["i512"])
                    P.op("dve", lambda e: e.tensor_copy(ang[:], i512[:]), reads=["i512"], writes=["f0"])
                    P.op("dve", lambda e: e.tensor_scalar(ang[:], ang[:], C("invf", 1), None, ALU.mult), reads=["f0", "cs"], writes=["f0"])
                    for (dst, dkey, shift) in ((S1, "f4", 0.0), (C1, "f3", float(np.pi / 2))):
                        P.op("dve", lambda e, shift=shift: e.tensor_scalar(t1[:], ang[:], shift, float(1.0 / (2 * np.pi)), ALU.add, ALU.mult),
                             reads=["f0"], writes=["f1"])
                        P.op("dve", lambda e: e.tensor_copy(i512[:], t1[:]), reads=["f1"], writes=["i512"])
                        P.op("dve", lambda e: e.tensor_copy(t1[:], i512[:]), reads=["i512"], writes=["f1"])
                        P.op("dve", lambda e: e.scalar_tensor_tensor(t1[:], t1[:], float(-2 * np.pi), ang[:], ALU.mult, ALU.add),
                             reads=["f1", "f0"], writes=["f1"])
                        if shift != 0.0:
                            P.op("dve", lambda e, shift=shift: e.tensor_scalar(t1[:], t1[:], shift, None, ALU.add), reads=["f1"], writes=["f1"])
                        P.op("act", lambda e, dst=dst: e.activation(dst[:], t1[:], AF.Sin), reads=["f1"], writes=[dkey])
                    P.op("dve", lambda e: e.tensor_scalar(S1[:], S1[:], C("sgn", 1), None, ALU.mult), reads=["f4", "cs"], writes=["f4"])
                    for (ps_, pk, xf, xk, swp, swk, tabo, outb, ok) in (
                            (qps, "ps0", qf, "f5", cps, "ps2", 0, qt, "b0"), (kps, "ps1", kf_, "f6", dps, "ps6", 128, kt, "b1")):
                        P.op("act", lambda e, xf=xf, ps_=ps_: e.activation(xf[:], ps_[:, :], AF.Identity), reads=[pk], writes=[xk])
                        P.op("pe", lambda e, swp=swp, xf=xf: e.matmul(swp[:, :], C("perm"), xf[:], start=True, stop=True),
                             reads=[xk, "cs"], writes=[swk])
                        P.op("dve", lambda e, xf=xf: e.tensor_tensor(xf[:], xf[:], C1[:], ALU.mult), reads=[xk, "f3"], writes=[xk])
                        P.op("dve", lambda e, swp=swp: e.tensor_tensor(t2[:], swp[:, :], S1[:], ALU.mult), reads=[swk, "f4"], writes=["f2"])
                        P.op("pool", lambda e, xf=xf: e.tensor_tensor(xf[:], xf[:], t2[:], ALU.add), reads=[xk, "f2"], writes=[xk])
                        for t in range(4):
                            P.op("dve" if t % 2 == 0 else "pool", lambda e, xf=xf, outb=outb, tabo=tabo, t=t: e.tensor_tensor(
                                outb[:, t * 128:(t + 1) * 128], xf[:, t * 128:(t + 1) * 128], tb[:, tabo:tabo + 128], ALU.mult),
                                reads=[xk, tkey], writes=[ok])
                    elast = lambda t: tb[:, 256:257]
                    ekey = tkey
                else:
                    y, ay, ee, ll, bcum, eb, enb = f512[0], f512[1], f512[2], f512[3], f512[4], f512[5], f512[6]
                    proj_fm(cps[0:16, :], Wt, wkeys, 768, 16, slot, "ps2")
                    P.op("act", lambda e: e.activation(a1f[:], cps[0:16, :], AF.Identity), reads=["ps2"], writes=["a1f"])
                    P.op("pe", lambda e: e.matmul(dps[:, :], tb[0:16, 0:128], a1f[:], start=True, stop=True),
                         reads=["a1f", tkey], writes=["ps6"])
                    P.op("dve", lambda e: e.tensor_scalar(y[:], dps[:, :], tb[:, 256:257], -1.0, ALU.add, ALU.mult),
                         reads=["ps6", tkey], writes=["f0"])
                    P.op("dve", lambda e: e.scalar_tensor_tensor(ay[:], y[:], -1.0, y[:], ALU.mult, ALU.max), reads=["f0"], writes=["f1"])
                    P.op("act", lambda e: e.activation(ee[:], ay[:], AF.Exp, scale=-1.0), reads=["f1"], writes=["f2"])
                    P.op("act", lambda e: e.activation(ll[:], ee[:], AF.Ln, bias=1.0), reads=["f2"], writes=["f3"])
                    P.op("dve", lambda e: e.scalar_tensor_tensor(ll[:], y[:], 0.0, ll[:], ALU.max, ALU.add), reads=["f0", "f3"], writes=["f3"])
                    P.op("dve", lambda e: e.tensor_scalar(ll[:], ll[:], -1.0 / 16.0, None, ALU.mult), reads=["f3"], writes=["f3"])
                    P.op("dve", lambda e: e.tensor_tensor_scan(bcum[:], C("reset", 512), ll[:], 0.0, ALU.mult, ALU.add),
                         reads=["f3", "cs"], writes=["f4"])
                    P.op("act", lambda e: e.activation(eb[:], bcum[:], AF.Exp), reads=["f4"], writes=["f5"])
                    P.op("act", lambda e: e.activation(enb[:], bcum[:], AF.Exp, scale=-1.0), reads=["f4"], writes=["f6"])
                    P.op("dve", lambda e: e.scalar_tensor_tensor(qt[:], qps[:, :], float(128 ** -0.5), eb[:], ALU.mult, ALU.mult),
                         reads=["ps0", "f5"], writes=["b0"])
                    P.op("dve", lambda e: e.tensor_tensor(kt[:], kps[:, :], enb[:], ALU.mult), reads=["ps1", "f6"], writes=["b1"])
                    elast = lambda t: eb[:, t * 128 + 127:t * 128 + 128]
                    ekey = "f5"
                if DBG == 2:
                    continue
                for t in range(4):
                    ts = slice(t * 128, (t + 1) * 128)
                    P.op("pe", lambda e, ts=ts: e.transpose(PSB[:, 0:128], kt[:, ts], CB("ident")), reads=["b1", "cb"], writes=["psb_k"])
                    P.op("act", lambda e: e.activation(ktok[:], PSB[:, 0:128], AF.Identity), reads=["psb_k"], writes=["ktok"])
                    proj_tm(PS[3][:, :], Wt, wkeys, 256, 512, slot, t, "ps3")
                    P.op("act", lambda e: e.activation(vbf[:], PS[3][:, 0:256], AF.Identity), reads=["ps3"], writes=["vbf"])
                    P.op("act", lambda e: e.activation(gs[:], PS[3][:, 256:512], AF.Silu), reads=["ps3"], writes=["gs"])
                    P.op("pe", lambda e, ts=ts: e.matmul(PS[4][:, 0:128], kt[:, ts], qt[:, ts], start=True, stop=True),
                         reads=["b0", "b1"], writes=["ps4s"])
                    P.op("dve", lambda e: e.tensor_tensor(PT[:], PS[4][:, 0:128], C("mincl"), ALU.mult), reads=["ps4s", "cs"], writes=["PT"])
                    if DBG == 3:
                        continue
                    P.op("pe", lambda e: e.matmul(PS[5][:, 0:256], PT[:], vbf[:], start=True, stop=False), reads=["PT", "vbf"], writes=["ps5"])
                    P.op("pe", lambda e, ts=ts: e.matmul(PS[5][:, 0:256], qt[:, ts], Sb[:], start=False, stop=True),
                         reads=["b0", "Sb"], writes=["ps5"])
                    P.op("pe", lambda e: e.matmul(PS[4][:, 128:384], ktok[:], vbf[:], start=True, stop=True),
                         reads=["ktok", "vbf"], writes=["ps4d"])
                    P.op("dve", lambda e: e.tensor_tensor(tmpS[:], PS[4][:, 128:384], S[:], ALU.add), reads=["ps4d", "S"], writes=["tmpS"])
                    P.op("act", lambda e, t=t: e.activation(S[:], tmpS[:], AF.Identity, scale=elast(t)), reads=["tmpS", ekey], writes=["S"])
                    P.op("pool", lambda e: e.tensor_copy(Sb[:], S[:]), reads=["S"], writes=["Sb"])
                    if DBG == 4:
                        continue
                    finish_tile(p, blk, t, PS[5][:, 0:256], "ps5", 256, gs[:], "gs", normg, tkey, mslot, kind == "ret")
                if DBG == 5:
                    continue
                store_mix(p, blk, mslot, 2)

        def dn_pass(p):
            Wt, wkeys, tb, tkey = load_w(p)
            normg = tb[:, 384:512]
            P.op("dve", lambda e: e.memset(S[:, 0:128], 0.0), writes=["S"])
            P.op("dve", lambda e: e.memset(Sb[:, 0:128], 0.0), writes=["Sb"])
            for s_ in range(3):
                P.op("dve", lambda e, s_=s_: e.memset(U[s_][:, 512:515], 0.0), writes=["U%d" % s_])
            P.op("act", lambda e: e.activation(col[:, 8:9], tb[:, 16:17], AF.Exp), reads=[tkey], writes=["col8"])
            P.op("dve", lambda e: e.tensor_scalar(col[:, 9:10], col[:, 8:9], -1.0, None, ALU.mult), reads=["col8"], writes=["col9"])
            for blk in range(NB):
                slot = get_x(blk)
                mslot = mcount[0] % 2
                mcount[0] += 1
                cvs = [f512[0], f512[1], f512[2]]
                for s_ in range(3):
                    ps_ = PS[s_ % 2]
                    pk = "ps%d" % (s_ % 2)
                    proj_fm(ps_[:, :], Wt, wkeys, s_ * 128, 128, slot, [pk] + ["pc%d_%d_1" % (s_ % 2, tt) for tt in range(4)])
                    Us, uk = U[s_], "U%d" % s_
                    P.op("pool", lambda e, Us=Us: e.tensor_copy(Us[:, 0:3], Us[:, 512:515]), reads=[uk], writes=[uk + "h"])
                    P.op("act", lambda e, Us=Us, ps_=ps_: e.activation(Us[:, 3:515], ps_[:, :], AF.Identity), reads=[pk, uk + "h"], writes=[uk])
                    acc, ak = f512[3 + s_ % 2], "f%d" % (3 + s_ % 2)
                    P.op("dve", lambda e, Us=Us, acc=acc, s_=s_: e.tensor_scalar(acc[:], Us[:, 0:512], tb[:, s_ * 4:s_ * 4 + 1], None, ALU.mult),
                         reads=[uk, uk + "h", tkey], writes=[ak])
                    for i in range(1, 4):
                        P.op("dve", lambda e, Us=Us, acc=acc, s_=s_, i=i: e.scalar_tensor_tensor(
                            acc[:], Us[:, i:i + 512], tb[:, s_ * 4 + i:s_ * 4 + i + 1], acc[:], ALU.mult, ALU.add),
                            reads=[uk, uk + "h", ak, tkey], writes=[ak])
                    P.op("act", lambda e, acc=acc, s_=s_: e.activation(cvs[s_][:], acc[:], AF.Silu), reads=[ak], writes=["f%d" % s_])
                qn, kn = f512[5], f512[6]
                qb, kb_ = b512[0], b512[1]
                for (s_, dst, dk_, scl, bdst, bk) in ((0, qn, "f5", float(128 ** -0.5), qb, "b0"), (1, kn, "f6", 1.0, kb_, "b1")):
                    sq, sqk = f512[3], "f3"
                    P.op("act", lambda e, s_=s_: e.activation(sq[:], cvs[s_][:], AF.Square), reads=["f%d" % s_], writes=[sqk])
                    P.op("pe", lambda e: e.matmul(PS[2][:, :], C("ones"), sq[:], start=True, stop=True), reads=[sqk, "cs"], writes=["ps2"])
                    P.op("act", lambda e: e.activation(f512[4][:], PS[2][:, :], AF.Sqrt, bias=EPS), reads=["ps2"], writes=["f4"])
                    P.op("dve", lambda e: e.reciprocal(f512[4][:], f512[4][:]), reads=["f4"], writes=["f4"])
                    P.op("dve", lambda e, s_=s_, dst=dst, scl=scl: e.scalar_tensor_tensor(dst[:], cvs[s_][:], scl, f512[4][:], ALU.mult, ALU.mult),
                         reads=["f%d" % s_, "f4"], writes=[dk_])
                    P.op("pool", lambda e, dst=dst, bdst=bdst: e.tensor_copy(bdst[:], dst[:]), reads=[dk_], writes=[bk])
                vs = cvs[2]
                if DBG == 11:
                    continue
                for t in range(4):
                    proj_tm(PS[2][:, 0:130], Wt, wkeys, 384, 130, slot, t, "ps2")
                    P.op("act", lambda e, t=t: e.activation(dgs[t][:], PS[2][:, 0:128], AF.Silu), reads=["ps2"], writes=["dgs%d" % t])
                    P.op("act", lambda e, t=t: e.activation(c4["beta"][:, t:t + 1], PS[2][:, 128:129], AF.Sigmoid), reads=["ps2"], writes=["c4beta"])
                    P.op("dve", lambda e, t=t: e.tensor_scalar(c4["y"][:, t:t + 1], PS[2][:, 129:130], tb[:, 17:18], None, ALU.add),
                         reads=["ps2", tkey], writes=["c4y"])
                P.op("dve", lambda e: e.scalar_tensor_tensor(c4["ay"][:], c4["y"][:], -1.0, c4["y"][:], ALU.mult, ALU.max), reads=["c4y"], writes=["c4ay"])
                P.op("act", lambda e: e.activation(c4["e"][:], c4["ay"][:], AF.Exp, scale=-1.0), reads=["c4ay"], writes=["c4e"])
                P.op("act", lambda e: e.activation(c4["l"][:], c4["e"][:], AF.Ln, bias=1.0), reads=["c4e"], writes=["c4l"])
                P.op("dve", lambda e: e.scalar_tensor_tensor(c4["l"][:], c4["y"][:], 0.0, c4["l"][:], ALU.max, ALU.add), reads=["c4y", "c4l"], writes=["c4l"])
                P.op("dve", lambda e: e.tensor_scalar(c4["g"][:], c4["l"][:], col[:, 9:10], None, ALU.mult), reads=["c4l", "col9"], writes=["c4g"])
                if DBG == 12:
                    continue
                for t in range(4):
                    ts = slice(t * 128, (t + 1) * 128)
                    tk = lambda nm: "%s%d" % (nm, t)
                    gb, gbk = f512[3], "f3"
                    P.op("dve", lambda e, t=t: e.tensor_scalar(gb[:, 0:128], C("mincl"), c4["g"][:, t:t + 1], None, ALU.mult),
                         reads=["cs", "c4g"], writes=[gbk])
                    P.op("dve", lambda e, t=t: e.tensor_scalar(gb[:, 128:256], C("ident"), c4["beta"][:, t:t + 1], None, ALU.mult),
                         reads=["cs", "c4beta"], writes=[gbk])
                    P.op("pe", lambda e: e.matmul(PS[3][:, 0:256], C("ones"), gb[:, 0:256], start=True, stop=True), reads=[gbk, "cs"], writes=["ps3"])
                    P.op("pe", lambda e, t=t: e.matmul(PS[3][:, 256:257], C("mincl"), c4["g"][:, t:t + 1], start=True, stop=True),
                         reads=["c4g", "cs"], writes=["ps3c"])
                    gcc = c4["gc"][:, t:t + 1]
                    P.op("act", lambda e, gcc=gcc: e.activation(gcc, PS[3][:, 256:257], AF.Identity), reads=["ps3c"], writes=["c4gc"])
                    P.op("act", lambda e, t=t: e.activation(c4["gl"][:, t:t + 1], PS[3][:, 127:128], AF.Identity), reads=["ps3"], writes=["c4gl"])
                    Dji, Dij = f128[0], f128[1]
                    P.op("dve", lambda e, gcc=gcc: e.tensor_scalar(Dji[:], PS[3][:, 0:128], gcc, 0.0, ALU.subtract, ALU.min),
                         reads=["ps3", "c4gc"], writes=["g0"])
                    P.op("act", lambda e: e.activation(Dji[:], Dji[:], AF.Exp), reads=["g0"], writes=["g0"])
                    P.op("dve", lambda e, gcc=gcc: e.tensor_scalar(Dij[:], PS[3][:, 0:128], gcc, 0.0, ALU.subtract, ALU.max),
                         reads=["ps3", "c4gc"], writes=["g1"])
                    P.op("act", lambda e: e.activation(Dij[:], Dij[:], AF.Exp, scale=-1.0), reads=["g1"], writes=["g1"])
                    P.op("pe", lambda e, ts=ts: e.matmul(PS[4][:, 0:128], kb_[:, ts], kb_[:, ts], start=True, stop=True), reads=["b1"], writes=["ps4a"])
                    P.op("pe", lambda e, ts=ts: e.matmul(PS[4][:, 128:256], kb_[:, ts], qb[:, ts], start=True, stop=True), reads=["b0", "b1"], writes=["ps4b"])
                    t1_, t2_, t3_, t4_ = f128[2], f128[3], f128[4], f128[5]
                    P.op("dve", lambda e: e.tensor_tensor(t1_[:], PS[4][:, 0:128], Dji[:], ALU.mult), reads=["ps4a", "g0"], writes=["g2"])
                    P.op("dve", lambda e: e.tensor_tensor(t2_[:], PS[3][:, 128:256], C("mstrict"), ALU.mult), reads=["ps3", "cs"], writes=["g3"])
                    P.op("pool", lambda e, t=t: e.tensor_tensor(Nf[t][:], t1_[:], t2_[:], ALU.mult), reads=["g2", "g3"], writes=[tk("Nf")])
                    P.op("dve", lambda e: e.tensor_tensor(t3_[:], PS[4][:, 0:128], Dij[:], ALU.mult), reads=["ps4a", "g1"], writes=["g4"])
                    P.op("dve", lambda e, t=t: e.scalar_tensor_tensor(Af[t][:], C("mlow"), c4["beta"][:, t:t + 1], t3_[:], ALU.mult, ALU.mult),
                         reads=["g4", "cs", "c4beta"], writes=[tk("Af")])
                    P.op("dve", lambda e: e.tensor_tensor(t4_[:], PS[4][:, 128:256], Dji[:], ALU.mult), reads=["ps4b", "g0"], writes=["g5"])
                    P.op("pool", lambda e, t=t: e.tensor_tensor(dnb["QKm"][t][:], t4_[:], C("mincl"), ALU.mult), reads=["g5", "cs"], writes=[tk("QKm")])
                    Eg = f128[6]
                    P.op("act", lambda e: e.activation(Eg[:], PS[3][:, 0:128], AF.Exp), reads=["ps3"], writes=["g6"])
                    P.op("dve", lambda e, t=t, ts=ts: e.tensor_tensor(dnb["qg"][t][:], qn[:, ts], Eg[:], ALU.mult), reads=["f5", "g6"], writes=[tk("qg")])
                    P.op("act", lambda e, t=t: e.activation(c4["eg"][:, t:t + 1], c4["gc"][:, t:t + 1], AF.Exp), reads=["c4gc"], writes=["c4eg"])
                    P.op("dve", lambda e, t=t: e.tensor_tensor(c4["bg"][:, t:t + 1], c4["eg"][:, t:t + 1], c4["beta"][:, t:t + 1], ALU.mult),
                         reads=["c4eg", "c4beta"], writes=["c4bg"])
                    P.op("act", lambda e, t=t: e.activation(c4["kdc"][:, t:t + 1], c4["gc"][:, t:t + 1], AF.Exp, scale=-1.0, bias=c4["gl"][:, t:t + 1]),
                         reads=["c4gc", "c4gl"], writes=["c4kdc"])
                    P.op("act", lambda e, t=t: e.activation(c4["egl"][:, t:t + 1], c4["gl"][:, t:t + 1], AF.Exp), reads=["c4gl"], writes=["c4egl"])
                    P.op("pe", lambda e, ts=ts: e.transpose(PS[4][:, 256:384], kn[:, ts], C("ident")), reads=["f6", "cs"], writes=["ps4c"])
                    P.op("pe", lambda e, ts=ts: e.transpose(PS[4][:, 384:512], vs[:, ts], C("ident")), reads=["f2", "cs"], writes=["ps4d"])
                    P.op("dve", lambda e, t=t: e.tensor_scalar(dnb["kb"][t][:], PS[4][:, 256:384], c4["bg"][:, t:t + 1], None, ALU.mult),
                         reads=["ps4c", "c4bg"], writes=[tk("kb")])
                    P.op("act", lambda e, t=t: e.activation(dnb["kd"][t][:], PS[4][:, 256:384], AF.Identity, scale=c4["kdc"][:, t:t + 1]),
                         reads=["ps4c", "c4kdc"], writes=[tk("kd")])
                    P.op("act", lambda e, t=t: e.activation(dnb["vb"][t][:], PS[4][:, 384:512], AF.Identity, scale=c4["beta"][:, t:t + 1]),
                         reads=["ps4d", "c4beta"], writes=[tk("vb")])
                    for (nm, src, m) in (("a", Af, "bd16"), ("n", Nf, "bd16"), ("L32", Af, "off32"), ("N32", Nf, "off32"),
                                         ("L64", Af, "off64"), ("N64", Nf, "off64"), ("L128", Af, "off128")):
                        P.op("pool", lambda e, nm=nm, src=src, m=m, t=t: e.tensor_tensor(dnb[nm][t][:], src[t][:], C(m), ALU.mult),
                             reads=[tk("Af") if src is Af else tk("Nf"), "cs"], writes=[tk(nm)])
                    for (nm, src, Mf) in (("Z", Af, Zf), ("Y", Nf, Yf)):
                        sk = tk("Af") if src is Af else tk("Nf")
                        P.op("pool", lambda e, src=src, Mf=Mf, t=t: e.tensor_tensor(Mf[t][:], src[t][:], C("bd16"), ALU.mult),
                             reads=[sk, "cs"], writes=["%sf%d" % (nm, t)])
                        P.op("pool", lambda e, Mf=Mf, t=t: e.tensor_tensor(Mf[t][:], C("ident"), Mf[t][:], ALU.subtract),
                             reads=["%sf%d" % (nm, t), "cs"], writes=["%sf%d" % (nm, t)])
                        P.op("pool", lambda e, Mf=Mf, nm=nm, t=t: e.tensor_copy(dnb[nm][t][:], Mf[t][:]), reads=["%sf%d" % (nm, t)], writes=[tk(nm)])

                if DBG == 13:
                    continue
                rnd = [0]

                def slotp(t, h, r=None):
                    r = rnd[0] if r is None else r
                    if r % 2 == 0:
                        return PS[5][:, t * 128:(t + 1) * 128] if h == 0 else PS[6][:, t * 128:(t + 1) * 128]
                    return PS[0][:, t * 128:(t + 1) * 128] if h == 0 else PS[1][:, t * 128:(t + 1) * 128]

                def pkey(t, h, r=None):
                    r = rnd[0] if r is None else r
                    return "pc%d_%d_%d" % (h, t, r % 2)

                def mm(t, h, lhs, rhs, lk, rk):
                    if DBG == 21 and lk[0] in "YZ":
                        return
                    sp_, pk_ = slotp(t, h), pkey(t, h)
                    wk_ = [pk_] if t != 0 else [pkey(tt, h) for tt in range(4)]
                    P.op("pe", lambda e: e.matmul(sp_, lhs, rhs, start=True, stop=True), reads=[lk, rk], writes=wk_)

                def ev_copy(eng, t, h, dst, dk_):
                    sp_, pk_ = slotp(t, h), pkey(t, h)
                    if eng == "act" and DBG != 30:
                        P.op("act", lambda e: e.activation(dst, sp_, AF.Identity), reads=[pk_], writes=[dk_])
                    else:
                        P.op("dve", lambda e: e.tensor_copy(dst, sp_), reads=[pk_], writes=[dk_])

                def ev_comb(t, h, nm, op):
                    if DBG == 20:
                        return
                    if DBG in (19, 21):
                        ev_copy("dve", t, h, dnb[nm][t][:], "%s%d" % (nm, t))
                        return
                    sc = 1.0 if op == "add" else -1.0
                    Mf = Zf[t] if nm == "Z" else Yf[t]
                    fk = "%sf%d" % (nm, t)
                    sp_, pk_ = slotp(t, h), pkey(t, h)
                    P.op("dve", lambda e: e.tensor_tensor(Mf[:], Mf[:], sp_, ALU.add if op == "add" else ALU.subtract),
                         reads=[pk_, fk], writes=[fk])
                    P.op("dve" if DBG >= 18 else "pool", lambda e: e.tensor_copy(dnb[nm][t][:], Mf[:]), reads=[fk], writes=["%s%d" % (nm, t)])

                def chain_for(tl):
                    pa, pn = ["a"] * 4, ["n"] * 4
                    for lvl in range(3):
                        na, nn = ("p2a", "p2n") if lvl % 2 == 0 else ("p4a", "p4n")
                        rnd[0] += 1
                        for t in tl:
                            tk = lambda nm: "%s%d" % (nm, t)
                            mm(t, 0, dnb[pn[t]][t][:], dnb[pa[t]][t][:], tk(pn[t]), tk(pa[t]))
                            mm(t, 1, dnb[pa[t]][t][:], dnb[pn[t]][t][:], tk(pa[t]), tk(pn[t]))
                            ev_copy("act", t, 0, dnb[na][t][:], tk(na))
                            ev_copy("dve", t, 1, dnb[nn][t][:], tk(nn))
                        if DBG == 15:
                            break
                        rnd[0] += 1
                        for t in tl:
                            tk = lambda nm: "%s%d" % (nm, t)
                            if DBG == 22:
                                mm(t, 0, dnb["n"][t][:], dnb["a"][t][:], tk("n"), tk("a"))
                                mm(t, 1, dnb["a"][t][:], dnb["n"][t][:], tk("a"), tk("n"))
                                continue
                            if DBG == 23:
                                mm(t, 0, dnb["n"][t][:], dnb[na][t][:], tk("n"), tk(na))
                                mm(t, 1, dnb["a"][t][:], dnb[nn][t][:], tk("a"), tk(nn))
                                continue
                            mm(t, 0, dnb["Y"][t][:], dnb[na][t][:], tk("Y"), tk(na))
                            mm(t, 1, dnb["Z"][t][:], dnb[nn][t][:], tk("Z"), tk(nn))
                            ev_comb(t, 0, "Z", "add")
                            ev_comb(t, 1, "Y", "add")
                            pa[t], pn[t] = na, nn
                        if DBG in (18, 19, 20, 21, 22, 23):
                            break
                    if DBG in (15, 16, 18, 19, 20, 21, 22, 23, 30, 31):
                        return
                    for (Ln, Nn) in (("L32", "N32"), ("L64", "N64")):
                        rnd[0] += 1
                        for t in tl:
                            tk = lambda nm: "%s%d" % (nm, t)
                            mm(t, 0, dnb[Nn][t][:], dnb["Z"][t][:], tk(Nn), tk("Z"))
                            mm(t, 1, dnb[Ln][t][:], dnb["Y"][t][:], tk(Ln), tk("Y"))
                            ev_copy("act", t, 0, dnb["W1"][t][:], tk("W1"))
                            ev_copy("dve", t, 1, dnb["V1"][t][:], tk("V1"))
                        rnd[0] += 1
                        for t in tl:
                            tk = lambda nm: "%s%d" % (nm, t)
                            mm(t, 0, dnb["Y"][t][:], dnb["W1"][t][:], tk("Y"), tk("W1"))
                            mm(t, 1, dnb["Z"][t][:], dnb["V1"][t][:], tk("Z"), tk("V1"))
                            ev_comb(t, 0, "Z", "sub")
                            ev_comb(t, 1, "Y", "sub")
                    rnd[0] += 1
                    for t in tl:
                        tk = lambda nm: "%s%d" % (nm, t)
                        mm(t, 1, dnb["L128"][t][:], dnb["Y"][t][:], tk("L128"), tk("Y"))
                        ev_copy("act", t, 1, dnb["V1"][t][:], tk("V1"))
                    rnd[0] += 1
                    for t in tl:
                        tk = lambda nm: "%s%d" % (nm, t)
                        mm(t, 1, dnb["Z"][t][:], dnb["V1"][t][:], tk("Z"), tk("V1"))
                        ev_comb(t, 1, "Y", "sub")
                        mm(t, 0, dnb["kb"][t][:], dnb["Y"][t][:], tk("kb"), tk("Y"))
                        P.op("act", lambda e, t=t, sp_=slotp(t, 0): e.activation(dnb["nwT"][t][:], sp_, AF.Identity, scale=-1.0),
                             reads=[pkey(t, 0)], writes=[tk("nwT")])

                Yc = ["Y"] * 4
                if DBG == 33:
                    chain_for([0, 1, 2, 3])
                else:
                    for tsel in range(4):
                        chain_for([tsel])
                if DBG == 14:
                    continue
                for t in range(4):
                    tk = lambda nm: "%s%d" % (nm, t)
                    Yt = dnb[Yc[t]][t]
                    P.op("pe", lambda e, Yt=Yt, t=t: e.matmul(PS[3][:, 0:128], Yt[:], dnb["vb"][t][:], start=True, stop=False),
                         reads=[tk(Yc[t]), tk("vb")], writes=["ps3"])
                    P.op("pe", lambda e, t=t: e.matmul(PS[3][:, 0:128], dnb["nwT"][t][:], Sb[:, 0:128], start=False, stop=True),
                         reads=[tk("nwT"), "Sb"], writes=["ps3"])
                    P.op("act", lambda e, t=t: e.activation(dnb["vnew"][t][:], PS[3][:, 0:128], AF.Identity), reads=["ps3"], writes=[tk("vnew")])
                    P.op("pe", lambda e, t=t: e.matmul(PS[3][:, 128:256], dnb["qg"][t][:], Sb[:, 0:128], start=True, stop=False),
                         reads=[tk("qg"), "Sb"], writes=["ps3o"])
                    P.op("pe", lambda e, t=t: e.matmul(PS[3][:, 128:256], dnb["QKm"][t][:], dnb["vnew"][t][:], start=False, stop=True),
                         reads=[tk("QKm"), tk("vnew")], writes=["ps3o"])
                    P.op("pe", lambda e, t=t: e.matmul(PS[3][:, 256:384], dnb["kd"][t][:], dnb["vnew"][t][:], start=True, stop=True),
                         reads=[tk("kd"), tk("vnew")], writes=["ps3d"])
                    P.op("dve", lambda e, t=t: e.scalar_tensor_tensor(S[:, 0:128], S[:, 0:128], c4["egl"][:, t:t + 1], PS[3][:, 256:384], ALU.mult, ALU.add),
                         reads=["S", "c4egl", "ps3d"], writes=["S"])
                    P.op("pool", lambda e: e.tensor_copy(Sb[:, 0:128], S[:, 0:128]), reads=["S"], writes=["Sb"])
                    finish_tile(p, blk, t, PS[3][:, 128:256], "ps3o", 128, dgs[t][:], "dgs%d" % t, normg, tkey, mslot, False)
                store_mix(p, blk, mslot, 1)

        for p in passes:
            if PASSES[p][0] == "dn":
                dn_pass(p)
            else:
                la_pass(p)
        P.emit(nc)
    return nc


_ST = np.concatenate([[0], np.cumsum([512, 512, 1024, 1024, 512, 512, 1024, 1024, 16, 6144, 2048, 16, 16])]).astype(np.int64)


def k1_inputs(xT_b, pos_b, l, hg, inp, passes=None, single=False):
    w_in = inp["w_in"][l]
    m = {"xT": xT_b, "pos": np.ascontiguousarray(pos_b), "cst": CST}
    rep = lambda v, n=128: np.broadcast_to(np.asarray(v, np.float32)[None, :], (n, len(v)))
    for p, (kind, ncols) in enumerate(PASSES):
        if passes is not None and p not in passes:
            continue
        tab = np.zeros((128, NTAB), np.float32)
        if kind == "ret":
            h = hg
            cols = np.concatenate([np.arange(_ST[0] + h * 128, _ST[0] + (h + 1) * 128), np.arange(_ST[1] + h * 128, _ST[1] + (h + 1) * 128),
                                   np.arange(_ST[2] + h * 256, _ST[2] + (h + 1) * 256), np.arange(_ST[3] + h * 256, _ST[3] + (h + 1) * 256),
                                   np.arange(_ST[8], _ST[8] + 16)])
            lg = np.log(np.float32(1.0) - np.power(np.float32(2.0), np.float32(-5.0 - h)))
            i = np.arange(128, dtype=np.float32)
            rq = np.exp((i + 1.0) * lg) * (128 ** -0.5)
            rk = np.exp(-(i + 1.0) * lg)
            tab[:, 0:128] = rep(rq)
            tab[:, 128:256] = rep(rk)
            tab[:, 256] = np.exp(128.0 * lg)
            tab[:, 384:640] = rep(inp["ret_norm_g"][l][h * 256:(h + 1) * 256])
        elif kind == "gla":
            h = hg
            cols = np.concatenate([np.arange(_ST[4] + h * 128, _ST[4] + (h + 1) * 128), np.arange(_ST[5] + h * 128, _ST[5] + (h + 1) * 128),
                                   np.arange(_ST[6] + h * 256, _ST[6] + (h + 1) * 256), np.arange(_ST[7] + h * 256, _ST[7] + (h + 1) * 256),
                                   np.arange(_ST[8], _ST[8] + 16)])
            tab[0:16, 0:128] = inp["gla_w_a2"][l][:, h * 128:(h + 1) * 128]
            tab[:, 256] = inp["gla_b_a"][l][h * 128:(h + 1) * 128]
            tab[:, 384:640] = rep(inp["gla_norm_g"][l][h * 256:(h + 1) * 256])
        else:
            idx = [1, 3, 4, 5].index(p)
            h = hg * 4 + idx
            b0 = _ST[9]
            cols = np.concatenate([np.arange(b0 + h * 128, b0 + (h + 1) * 128), np.arange(b0 + 2048 + h * 128, b0 + 2048 + (h + 1) * 128),
                                   np.arange(b0 + 4096 + h * 128, b0 + 4096 + (h + 1) * 128),
                                   np.arange(_ST[10] + h * 128, _ST[10] + (h + 1) * 128), [_ST[11] + h], [_ST[12] + h]])
            cw = inp["dn_conv_w"][l]
            for s_ in range(3):
                tab[:, s_ * 4:(s_ + 1) * 4] = cw[:, s_ * 2048 + h * 128:s_ * 2048 + (h + 1) * 128].T
            tab[:, 16] = inp["dn_a_log"][l][h]
            tab[:, 17] = inp["dn_dt_bias"][l][h]
            tab[:, 384:512] = rep(inp["dn_norm_g"][l][h * 128:(h + 1) * 128])
        wp = w_in[:, cols]
        m["w" if single else "w%d" % p] = np.ascontiguousarray(wp.reshape(NCH, 128, ncols).transpose(1, 0, 2).reshape(128, NCH * ncols))
        m["tab" if single else "tab%d" % p] = tab
    return m


def mix_row_perm():
    perm = []
    for hg in range(4):
        rows = np.zeros(1024, np.int64)
        rows[0:256] = np.arange(hg * 256, (hg + 1) * 256)
        rows[256:512] = 1024 + np.arange(hg * 256, (hg + 1) * 256)
        for i in range(4):
            h = hg * 4 + i
            rows[512 + i * 128:512 + (i + 1) * 128] = 2048 + np.arange(h * 128, (h + 1) * 128)
        perm.append(rows)
    return np.concatenate(perm)


def block_x(xT, nb):
    return np.ascontiguousarray(xT.reshape(NCH, 128, nb, 512).transpose(2, 1, 0, 3).reshape(nb, 128, NCH * 512))


def unblock_x(xb):
    nb = xb.shape[0]
    return np.ascontiguousarray(xb.reshape(nb, 128, NCH, 512).transpose(2, 1, 0, 3).reshape(D, nb * 512))


def build_k2(NH=2):
    nc = bass.Bass("TRN2", target_bir_lowering=False)
    mixc = nc.dram_tensor("mixc", [NH, 128, NCH * 512], BF16, kind="ExternalInput").ap()
    xres = nc.dram_tensor("xres", [NH, 128, NCH * 512], F32, kind="ExternalInput").ap()
    wo = nc.dram_tensor("wo", [8, 128, 4 * NCH * 128], F32, kind="ExternalInput").ap()
    lgb = nc.dram_tensor("lgb", [128, 2 * NCH], F32, kind="ExternalInput").ap()
    cst = nc.dram_tensor("cst", [128, NCST], F32, kind="ExternalInput").ap()
    xo = nc.dram_tensor("xo", [NH, 128, NCH * 512], F32, kind="ExternalOutput").ap()
    P = Prog()
    with contextlib.ExitStack() as st:
        def sb(name, shape, dt=F32):
            return st.enter_context(nc.sbuf_tensor(name, shape, dt))
        cs = sb("cs", [128, NCST])
        gb = sb("gb", [128, 2 * NCH])
        mx = sb("mx", [128, NCH, 512], BF16)
        z = sb("z", [128, NCH, 512])
        wt = [sb("wt%d" % i, [128, 4, NCH, 128], BF16) for i in range(2)]
        sq = sb("sq", [128, 512])
        s1 = sb("s1", [128, 512])
        s2 = sb("s2", [128, 512])
        mean = sb("mean", [128, 512])
        rstd = sb("rstd", [128, 512])
        PS = [st.enter_context(nc.psum_tensor("ps%d" % i, [128, 512], F32)) for i in range(4)]
        ones = cs[:, COFF["ones"]:COFF["ones"] + 128]
        P.dma("sp", cs[:], cst[:, :], writes=["cs"])
        P.dma("sp", gb[:], lgb[:, :], writes=["gb"])
        wcount = 0
        for half in range(NH):
            for q in range(2):
                P.dma("sp", mx[:, q * 16:(q + 1) * 16, :], mixc[half][:, q * 8192:(q + 1) * 8192].rearrange("p (c t) -> p c t", t=512),
                      writes=["mx%d" % q])
            for q in range(4):
                P.dma("sp", z[:, q * 8:(q + 1) * 8, :], xres[half][:, q * 4096:(q + 1) * 4096].rearrange("p (c t) -> p c t", t=512),
                      writes=["z%d" % d_ for d_ in range(q * 8, (q + 1) * 8)])
            for dc in range(NCH):
                dcg, j = dc // 4, dc % 4
                w = wt[(wcount // 4) % 2]
                wk = "wt%d" % ((wcount // 4) % 2)
                if j == 0:
                    for hh in range(2):
                        P.dma("pool", w[:, hh * 2:(hh + 1) * 2, :, :],
                              wo[dcg][:, hh * 8192:(hh + 1) * 8192].rearrange("p (j c n) -> p j c n", j=2, n=128), writes=[wk + "_%d" % hh])
                wcount += 1
                ps = PS[dc % 2]
                pk = "ps%d" % (dc % 2)
                for ec in range(NCH):
                    P.op("pe", lambda e, ec=ec, w=w, ps=ps, j=j: e.matmul(ps[:, :], w[:, j, ec, :], mx[:, ec, :], start=(ec == 0), stop=(ec == NCH - 1)),
                         reads=[wk + "_%d" % (j // 2), "mx0", "mx1"], writes=[pk])
                zk = "z%d" % dc
                P.op("dve", lambda e, dc=dc, ps=ps: e.scalar_tensor_tensor(z[:, dc, :], z[:, dc, :], ALPHA, ps[:, :], ALU.mult, ALU.add),
                     reads=[zk, pk], writes=[zk])
                P.op("act", lambda e, dc=dc: e.activation(sq[:], z[:, dc, :], AF.Square), reads=[zk], writes=["sq"])
                P.op("pe", lambda e, dc=dc: e.matmul(PS[2][:, :], ones, z[:, dc, :], start=True, stop=True), reads=[zk, "cs"], writes=["ps2"])
                P.op("pe", lambda e: e.matmul(PS[3][:, :], ones, sq[:], start=True, stop=True), reads=["sq", "cs"], writes=["ps3"])
                if dc == 0:
                    P.op("dve", lambda e: e.tensor_copy(s1[:], PS[2][:, :]), reads=["ps2"], writes=["s1"])
                    P.op("dve", lambda e: e.tensor_copy(s2[:], PS[3][:, :]), reads=["ps3"], writes=["s2"])
                else:
                    P.op("dve", lambda e: e.tensor_tensor(s1[:], s1[:], PS[2][:, :], ALU.add), reads=["ps2", "s1"], writes=["s1"])
                    P.op("dve", lambda e: e.tensor_tensor(s2[:], s2[:], PS[3][:, :], ALU.add), reads=["ps3", "s2"], writes=["s2"])
            P.op("dve", lambda e: e.tensor_scalar(mean[:], s1[:], 1.0 / D, None, ALU.mult), reads=["s1"], writes=["mean"])
            P.op("dve", lambda e: e.tensor_tensor(sq[:], mean[:], mean[:], ALU.mult), reads=["mean"], writes=["sq"])
            P.op("dve", lambda e: e.scalar_tensor_tensor(rstd[:], s2[:], 1.0 / D, sq[:], ALU.mult, ALU.subtract), reads=["s2", "sq"], writes=["rstd"])
            P.op("act", lambda e: e.activation(rstd[:], rstd[:], AF.Sqrt, bias=LN_EPS), reads=["rstd"], writes=["rstd"])
            P.op("dve", lambda e: e.reciprocal(rstd[:], rstd[:]), reads=["rstd"], writes=["rstd"])
            for dc in range(NCH):
                zk = "z%d" % dc
                P.op("dve", lambda e, dc=dc: e.tensor_tensor(z[:, dc, :], z[:, dc, :], mean[:], ALU.subtract), reads=[zk, "mean"], writes=[zk])
                P.op("pool", lambda e, dc=dc: e.tensor_tensor(z[:, dc, :], z[:, dc, :], rstd[:], ALU.mult), reads=[zk, "rstd"], writes=[zk])
                P.op("dve", lambda e, dc=dc: e.tensor_scalar(z[:, dc, :], z[:, dc, :], gb[:, dc:dc + 1], gb[:, NCH + dc:NCH + dc + 1], ALU.mult, ALU.add),
                     reads=[zk, "gb"], writes=[zk])
                if dc % 8 == 7:
                    q = dc // 8
                    P.dma("sp", xo[half][:, q * 4096:(q + 1) * 4096].rearrange("p (c t) -> p c t", t=512), z[:, q * 8:(q + 1) * 8, :],
                          reads=["z%d" % d_ for d_ in range(q * 8, (q + 1) * 8)])
        P.emit(nc)
    return nc


def k2_weights(inp, l):
    perm = mix_row_perm()
    w = inp["w_out"][l][perm]
    wo = np.ascontiguousarray(w.reshape(NCH, 128, 8, 4, 128).transpose(2, 1, 3, 0, 4).reshape(8, 128, 4 * NCH * 128))
    lgb = np.ascontiguousarray(np.concatenate([inp["ln_g"][l].reshape(NCH, 128).T, inp["ln_b"][l].reshape(NCH, 128).T], axis=1).astype(np.float32))
    return wo, lgb


def kernel(**inp):
    inp = {k: np.asarray(v) for k, v in inp.items()}
    x = inp["x"]
    B = x.shape[0]
    NB = SEQ // 512
    pos = inp["positions"].astype(np.int32)
    xblk = [block_x(np.ascontiguousarray(x[b].T), NB) for b in range(B)]
    progs = {"ret": build_k1(SEQ, [0], single=True), "gla": build_k1(SEQ, [2], single=True), "dn": build_k1(SEQ, [1], single=True)}
    nc2 = build_k2(2)
    for l in range(DEPTH):
        mixs = [np.zeros((NB, 128, 8, 512), ml_dtypes.bfloat16) for _ in range(8)]
        for p in range(6):
            kind = PASSES[p][0]
            in_maps = [k1_inputs(xblk[c // 4], pos[c // 4:c // 4 + 1], l, c % 4, inp, passes=[p], single=True) for c in range(8)]
            r1 = run_bass_kernel_spmd(progs[kind], in_maps, core_ids=list(range(8)))
            rc0 = PASS_ROWS[p] // 128
            n = 1 if kind == "dn" else 2
            for c in range(8):
                mixs[c][:, :, rc0:rc0 + n, :] = np.asarray(r1.results[c]["mixT"]).reshape(NB, 128, n, 512)
            del in_maps, r1
        wo, lgb = k2_weights(inp, l)
        in2 = []
        for c in range(8):
            b, sq_ = c // 4, c % 4
            mixc = np.ascontiguousarray(np.concatenate([mixs[b * 4 + hg][sq_ * 2:sq_ * 2 + 2] for hg in range(4)], axis=2).reshape(2, 128, NCH * 512))
            in2.append({"mixc": mixc, "xres": np.ascontiguousarray(xblk[b][sq_ * 2:sq_ * 2 + 2]), "wo": wo, "lgb": lgb, "cst": CST})
        r2 = run_bass_kernel_spmd(nc2, in2, core_ids=list(range(8)))
        xblk = [np.ascontiguousarray(np.concatenate([np.asarray(r2.results[b * 4 + s]["xo"]) for s in range(4)], axis=0)) for b in range(B)]
        del in2, r2, mixs
    out = np.stack([unblock_x(xblk[b]).T for b in range(B)], axis=0).astype(np.float32)
    return out
```

```python
import contextlib
import numpy as np
import ml_dtypes
import concourse.bass as bass
import concourse.mybir as mybir
from concourse.bass_utils import run_bass_kernel_spmd

F32 = mybir.dt.float32
BF16 = mybir.dt.bfloat16
I32 = mybir.dt.int32
AF = mybir.ActivationFunctionType
ALU = mybir.AluOpType

D = 4096
SEQ = 4096
DEPTH = 2
DIN = 14384
EPS = 1e-6
LN_EPS = 1e-5
ALPHA = float((2 * DEPTH) ** 0.25)
NCH = 32
DBG = 0

ENGS = ("pe", "dve", "act", "pool", "sp")
NDSEM = 6


class _Op:
    __slots__ = ("eng", "fn", "idx", "is_dma", "dsem", "dval", "waits", "mark")


class Prog:
    def __init__(self):
        self.ops = {e: [] for e in ENGS}
        self.lastw = {}
        self.readers = {}
        self.ndma = {e: 0 for e in ENGS}
        self.known = {e: {} for e in ENGS}
        self.snap = {e: [] for e in ENGS}

    def _need(self, eng, dep, waits):
        kn = self.known[eng]
        if dep[0] == "c":
            _, e2, idx = dep
            if e2 == eng:
                if e2 == "pe":
                    return
                if idx < len(self.ops[eng]) - 3:
                    return
            key = ("c", e2)
            if kn.get(key, -1) >= idx:
                return
            waits.append(dep)
            kn[key] = idx
            for k, v in self.snap[e2][idx].items():
                if kn.get(k, -1) < v:
                    kn[k] = v
        else:
            _, e2, slot, val = dep
            key = ("d", e2, slot)
            if kn.get(key, -1) >= val:
                return
            waits.append(dep)
            kn[key] = val

    def _add(self, eng, fn, reads, writes, is_dma):
        op = _Op()
        op.eng = eng
        op.fn = fn
        op.idx = len(self.ops[eng])
        op.is_dma = is_dma
        op.mark = False
        op.waits = []
        if is_dma:
            j = self.ndma[eng]
            self.ndma[eng] += 1
            op.dsem = j % NDSEM
            op.dval = 16 * (j // NDSEM + 1)
            if j >= NDSEM:
                self._need(eng, ("d", eng, op.dsem, op.dval - 16), op.waits)
            me = ("d", eng, op.dsem, op.dval)
        else:
            me = ("c", eng, op.idx)
        for r in reads:
            w = self.lastw.get(r)
            if w is not None:
                self._need(eng, w, op.waits)
        for w_ in writes:
            w = self.lastw.get(w_)
            if w is not None:
                self._need(eng, w, op.waits)
            for rd in self.readers.get(w_, ()):
                if rd != me:
                    self._need(eng, rd, op.waits)
        for r in reads:
            self.readers.setdefault(r, []).append(me)
        for w_ in writes:
            self.lastw[w_] = me
            self.readers[w_] = []
        self.ops[eng].append(op)
        self.snap[eng].append(dict(self.known[eng]))
        return op

    def op(self, eng, fn, reads=(), writes=()):
        return self._add(eng, fn, reads, writes, False)

    def dma(self, eng, out, in_, reads=(), writes=()):
        return self._add(eng, lambda e: e.dma_start(out=out, in_=in_), reads, writes, True)

    def emit(self, nc):
        for e in ENGS:
            for op in self.ops[e]:
                for d in op.waits:
                    if d[0] == "c":
                        self.ops[d[1]][d[2]].mark = True
        cnt = {}
        for e in ENGS:
            c = 0
            arr = []
            for op in self.ops[e]:
                if op.mark:
                    c += 1
                arr.append(c)
            cnt[e] = arr
        with contextlib.ExitStack() as st:
            csem = {e: st.enter_context(nc.semaphore("c_" + e)) for e in ENGS}
            dsem = {e: [st.enter_context(nc.semaphore("d_%s%d" % (e, i))) for i in range(NDSEM)]
                    for e in ENGS if self.ndma[e] > 0}
            block = st.enter_context(nc.Block())
            handles = {"pe": block.tensor, "dve": block.vector, "act": block.scalar,
                       "pool": block.gpsimd, "sp": block.sync}

            def mk(e):
                def body(h):
                    for op in self.ops[e]:
                        for d in op.waits:
                            if d[0] == "c":
                                h.wait_ge(csem[d[1]], cnt[d[1]][d[2]])
                            else:
                                h.wait_ge(dsem[d[1]][d[2]], d[3])
                        ins = op.fn(h)
                        if op.is_dma:
                            ins.then_inc(dsem[e][op.dsem], 16)
                        elif op.mark:
                            ins.then_inc(csem[e], 1)
                    if self.ndma[e] > 0:
                        n = self.ndma[e]
                        for s in range(min(NDSEM, n)):
                            last = ((n - 1 - s) // NDSEM) * NDSEM + s
                            h.wait_ge(dsem[e][s], 16 * (last // NDSEM + 1))
                return body

            for e in ENGS:
                if self.ops[e]:
                    handles[e](mk(e))


def _consts():
    i = np.arange(128)
    c = {}
    c["ident"] = np.eye(128, dtype=np.float32)
    c["ones"] = np.ones((128, 128), np.float32)
    perm = np.zeros((128, 128), np.float32)
    perm[(i + 64) % 128, i] = 1.0
    c["perm"] = perm
    c["mincl"] = (i[None, :] >= i[:, None]).astype(np.float32)
    c["mstrict"] = (i[None, :] > i[:, None]).astype(np.float32)
    c["mlow"] = (i[:, None] > i[None, :]).astype(np.float32)
    blk = lambda s: ((i[:, None] // s) == (i[None, :] // s)).astype(np.float32)
    c["bd16"] = blk(16)
    c["off32"] = blk(32) - blk(16)
    c["off64"] = blk(64) - blk(32)
    c["off128"] = 1.0 - blk(64)
    t = np.arange(512)
    c["reset"] = np.broadcast_to(((t % 128) != 0).astype(np.float32)[None, :], (128, 512)).copy()
    names = ["ident", "ones", "perm", "mincl", "mstrict", "mlow", "bd16", "off32", "off64", "off128", "reset"]
    off = {}
    o = 0
    for n in names:
        off[n] = o
        o += c[n].shape[1]
    half = 64
    inv = (10000.0 ** (-np.arange(half, dtype=np.float32) / half)).astype(np.float32)
    invf = np.concatenate([inv, inv]).astype(np.float32)
    sgn = np.concatenate([-np.ones(64), np.ones(64)]).astype(np.float32)
    cols = np.stack([invf, sgn], axis=1).astype(np.float32)
    off["invf"] = o
    off["sgn"] = o + 1
    o += 2
    arr = np.concatenate([c[n] for n in names] + [cols], axis=1).astype(np.float32)
    return arr, off


CST, COFF = _consts()
NCST = CST.shape[1]

LA_COLS = 784
DN_COLS = 514
PASSES = [("ret", LA_COLS), ("dn", DN_COLS), ("gla", LA_COLS), ("dn", DN_COLS), ("dn", DN_COLS), ("dn", DN_COLS)]
PASS_ROWS = [0, 512, 256, 640, 768, 896]
NTAB = 640


def build_k1(T, passes=None, single=False):
    if passes is None:
        passes = list(range(6))
    NB = T // 512
    nc = bass.Bass("TRN2", target_bir_lowering=False)
    xT = nc.dram_tensor("xT", [NB, 128, NCH * 512], F32, kind="ExternalInput").ap()
    pos = nc.dram_tensor("pos", [1, T], I32, kind="ExternalInput").ap()
    cst = nc.dram_tensor("cst", [128, NCST], F32, kind="ExternalInput").ap()
    if single:
        p0 = passes[0]
        Wd = {p0: nc.dram_tensor("w", [128, NCH * PASSES[p0][1]], F32, kind="ExternalInput").ap()}
        tabd = {p0: nc.dram_tensor("tab", [128, NTAB], F32, kind="ExternalInput").ap()}
        nrows = 128 if PASSES[p0][0] == "dn" else 256
        mixT = nc.dram_tensor("mixT", [NB, 128, (nrows // 128) * 512], BF16, kind="ExternalOutput").ap()
    else:
        Wd = {p: nc.dram_tensor("w%d" % p, [128, NCH * PASSES[p][1]], F32, kind="ExternalInput").ap() for p in sorted(set(passes))}
        tabd = {p: nc.dram_tensor("tab%d" % p, [128, NTAB], F32, kind="ExternalInput").ap() for p in sorted(set(passes))}
        mixT = nc.dram_tensor("mixT", [NB, 128, 8 * 512], BF16, kind="ExternalOutput").ap()
    xb = nc.dram_tensor("xb", [NB, 128, NCH * 512], BF16).ap()

    P = Prog()
    with contextlib.ExitStack() as st:
        def sb(name, shape, dt=F32):
            return st.enter_context(nc.sbuf_tensor(name, shape, dt))

        def psum(name, shape, dt=F32):
            return st.enter_context(nc.psum_tensor(name, shape, dt))

        WA = sb("WA", [128, NCH, LA_COLS], BF16)
        xt = [sb("xt%d" % i, [128, NCH, 512], BF16) for i in range(2)]
        cs = sb("cs", [128, NCST])
        cb = sb("cb", [128, 128 * 10], BF16)
        tab = [sb("tabA", [128, NTAB]), sb("tabB", [128, NTAB])]
        PS = [psum("ps%d" % i, [128, 512]) for i in range(7)]
        PSB = psum("psb", [128, 1024], BF16)

        def C(name, w=128):
            o = COFF[name]
            return cs[:, o:o + w]

        def CB(name):
            o = COFF[name]
            return cb[:, o:o + 128]

        f512 = [sb("f512_%d" % i, [128, 512]) for i in range(7)]
        b512 = [sb("b512_%d" % i, [128, 512], BF16) for i in range(3)]
        i512 = sb("i512", [128, 512], I32)
        S = sb("S", [128, 256])
        Sb = sb("Sb", [128, 256], BF16)
        vbf = sb("vbf", [128, 256], BF16)
        gs = sb("gs", [128, 256])
        ktok = sb("ktok", [128, 128], BF16)
        PT = sb("PT", [128, 128], BF16)
        tmpS = sb("tmpS", [128, 256])
        onrm = sb("onrm", [128, 256])
        mixb = sb("mixb", [128, 256], BF16)
        mixTs = [sb("mixTs%d" % i, [128, 1024], BF16) for i in range(2)]
        st6 = sb("st6", [128, 6])
        mv = sb("mv", [128, 2])
        col = sb("col", [128, 64])
        a1f = sb("a1f", [16, 512])
        U = [sb("U%d" % i, [128, 515]) for i in range(3)]
        f128 = [sb("f128_%d" % i, [128, 128]) for i in range(8)]
        dnb = {}
        for nm in ["a", "n", "L32", "N32", "L64", "N64", "L128", "Z", "Y", "p2a", "p2n", "p4a", "p4n",
                   "W1", "V1", "QKm", "qg", "kb", "kd", "vb", "nwT", "vnew"]:
            dnb[nm] = [sb("dn_%s%d" % (nm, t), [128, 128], BF16) for t in range(4)]
        dgs = [sb("dgs%d" % t, [128, 128]) for t in range(4)]
        Af = [sb("Af%d" % t, [128, 128]) for t in range(4)]
        Nf = [sb("Nf%d" % t, [128, 128]) for t in range(4)]
        Zf = [sb("Zf%d" % t, [128, 128]) for t in range(4)]
        Yf = [sb("Yf%d" % t, [128, 128]) for t in range(4)]
        c4 = {nm: sb("c4_" + nm, [128, 4]) for nm in ["beta", "y", "ay", "e", "l", "g", "gc", "eg", "bg", "kdc", "gl", "egl"]}

        P.dma("sp", cs[:], cst[:, :], writes=["cs"])
        P.op("dve", lambda e: e.tensor_copy(cb[:], cs[:, 0:1280]), reads=["cs"], writes=["cb"])
        for blk in range(NB):
            for hh in range(2):
                P.dma("pool", xb[blk][:, hh * 8192:(hh + 1) * 8192], xT[blk][:, hh * 8192:(hh + 1) * 8192], writes=["xb_%d_%d" % (blk, hh)])

        xcount = [0]

        def load_x(blk):
            slot = xcount[0] % 2
            xcount[0] += 1
            for hh in range(2):
                src = xb[blk][:, hh * 8192:(hh + 1) * 8192].rearrange("p (c t) -> p c t", t=512)
                P.dma("sp", xt[slot][:, hh * 16:(hh + 1) * 16, :], src,
                      reads=["xb_%d_%d" % (blk, hh)], writes=["xt%d_%d" % (slot, hh)])
            return slot

        pref = {}

        def get_x(blk):
            slot = pref.pop(blk) if blk in pref else load_x(blk)
            if blk + 1 < NB:
                pref[blk + 1] = load_x(blk + 1)
            return slot

        def xkeys(slot):
            return ["xt%d_0" % slot, "xt%d_1" % slot]

        def load_w(p):
            kind, ncols = PASSES[p]
            Wt, key = WA, "WA"
            for q in range(4):
                src = Wd[p][:, q * 8 * ncols:(q + 1) * 8 * ncols].rearrange("p (c n) -> p c n", n=ncols)
                P.dma("pool", Wt[:, q * 8:(q + 1) * 8, 0:ncols], src, writes=["%s_%d" % (key, q)])
            tslot = 0 if single else p % 2
            P.dma("sp", tab[tslot][:], tabd[p][:, :], writes=["tab%d" % tslot])
            return Wt, ["%s_%d" % (key, q) for q in range(4)], tab[tslot], "tab%d" % tslot

        def proj_fm(ps_ap, Wt, wkeys, c0, ncol, slot, pskey, M=128):
            for c in range(NCH):
                P.op("pe", lambda e, c=c: e.matmul(ps_ap, Wt[:, c, c0:c0 + ncol], xt[slot][:, c, :],
                                                  start=(c == 0), stop=(c == NCH - 1)),
                     reads=wkeys + xkeys(slot), writes=pskey if isinstance(pskey, list) else [pskey])

        def proj_tm(ps_ap, Wt, wkeys, c0, ncol, slot, t, pskey):
            for c in range(NCH):
                P.op("pe", lambda e, c=c: e.matmul(ps_ap, xt[slot][:, c, t * 128:(t + 1) * 128], Wt[:, c, c0:c0 + ncol],
                                                  start=(c == 0), stop=(c == NCH - 1)),
                     reads=wkeys + xkeys(slot), writes=[pskey])

        def finish_tile(p, blk, t, ps_o, okey, ncol, gate_ap, gkey, normg_ap, tkey, mslot, use_ln):
            if use_ln and DBG != 6:
                P.op("dve", lambda e: e.bn_stats(st6[:], ps_o), reads=[okey], writes=["st6"])
                P.op("dve", lambda e: e.bn_aggr(mv[:], st6[:]), reads=["st6"], writes=["mv"])
                P.op("act", lambda e: e.activation(col[:, 0:1], mv[:, 1:2], AF.Sqrt, bias=EPS), reads=["mv"], writes=["col0"])
                P.op("dve", lambda e: e.reciprocal(col[:, 1:2], col[:, 0:1]), reads=["col0"], writes=["col1"])
                P.op("dve", lambda e: e.tensor_scalar(onrm[:, 0:ncol], ps_o, mv[:, 0:1], col[:, 1:2], ALU.subtract, ALU.mult),
                     reads=[okey, "mv", "col1"], writes=["onrm"])
            else:
                P.op("act", lambda e: e.activation(tmpS[:, 0:ncol], ps_o, AF.Square, accum_out=col[:, 2:3]),
                     reads=[okey], writes=["tmpS", "col2"])
                P.op("act", lambda e: e.activation(col[:, 0:1], col[:, 2:3], AF.Sqrt, bias=EPS, scale=1.0 / ncol),
                     reads=["col2"], writes=["col0"])
                P.op("dve", lambda e: e.reciprocal(col[:, 1:2], col[:, 0:1]), reads=["col0"], writes=["col1"])
                P.op("dve", lambda e: e.tensor_scalar(onrm[:, 0:ncol], ps_o, col[:, 1:2], None, ALU.mult),
                     reads=[okey, "col1"], writes=["onrm"])
            if DBG == 7:
                return
            P.op("pool", lambda e: e.tensor_tensor(onrm[:, 0:ncol], onrm[:, 0:ncol], normg_ap, ALU.mult),
                 reads=["onrm", tkey], writes=["onrm"])
            if DBG == 8:
                return
            P.op("dve", lambda e: e.tensor_tensor(mixb[:, 0:ncol], onrm[:, 0:ncol], gate_ap, ALU.mult),
                 reads=["onrm", gkey], writes=["mixb"])
            if DBG == 9:
                return
            for c in range(ncol // 128):
                P.op("pe", lambda e, c=c: e.transpose(PSB[:, 128 + c * 128:128 + (c + 1) * 128], mixb[:, c * 128:(c + 1) * 128], CB("ident")),
                     reads=["mixb", "cb"], writes=["psb_m%d" % c])
                if DBG == 10:
                    continue
                P.op("dve", lambda e, c=c: e.tensor_copy(mixTs[mslot][:, c * 512 + t * 128:c * 512 + (t + 1) * 128], PSB[:, 128 + c * 128:128 + (c + 1) * 128]),
                     reads=["psb_m%d" % c], writes=["mixTs%d" % mslot])

        def store_mix(p, blk, mslot, nchunk):
            r0 = 0 if single else PASS_ROWS[p]
            rc0 = r0 // 128
            P.dma("sp", mixT[blk][:, rc0 * 512:(rc0 + nchunk) * 512], mixTs[mslot][:, 0:nchunk * 512], reads=["mixTs%d" % mslot])

        mcount = [0]

        def la_pass(p):
            kind = PASSES[p][0]
            Wt, wkeys, tb, tkey = load_w(p)
            normg = tb[:, 384:640]
            P.op("dve", lambda e: e.memset(S[:], 0.0), writes=["S"])
            P.op("dve", lambda e: e.memset(Sb[:], 0.0), writes=["Sb"])
            for blk in range(NB):
                slot = get_x(blk)
                mslot = mcount[0] % 2
                mcount[0] += 1
                qps, kps, cps, dps = PS[0], PS[1], PS[2], PS[6]
                proj_fm(qps[:, :], Wt, wkeys, 0, 128, slot, "ps0")
                proj_fm(kps[:, :], Wt, wkeys, 128, 128, slot, "ps1")
                qt, kt = b512[0], b512[1]
                if DBG == 1:
                    continue
                if kind == "ret":
                    ang, t1, t2, C1, S1, qf, kf_ = f512[0], f512[1], f512[2], f512[3], f512[4], f512[5], f512[6]
                    P.dma("sp", i512[:], pos[0:1, blk * 512:(blk + 1) * 512].partition_broadcast(128), writes=["i512"])
                    P.op("dve", lambda e: e.tensor_copy(ang[:], i512[:]), reads=["i512"], writes=["f0"])
                    P.op("dve", lambda e: e.tensor_scalar(ang[:], ang[:], C("invf", 1), None, ALU.mult), reads=["f0", "cs"], writes=["f0"])
                    for (dst, dkey, shift) in ((S1, "f4", 0.0), (C1, "f3", float(np.pi / 2))):
                        P.op("dve", lambda e, shift=shift: e.tensor_scalar(t1[:], ang[:], shift, float(1.0 / (2 * np.pi)), ALU.add, ALU.mult),
                             reads=["f0"], writes=["f1"])
                        P.op("dve", lambda e: e.tensor_copy(i512[:], t1[:]), reads=["f1"], writes=["i512"])
                        P.op("dve", lambda e: e.tensor_copy(t1[:], i512[:]), reads=["i512"], writes=["f1"])
                        P.op("dve", lambda e: e.scalar_tensor_tensor(t1[:], t1[:], float(-2 * np.pi), ang[:], ALU.mult, ALU.add),
                             reads=["f1", "f0"], writes=["f1"])
                        if shift != 0.0:
                            P.op("dve", lambda e, shift=shift: e.tensor_scalar(t1[:], t1[:], shift, None, ALU.add), reads=["f1"], writes=["f1"])
                        P.op("act", lambda e, dst=dst: e.activation(dst[:], t1[:], AF.Sin), reads=["f1"], writes=[dkey])
                    P.op("dve", lambda e: e.tensor_scalar(S1[:], S1[:], C("sgn", 1), None, ALU.mult), reads=["f4", "cs"], writes=["f4"])
                    for (ps_, pk, xf, xk, swp, swk, tabo, outb, ok) in (
                            (qps, "ps0", qf, "f5", cps, "ps2", 0, qt, "b0"), (kps, "ps1", kf_, "f6", dps, "ps6", 128, kt, "b1")):
                        P.op("act", lambda e, xf=xf, ps_=ps_: e.activation(xf[:], ps_[:, :], AF.Identity), reads=[pk], writes=[xk])
                        P.op("pe", lambda e, swp=swp, xf=xf: e.matmul(swp[:, :], C("perm"), xf[:], start=True, stop=True),
                             reads=[xk, "cs"], writes=[swk])
                        P.op("dve", lambda e, xf=xf: e.tensor_tensor(xf[:], xf[:], C1[:], ALU.mult), reads=[xk, "f3"], writes=[xk])
                        P.op("dve", lambda e, swp=swp: e.tensor_tensor(t2[:], swp[:, :], S1[:], ALU.mult), reads=[swk, "f4"], writes=["f2"])
                        P.op("pool", lambda e, xf=xf: e.tensor_tensor(xf[:], xf[:], t2[:], ALU.add), reads=[xk, "f2"], writes=[xk])
                        for t in range(4):
                            P.op("dve" if t % 2 == 0 else "pool", lambda e, xf=xf, outb=outb, tabo=tabo, t=t: e.tensor_tensor(
                                outb[:, t * 128:(t + 1) * 128], xf[:, t * 128:(t + 1) * 128], tb[:, tabo:tabo + 128], ALU.mult),
                                reads=[xk, tkey], writes=[ok])
                    elast = lambda t: tb[:, 256:257]
                    ekey = tkey
                else:
                    y, ay, ee, ll, bcum, eb, enb = f512[0], f512[1], f512[2], f512[3], f512[4], f512[5], f512[6]
                    proj_fm(cps[0:16, :], Wt, wkeys, 768, 16, slot, "ps2")
                    P.op("act", lambda e: e.activation(a1f[:], cps[0:16, :], AF.Identity), reads=["ps2"], writes=["a1f"])
                    P.op("pe", lambda e: e.matmul(dps[:, :], tb[0:16, 0:128], a1f[:], start=True, stop=True),
                         reads=["a1f", tkey], writes=["ps6"])
                    P.op("dve", lambda e: e.tensor_scalar(y[:], dps[:, :], tb[:, 256:257], -1.0, ALU.add, ALU.mult),
                         reads=["ps6", tkey], writes=["f0"])
                    P.op("dve", lambda e: e.scalar_tensor_tensor(ay[:], y[:], -1.0, y[:], ALU.mult, ALU.max), reads=["f0"], writes=["f1"])
                    P.op("act", lambda e: e.activation(ee[:], ay[:], AF.Exp, scale=-1.0), reads=["f1"], writes=["f2"])
                    P.op("act", lambda e: e.activation(ll[:], ee[:], AF.Ln, bias=1.0), reads=["f2"], writes=["f3"])
                    P.op("dve", lambda e: e.scalar_tensor_tensor(ll[:], y[:], 0.0, ll[:], ALU.max, ALU.add), reads=["f0", "f3"], writes=["f3"])
                    P.op("dve", lambda e: e.tensor_scalar(ll[:], ll[:], -1.0 / 16.0, None, ALU.mult), reads=["f3"], writes=["f3"])
                    P.op("dve", lambda e: e.tensor_tensor_scan(bcum[:], C("reset", 512), ll[:], 0.0, ALU.mult, ALU.add),
                         reads=["f3", "cs"], writes=["f4"])
                    P.op("act", lambda e: e.activation(eb[:], bcum[:], AF.Exp), reads=["f4"], writes=["f5"])
                    P.op("act", lambda e: e.activation(enb[:], bcum[:], AF.Exp, scale=-1.0), reads=["f4"], writes=["f6"])
                    P.op("dve", lambda e: e.scalar_tensor_tensor(qt[:], qps[:, :], float(128 ** -0.5), eb[:], ALU.mult, ALU.mult),
                         reads=["ps0", "f5"], writes=["b0"])
                    P.op("dve", lambda e: e.tensor_tensor(kt[:], kps[:, :], enb[:], ALU.mult), reads=["ps1", "f6"], writes=["b1"])
                    elast = lambda t: eb[:, t * 128 + 127:t * 128 + 128]
                    ekey = "f5"
                if DBG == 2:
                    continue
                for t in range(4):
                    ts = slice(t * 128, (t + 1) * 128)
                    P.op("pe", lambda e, ts=ts: e.transpose(PSB[:, 0:128], kt[:, ts], CB("ident")), reads=["b1", "cb"], writes=["psb_k"])
                    P.op("act", lambda e: e.activation(ktok[:], PSB[:, 0:128], AF.Identity), reads=["psb_k"], writes=["ktok"])
                    proj_tm(PS[3][:, :], Wt, wkeys, 256, 512, slot, t, "ps3")
                    P.op("act", lambda e: e.activation(vbf[:], PS[3][:, 0:256], AF.Identity), reads=["ps3"], writes=["vbf"])
                    P.op("act", lambda e: e.activation(gs[:], PS[3][:, 256:512], AF.Silu), reads=["ps3"], writes=["gs"])
                    P.op("pe", lambda e, ts=ts: e.matmul(PS[4][:, 0:128], kt[:, ts], qt[:, ts], start=True, stop=True),
                         reads=["b0", "b1"], writes=["ps4s"])
                    P.op("dve", lambda e: e.tensor_tensor(PT[:], PS[4][:, 0:128], C("mincl"), ALU.mult), reads=["ps4s", "cs"], writes=["PT"])
                    if DBG == 3:
                        continue
                    P.op("pe", lambda e: e.matmul(PS[5][:, 0:256], PT[:], vbf[:], start=True, stop=False), reads=["PT", "vbf"], writes=["ps5"])
                    P.op("pe", lambda e, ts=ts: e.matmul(PS[5][:, 0:256], qt[:, ts], Sb[:], start=False, stop=True),
                         reads=["b0", "Sb"], writes=["ps5"])
                    P.op("pe", lambda e: e.matmul(PS[4][:, 128:384], ktok[:], vbf[:], start=True, stop=True),
                         reads=["ktok", "vbf"], writes=["ps4d"])
                    P.op("dve", lambda e: e.tensor_tensor(tmpS[:], PS[4][:, 128:384], S[:], ALU.add), reads=["ps4d", "S"], writes=["tmpS"])
                    P.op("act", lambda e, t=t: e.activation(S[:], tmpS[:], AF.Identity, scale=elast(t)), reads=["tmpS", ekey], writes=["S"])
                    P.op("pool", lambda e: e.tensor_copy(Sb[:], S[:]), reads=["S"], writes=["Sb"])
                    if DBG == 4:
                        continue
                    finish_tile(p, blk, t, PS[5][:, 0:256], "ps5", 256, gs[:], "gs", normg, tkey, mslot, kind == "ret")
                if DBG == 5:
                    continue
                store_mix(p, blk, mslot, 2)

        def dn_pass(p):
            Wt, wkeys, tb, tkey = load_w(p)
            normg = tb[:, 384:512]
            P.op("dve", lambda e: e.memset(S[:, 0:128], 0.0), writes=["S"])
            P.op("dve", lambda e: e.memset(Sb[:, 0:128], 0.0), writes=["Sb"])
            for s_ in range(3):
                P.op("dve", lambda e, s_=s_: e.memset(U[s_][:, 512:515], 0.0), writes=["U%d" % s_])
            P.op("act", lambda e: e.activation(col[:, 8:9], tb[:, 16:17], AF.Exp), reads=[tkey], writes=["col8"])
            P.op("dve", lambda e: e.tensor_scalar(col[:, 9:10], col[:, 8:9], -1.0, None, ALU.mult), reads=["col8"], writes=["col9"])
            for blk in range(NB):
                slot = get_x(blk)
                mslot = mcount[0] % 2
                mcount[0] += 1
                cvs = [f512[0], f512[1], f512[2]]
                for s_ in range(3):
                    ps_ = PS[s_ % 2]
                    pk = "ps%d" % (s_ % 2)
                    proj_fm(ps_[:, :], Wt, wkeys, s_ * 128, 128, slot, [pk, "pc%d_1" % (s_ % 2)])
                    Us, uk = U[s_], "U%d" % s_
                    P.op("pool", lambda e, Us=Us: e.tensor_copy(Us[:, 0:3], Us[:, 512:515]), reads=[uk], writes=[uk + "h"])
                    P.op("act", lambda e, Us=Us, ps_=ps_: e.activation(Us[:, 3:515], ps_[:, :], AF.Identity), reads=[pk, uk + "h"], writes=[uk])
                    acc, ak = f512[3 + s_ % 2], "f%d" % (3 + s_ % 2)
                    P.op("dve", lambda e, Us=Us, acc=acc, s_=s_: e.tensor_scalar(acc[:], Us[:, 0:512], tb[:, s_ * 4:s_ * 4 + 1], None, ALU.mult),
                         reads=[uk, uk + "h", tkey], writes=[ak])
                    for i in range(1, 4):
                        P.op("dve", lambda e, Us=Us, acc=acc, s_=s_, i=i: e.scalar_tensor_tensor(
                            acc[:], Us[:, i:i + 512], tb[:, s_ * 4 + i:s_ * 4 + i + 1], acc[:], ALU.mult, ALU.add),
                            reads=[uk, uk + "h", ak, tkey], writes=[ak])
                    P.op("act", lambda e, acc=acc, s_=s_: e.activation(cvs[s_][:], acc[:], AF.Silu), reads=[ak], writes=["f%d" % s_])
                qn, kn = f512[5], f512[6]
                qb, kb_ = b512[0], b512[1]
                for (s_, dst, dk_, scl, bdst, bk) in ((0, qn, "f5", float(128 ** -0.5), qb, "b0"), (1, kn, "f6", 1.0, kb_, "b1")):
                    sq, sqk = f512[3], "f3"
                    P.op("act", lambda e, s_=s_: e.activation(sq[:], cvs[s_][:], AF.Square), reads=["f%d" % s_], writes=[sqk])
                    P.op("pe", lambda e: e.matmul(PS[2][:, :], C("ones"), sq[:], start=True, stop=True), reads=[sqk, "cs"], writes=["ps2"])
                    P.op("act", lambda e: e.activation(f512[4][:], PS[2][:, :], AF.Sqrt, bias=EPS), reads=["ps2"], writes=["f4"])
                    P.op("dve", lambda e: e.reciprocal(f512[4][:], f512[4][:]), reads=["f4"], writes=["f4"])
                    P.op("dve", lambda e, s_=s_, dst=dst, scl=scl: e.scalar_tensor_tensor(dst[:], cvs[s_][:], scl, f512[4][:], ALU.mult, ALU.mult),
                         reads=["f%d" % s_, "f4"], writes=[dk_])
                    P.op("pool", lambda e, dst=dst, bdst=bdst: e.tensor_copy(bdst[:], dst[:]), reads=[dk_], writes=[bk])
                vs = cvs[2]
                if DBG == 11:
                    continue
                for t in range(4):
                    proj_tm(PS[2][:, 0:130], Wt, wkeys, 384, 130, slot, t, "ps2")
                    P.op("act", lambda e, t=t: e.activation(dgs[t][:], PS[2][:, 0:128], AF.Silu), reads=["ps2"], writes=["dgs%d" % t])
                    P.op("act", lambda e, t=t: e.activation(c4["beta"][:, t:t + 1], PS[2][:, 128:129], AF.Sigmoid), reads=["ps2"], writes=["c4beta"])
                    P.op("dve", lambda e, t=t: e.tensor_scalar(c4["y"][:, t:t + 1], PS[2][:, 129:130], tb[:, 17:18], None, ALU.add),
                         reads=["ps2", tkey], writes=["c4y"])
                P.op("dve", lambda e: e.scalar_tensor_tensor(c4["ay"][:], c4["y"][:], -1.0, c4["y"][:], ALU.mult, ALU.max), reads=["c4y"], writes=["c4ay"])
                P.op("act", lambda e: e.activation(c4["e"][:], c4["ay"][:], AF.Exp, scale=-1.0), reads=["c4ay"], writes=["c4e"])
                P.op("act", lambda e: e.activation(c4["l"][:], c4["e"][:], AF.Ln, bias=1.0), reads=["c4e"], writes=["c4l"])
                P.op("dve", lambda e: e.scalar_tensor_tensor(c4["l"][:], c4["y"][:], 0.0, c4["l"][:], ALU.max, ALU.add), reads=["c4y", "c4l"], writes=["c4l"])
                P.op("dve", lambda e: e.tensor_scalar(c4["g"][:], c4["l"][:], col[:, 9:10], None, ALU.mult), reads=["c4l", "col9"], writes=["c4g"])
                if DBG == 12:
                    continue
                for t in range(4):
                    ts = slice(t * 128, (t + 1) * 128)
                    tk = lambda nm: "%s%d" % (nm, t)
                    gb, gbk = f512[3], "f3"
                    P.op("dve", lambda e, t=t: e.tensor_scalar(gb[:, 0:128], C("mincl"), c4["g"][:, t:t + 1], None, ALU.mult),
                         reads=["cs", "c4g"], writes=[gbk])
                    P.op("dve", lambda e, t=t: e.tensor_scalar(gb[:, 128:256], C("ident"), c4["beta"][:, t:t + 1], None, ALU.mult),
                         reads=["cs", "c4beta"], writes=[gbk])
                    P.op("pe", lambda e: e.matmul(PS[3][:, 0:256], C("ones"), gb[:, 0:256], start=True, stop=True), reads=[gbk, "cs"], writes=["ps3"])
                    P.op("pe", lambda e, t=t: e.matmul(PS[3][:, 256:257], C("mincl"), c4["g"][:, t:t + 1], start=True, stop=True),
                         reads=["c4g", "cs"], writes=["ps3c"])
                    gcc = c4["gc"][:, t:t + 1]
                    P.op("act", lambda e, gcc=gcc: e.activation(gcc, PS[3][:, 256:257], AF.Identity), reads=["ps3c"], writes=["c4gc"])
                    P.op("act", lambda e, t=t: e.activation(c4["gl"][:, t:t + 1], PS[3][:, 127:128], AF.Identity), reads=["ps3"], writes=["c4gl"])
                    Dji, Dij = f128[0], f128[1]
                    P.op("dve", lambda e, gcc=gcc: e.tensor_scalar(Dji[:], PS[3][:, 0:128], gcc, 0.0, ALU.subtract, ALU.min),
                         reads=["ps3", "c4gc"], writes=["g0"])
                    P.op("act", lambda e: e.activation(Dji[:], Dji[:], AF.Exp), reads=["g0"], writes=["g0"])
                    P.op("dve", lambda e, gcc=gcc: e.tensor_scalar(Dij[:], PS[3][:, 0:128], gcc, 0.0, ALU.subtract, ALU.max),
                         reads=["ps3", "c4gc"], writes=["g1"])
                    P.op("act", lambda e: e.activation(Dij[:], Dij[:], AF.Exp, scale=-1.0), reads=["g1"], writes=["g1"])
                    P.op("pe", lambda e, ts=ts: e.matmul(PS[4][:, 0:128], kb_[:, ts], kb_[:, ts], start=True, stop=True), reads=["b1"], writes=["ps4a"])
                    P.op("pe", lambda e, ts=ts: e.matmul(PS[4][:, 128:256], kb_[:, ts], qb[:, ts], start=True, stop=True), reads=["b0", "b1"], writes=["ps4b"])
                    t1_, t2_, t3_, t4_ = f128[2], f128[3], f128[4], f128[5]
                    P.op("dve", lambda e: e.tensor_tensor(t1_[:], PS[4][:, 0:128], Dji[:], ALU.mult), reads=["ps4a", "g0"], writes=["g2"])
                    P.op("dve", lambda e: e.tensor_tensor(t2_[:], PS[3][:, 128:256], C("mstrict"), ALU.mult), reads=["ps3", "cs"], writes=["g3"])
                    P.op("pool", lambda e, t=t: e.tensor_tensor(Nf[t][:], t1_[:], t2_[:], ALU.mult), reads=["g2", "g3"], writes=[tk("Nf")])
                    P.op("dve", lambda e: e.tensor_tensor(t3_[:], PS[4][:, 0:128], Dij[:], ALU.mult), reads=["ps4a", "g1"], writes=["g4"])
                    P.op("dve", lambda e, t=t: e.scalar_tensor_tensor(Af[t][:], C("mlow"), c4["beta"][:, t:t + 1], t3_[:], ALU.mult, ALU.mult),
                         reads=["g4", "cs", "c4beta"], writes=[tk("Af")])
                    P.op("dve", lambda e: e.tensor_tensor(t4_[:], PS[4][:, 128:256], Dji[:], ALU.mult), reads=["ps4b", "g0"], writes=["g5"])
                    P.op("pool", lambda e, t=t: e.tensor_tensor(dnb["QKm"][t][:], t4_[:], C("mincl"), ALU.mult), reads=["g5", "cs"], writes=[tk("QKm")])
                    Eg = f128[6]
                    P.op("act", lambda e: e.activation(Eg[:], PS[3][:, 0:128], AF.Exp), reads=["ps3"], writes=["g6"])
                    P.op("dve", lambda e, t=t, ts=ts: e.tensor_tensor(dnb["qg"][t][:], qn[:, ts], Eg[:], ALU.mult), reads=["f5", "g6"], writes=[tk("qg")])
                    P.op("act", lambda e, t=t: e.activation(c4["eg"][:, t:t + 1], c4["gc"][:, t:t + 1], AF.Exp), reads=["c4gc"], writes=["c4eg"])
                    P.op("dve", lambda e, t=t: e.tensor_tensor(c4["bg"][:, t:t + 1], c4["eg"][:, t:t + 1], c4["beta"][:, t:t + 1], ALU.mult),
                         reads=["c4eg", "c4beta"], writes=["c4bg"])
                    P.op("act", lambda e, t=t: e.activation(c4["kdc"][:, t:t + 1], c4["gc"][:, t:t + 1], AF.Exp, scale=-1.0, bias=c4["gl"][:, t:t + 1]),
                         reads=["c4gc", "c4gl"], writes=["c4kdc"])
                    P.op("act", lambda e, t=t: e.activation(c4["egl"][:, t:t + 1], c4["gl"][:, t:t + 1], AF.Exp), reads=["c4gl"], writes=["c4egl"])
                    P.op("pe", lambda e, ts=ts: e.transpose(PS[4][:, 256:384], kn[:, ts], C("ident")), reads=["f6", "cs"], writes=["ps4c"])
                    P.op("pe", lambda e, ts=ts: e.transpose(PS[4][:, 384:512], vs[:, ts], C("ident")), reads=["f2", "cs"], writes=["ps4d"])
                    P.op("dve", lambda e, t=t: e.tensor_scalar(dnb["kb"][t][:], PS[4][:, 256:384], c4["bg"][:, t:t + 1], None, ALU.mult),
                         reads=["ps4c", "c4bg"], writes=[tk("kb")])
                    P.op("act", lambda e, t=t: e.activation(dnb["kd"][t][:], PS[4][:, 256:384], AF.Identity, scale=c4["kdc"][:, t:t + 1]),
                         reads=["ps4c", "c4kdc"], writes=[tk("kd")])
                    P.op("act", lambda e, t=t: e.activation(dnb["vb"][t][:], PS[4][:, 384:512], AF.Identity, scale=c4["beta"][:, t:t + 1]),
                         reads=["ps4d", "c4beta"], writes=[tk("vb")])
                    for (nm, src, m) in (("a", Af, "bd16"), ("n", Nf, "bd16"), ("L32", Af, "off32"), ("N32", Nf, "off32"),
                                         ("L64", Af, "off64"), ("N64", Nf, "off64"), ("L128", Af, "off128")):
                        P.op("pool", lambda e, nm=nm, src=src, m=m, t=t: e.tensor_tensor(dnb[nm][t][:], src[t][:], C(m), ALU.mult),
                             reads=[tk("Af") if src is Af else tk("Nf"), "cs"], writes=[tk(nm)])
                    for (nm, src, Mf) in (("Z", Af, Zf), ("Y", Nf, Yf)):
                        sk = tk("Af") if src is Af else tk("Nf")
                        P.op("pool", lambda e, src=src, Mf=Mf, t=t: e.tensor_tensor(Mf[t][:], src[t][:], C("bd16"), ALU.mult),
                             reads=[sk, "cs"], writes=["%sf%d" % (nm, t)])
                        P.op("pool", lambda e, Mf=Mf, t=t: e.tensor_tensor(Mf[t][:], C("ident"), Mf[t][:], ALU.subtract),
                             reads=["%sf%d" % (nm, t), "cs"], writes=["%sf%d" % (nm, t)])
                        P.op("pool", lambda e, Mf=Mf, nm=nm, t=t: e.tensor_copy(dnb[nm][t][:], Mf[t][:]), reads=["%sf%d" % (nm, t)], writes=[tk(nm)])

                if DBG == 13:
                    continue
                rnd = [0]

                def slotp(t, h, r=None):
                    if t % 2 == 0:
                        return (PS[5] if h == 0 else PS[6])[:, 0:128]
                    return (PS[0] if h == 0 else PS[1])[:, 0:128]

                def pkey(t, h, r=None):
                    return "pc%d_%d" % (h, t % 2)

                def mm(t, h, lhs, rhs, lk, rk):
                    if DBG == 21 and lk[0] in "YZ":
                        return
                    sp_, pk_ = slotp(t, h), pkey(t, h)
                    wk_ = [pk_]
                    P.op("pe", lambda e: e.matmul(sp_, lhs, rhs, start=True, stop=True), reads=[lk, rk], writes=wk_)

                def ev_copy(eng, t, h, dst, dk_):
                    sp_, pk_ = slotp(t, h), pkey(t, h)
                    if eng == "act" and DBG != 30:
                        P.op("act", lambda e: e.activation(dst, sp_, AF.Identity), reads=[pk_], writes=[dk_])
                    else:
                        P.op("dve", lambda e: e.tensor_copy(dst, sp_), reads=[pk_], writes=[dk_])

                def ev_comb(t, h, nm, op):
                    if DBG == 20:
                        return
                    if DBG in (19, 21):
                        ev_copy("dve", t, h, dnb[nm][t][:], "%s%d" % (nm, t))
                        return
                    sc = 1.0 if op == "add" else -1.0
                    Mf = Zf[t] if nm == "Z" else Yf[t]
                    fk = "%sf%d" % (nm, t)
                    sp_, pk_ = slotp(t, h), pkey(t, h)
                    P.op("dve", lambda e: e.tensor_tensor(Mf[:], Mf[:], sp_, ALU.add if op == "add" else ALU.subtract),
                         reads=[pk_, fk], writes=[fk])
                    P.op("dve" if DBG >= 18 else "pool", lambda e: e.tensor_copy(dnb[nm][t][:], Mf[:]), reads=[fk], writes=["%s%d" % (nm, t)])

                def chain_for(tl):
                    pa, pn = ["a"] * 4, ["n"] * 4
                    for lvl in range(3):
                        na, nn = ("p2a", "p2n") if lvl % 2 == 0 else ("p4a", "p4n")
                        rnd[0] += 1
                        for t in tl:
                            tk = lambda nm: "%s%d" % (nm, t)
                            mm(t, 0, dnb[pn[t]][t][:], dnb[pa[t]][t][:], tk(pn[t]), tk(pa[t]))
                            mm(t, 1, dnb[pa[t]][t][:], dnb[pn[t]][t][:], tk(pa[t]), tk(pn[t]))
                            ev_copy("act", t, 0, dnb[na][t][:], tk(na))
                            ev_copy("dve", t, 1, dnb[nn][t][:], tk(nn))
                        if DBG == 15:
                            break
                        rnd[0] += 1
                        for t in tl:
                            tk = lambda nm: "%s%d" % (nm, t)
                            if DBG == 22:
                                mm(t, 0, dnb["n"][t][:], dnb["a"][t][:], tk("n"), tk("a"))
                                mm(t, 1, dnb["a"][t][:], dnb["n"][t][:], tk("a"), tk("n"))
                                continue
                            if DBG == 23:
                                mm(t, 0, dnb["n"][t][:], dnb[na][t][:], tk("n"), tk(na))
                                mm(t, 1, dnb["a"][t][:], dnb[nn][t][:], tk("a"), tk(nn))
                                continue
                            mm(t, 0, dnb["Y"][t][:], dnb[na][t][:], tk("Y"), tk(na))
                            mm(t, 1, dnb["Z"][t][:], dnb[nn][t][:], tk("Z"), tk(nn))
                            ev_comb(t, 0, "Z", "add")
                            ev_comb(t, 1, "Y", "add")
                            pa[t], pn[t] = na, nn
                        if DBG in (18, 19, 20, 21, 22, 23):
                            break
                    if DBG in (15, 16, 18, 19, 20, 21, 22, 23, 30, 31):
                        return
                    for (Ln, Nn) in (("L32", "N32"), ("L64", "N64")):
                        rnd[0] += 1
                        for t in tl:
                            tk = lambda nm: "%s%d" % (nm, t)
                            mm(t, 0, dnb[Nn][t][:], dnb["Z"][t][:], tk(Nn), tk("Z"))
                            mm(t, 1, dnb[Ln][t][:], dnb["Y"][t][:], tk(Ln), tk("Y"))
                            ev_copy("act", t, 0, dnb["W1"][t][:], tk("W1"))
                            ev_copy("dve", t, 1, dnb["V1"][t][:], tk("V1"))
                        rnd[0] += 1
                        for t in tl:
                            tk = lambda nm: "%s%d" % (nm, t)
                            mm(t, 0, dnb["Y"][t][:], dnb["W1"][t][:], tk("Y"), tk("W1"))
                            mm(t, 1, dnb["Z"][t][:], dnb["V1"][t][:], tk("Z"), tk("V1"))
                            ev_comb(t, 0, "Z", "sub")
                            ev_comb(t, 1, "Y", "sub")
                    rnd[0] += 1
                    for t in tl:
                        tk = lambda nm: "%s%d" % (nm, t)
                        mm(t, 1, dnb["L128"][t][:], dnb["Y"][t][:], tk("L128"), tk("Y"))
                        ev_copy("act", t, 1, dnb["V1"][t][:], tk("V1"))
                    rnd[0] += 1
                    for t in tl:
                        tk = lambda nm: "%s%d" % (nm, t)
                        mm(t, 1, dnb["Z"][t][:], dnb["V1"][t][:], tk("Z"), tk("V1"))
                        ev_comb(t, 1, "Y", "sub")
                        mm(t, 0, dnb["kb"][t][:], dnb["Y"][t][:], tk("kb"), tk("Y"))
                        P.op("act", lambda e, t=t, sp_=slotp(t, 0): e.activation(dnb["nwT"][t][:], sp_, AF.Identity, scale=-1.0),
                             reads=[pkey(t, 0)], writes=[tk("nwT")])

                Yc = ["Y"] * 4
                if DBG == 33:
                    chain_for([0, 1, 2, 3])
                elif DBG == 35:
                    for tsel in range(4):
                        chain_for([tsel])
                else:
                    chain_for([0, 1])
                    chain_for([2, 3])
                if DBG == 14:
                    continue
                for t in range(4):
                    tk = lambda nm: "%s%d" % (nm, t)
                    Yt = dnb[Yc[t]][t]
                    P.op("pe", lambda e, Yt=Yt, t=t: e.matmul(PS[3][:, 0:128], Yt[:], dnb["vb"][t][:], start=True, stop=False),
                         reads=[tk(Yc[t]), tk("vb")], writes=["ps3"])
                    P.op("pe", lambda e, t=t: e.matmul(PS[3][:, 0:128], dnb["nwT"][t][:], Sb[:, 0:128], start=False, stop=True),
                         reads=[tk("nwT"), "Sb"], writes=["ps3"])
                    P.op("act", lambda e, t=t: e.activation(dnb["vnew"][t][:], PS[3][:, 0:128], AF.Identity), reads=["ps3"], writes=[tk("vnew")])
                    P.op("pe", lambda e, t=t: e.matmul(PS[3][:, 128:256], dnb["qg"][t][:], Sb[:, 0:128], start=True, stop=False),
                         reads=[tk("qg"), "Sb"], writes=["ps3o"])
                    P.op("pe", lambda e, t=t: e.matmul(PS[3][:, 128:256], dnb["QKm"][t][:], dnb["vnew"][t][:], start=False, stop=True),
                         reads=[tk("QKm"), tk("vnew")], writes=["ps3o"])
                    P.op("pe", lambda e, t=t: e.matmul(PS[3][:, 256:384], dnb["kd"][t][:], dnb["vnew"][t][:], start=True, stop=True),
                         reads=[tk("kd"), tk("vnew")], writes=["ps3d"])
                    P.op("dve", lambda e, t=t: e.scalar_tensor_tensor(S[:, 0:128], S[:, 0:128], c4["egl"][:, t:t + 1], PS[3][:, 256:384], ALU.mult, ALU.add),
                         reads=["S", "c4egl", "ps3d"], writes=["S"])
                    P.op("pool", lambda e: e.tensor_copy(Sb[:, 0:128], S[:, 0:128]), reads=["S"], writes=["Sb"])
                    finish_tile(p, blk, t, PS[3][:, 128:256], "ps3o", 128, dgs[t][:], "dgs%d" % t, normg, tkey, mslot, False)
                store_mix(p, blk, mslot, 1)

        for p in passes:
            if PASSES[p][0] == "dn":
                dn_pass(p)
            else:
                la_pass(p)
        P.emit(nc)
    return nc


_ST = np.concatenate([[0], np.cumsum([512, 512, 1024, 1024, 512, 512, 1024, 1024, 16, 6144, 2048, 16, 16])]).astype(np.int64)


def k1_inputs(xT_b, pos_b, l, hg, inp, passes=None, single=False):
    w_in = inp["w_in"][l]
    m = {"xT": xT_b, "pos": np.ascontiguousarray(pos_b), "cst": CST}
    rep = lambda v, n=128: np.broadcast_to(np.asarray(v, np.float32)[None, :], (n, len(v)))
    for p, (kind, ncols) in enumerate(PASSES):
        if passes is not None and p not in passes:
            continue
        tab = np.zeros((128, NTAB), np.float32)
        if kind == "ret":
            h = hg
            cols = np.concatenate([np.arange(_ST[0] + h * 128, _ST[0] + (h + 1) * 128), np.arange(_ST[1] + h * 128, _ST[1] + (h + 1) * 128),
                                   np.arange(_ST[2] + h * 256, _ST[2] + (h + 1) * 256), np.arange(_ST[3] + h * 256, _ST[3] + (h + 1) * 256),
                                   np.arange(_ST[8], _ST[8] + 16)])
            lg = np.log(np.float32(1.0) - np.power(np.float32(2.0), np.float32(-5.0 - h)))
            i = np.arange(128, dtype=np.float32)
            rq = np.exp((i + 1.0) * lg) * (128 ** -0.5)
            rk = np.exp(-(i + 1.0) * lg)
            tab[:, 0:128] = rep(rq)
            tab[:, 128:256] = rep(rk)
            tab[:, 256] = np.exp(128.0 * lg)
            tab[:, 384:640] = rep(inp["ret_norm_g"][l][h * 256:(h + 1) * 256])
        elif kind == "gla":
            h = hg
            cols = np.concatenate([np.arange(_ST[4] + h * 128, _ST[4] + (h + 1) * 128), np.arange(_ST[5] + h * 128, _ST[5] + (h + 1) * 128),
                                   np.arange(_ST[6] + h * 256, _ST[6] + (h + 1) * 256), np.arange(_ST[7] + h * 256, _ST[7] + (h + 1) * 256),
                                   np.arange(_ST[8], _ST[8] + 16)])
            tab[0:16, 0:128] = inp["gla_w_a2"][l][:, h * 128:(h + 1) * 128]
            tab[:, 256] = inp["gla_b_a"][l][h * 128:(h + 1) * 128]
            tab[:, 384:640] = rep(inp["gla_norm_g"][l][h * 256:(h + 1) * 256])
        else:
            idx = [1, 3, 4, 5].index(p)
            h = hg * 4 + idx
            b0 = _ST[9]
            cols = np.concatenate([np.arange(b0 + h * 128, b0 + (h + 1) * 128), np.arange(b0 + 2048 + h * 128, b0 + 2048 + (h + 1) * 128),
                                   np.arange(b0 + 4096 + h * 128, b0 + 4096 + (h + 1) * 128),
                                   np.arange(_ST[10] + h * 128, _ST[10] + (h + 1) * 128), [_ST[11] + h], [_ST[12] + h]])
            cw = inp["dn_conv_w"][l]
            for s_ in range(3):
                tab[:, s_ * 4:(s_ + 1) * 4] = cw[:, s_ * 2048 + h * 128:s_ * 2048 + (h + 1) * 128].T
            tab[:, 16] = inp["dn_a_log"][l][h]
            tab[:, 17] = inp["dn_dt_bias"][l][h]
            tab[:, 384:512] = rep(inp["dn_norm_g"][l][h * 128:(h + 1) * 128])
        wp = w_in[:, cols]
        m["w" if single else "w%d" % p] = np.ascontiguousarray(wp.reshape(NCH, 128, ncols).transpose(1, 0, 2).reshape(128, NCH * ncols))
        m["tab" if single else "tab%d" % p] = tab
    return m


def mix_row_perm():
    perm = []
    for hg in range(4):
        rows = np.zeros(1024, np.int64)
        rows[0:256] = np.arange(hg * 256, (hg + 1) * 256)
        rows[256:512] = 1024 + np.arange(hg * 256, (hg + 1) * 256)
        for i in range(4):
            h = hg * 4 + i
            rows[512 + i * 128:512 + (i + 1) * 128] = 2048 + np.arange(h * 128, (h + 1) * 128)
        perm.append(rows)
    return np.concatenate(perm)


def block_x(xT, nb):
    return np.ascontiguousarray(xT.reshape(NCH, 128, nb, 512).transpose(2, 1, 0, 3).reshape(nb, 128, NCH * 512))


def unblock_x(xb):
    nb = xb.shape[0]
    return np.ascontiguousarray(xb.reshape(nb, 128, NCH, 512).transpose(2, 1, 0, 3).reshape(D, nb * 512))


def build_k2(NH=2):
    nc = bass.Bass("TRN2", target_bir_lowering=False)
    mixc = nc.dram_tensor("mixc", [NH, 128, NCH * 512], BF16, kind="ExternalInput").ap()
    xres = nc.dram_tensor("xres", [NH, 128, NCH * 512], F32, kind="ExternalInput").ap()
    wo = nc.dram_tensor("wo", [8, 128, 4 * NCH * 128], F32, kind="ExternalInput").ap()
    lgb = nc.dram_tensor("lgb", [128, 2 * NCH], F32, kind="ExternalInput").ap()
    cst = nc.dram_tensor("cst", [128, NCST], F32, kind="ExternalInput").ap()
    xo = nc.dram_tensor("xo", [NH, 128, NCH * 512], F32, kind="ExternalOutput").ap()
    P = Prog()
    with contextlib.ExitStack() as st:
        def sb(name, shape, dt=F32):
            return st.enter_context(nc.sbuf_tensor(name, shape, dt))
        cs = sb("cs", [128, NCST])
        gb = sb("gb", [128, 2 * NCH])
        mx = sb("mx", [128, NCH, 512], BF16)
        z = sb("z", [128, NCH, 512])
        wt = [sb("wt%d" % i, [128, 4, NCH, 128], BF16) for i in range(2)]
        sq = sb("sq", [128, 512])
        s1 = sb("s1", [128, 512])
        s2 = sb("s2", [128, 512])
        mean = sb("mean", [128, 512])
        rstd = sb("rstd", [128, 512])
        PS = [st.enter_context(nc.psum_tensor("ps%d" % i, [128, 512], F32)) for i in range(4)]
        ones = cs[:, COFF["ones"]:COFF["ones"] + 128]
        P.dma("sp", cs[:], cst[:, :], writes=["cs"])
        P.dma("sp", gb[:], lgb[:, :], writes=["gb"])
        wcount = 0
        for half in range(NH):
            for q in range(2):
                P.dma("sp", mx[:, q * 16:(q + 1) * 16, :], mixc[half][:, q * 8192:(q + 1) * 8192].rearrange("p (c t) -> p c t", t=512),
                      writes=["mx%d" % q])
            for q in range(4):
                P.dma("sp", z[:, q * 8:(q + 1) * 8, :], xres[half][:, q * 4096:(q + 1) * 4096].rearrange("p (c t) -> p c t", t=512),
                      writes=["z%d" % d_ for d_ in range(q * 8, (q + 1) * 8)])
            for dc in range(NCH):
                dcg, j = dc // 4, dc % 4
                w = wt[(wcount // 4) % 2]
                wk = "wt%d" % ((wcount // 4) % 2)
                if j == 0:
                    for hh in range(2):
                        P.dma("pool", w[:, hh * 2:(hh + 1) * 2, :, :],
                              wo[dcg][:, hh * 8192:(hh + 1) * 8192].rearrange("p (j c n) -> p j c n", j=2, n=128), writes=[wk + "_%d" % hh])
                wcount += 1
                ps = PS[dc % 2]
                pk = "ps%d" % (dc % 2)
                for ec in range(NCH):
                    P.op("pe", lambda e, ec=ec, w=w, ps=ps, j=j: e.matmul(ps[:, :], w[:, j, ec, :], mx[:, ec, :], start=(ec == 0), stop=(ec == NCH - 1)),
                         reads=[wk + "_%d" % (j // 2), "mx0", "mx1"], writes=[pk])
                zk = "z%d" % dc
                P.op("dve", lambda e, dc=dc, ps=ps: e.scalar_tensor_tensor(z[:, dc, :], z[:, dc, :], ALPHA, ps[:, :], ALU.mult, ALU.add),
                     reads=[zk, pk], writes=[zk])
                P.op("act", lambda e, dc=dc: e.activation(sq[:], z[:, dc, :], AF.Square), reads=[zk], writes=["sq"])
                P.op("pe", lambda e, dc=dc: e.matmul(PS[2][:, :], ones, z[:, dc, :], start=True, stop=True), reads=[zk, "cs"], writes=["ps2"])
                P.op("pe", lambda e: e.matmul(PS[3][:, :], ones, sq[:], start=True, stop=True), reads=["sq", "cs"], writes=["ps3"])
                if dc == 0:
                    P.op("dve", lambda e: e.tensor_copy(s1[:], PS[2][:, :]), reads=["ps2"], writes=["s1"])
                    P.op("dve", lambda e: e.tensor_copy(s2[:], PS[3][:, :]), reads=["ps3"], writes=["s2"])
                else:
                    P.op("dve", lambda e: e.tensor_tensor(s1[:], s1[:], PS[2][:, :], ALU.add), reads=["ps2", "s1"], writes=["s1"])
                    P.op("dve", lambda e: e.tensor_tensor(s2[:], s2[:], PS[3][:, :], ALU.add), reads=["ps3", "s2"], writes=["s2"])
            P.op("dve", lambda e: e.tensor_scalar(mean[:], s1[:], 1.0 / D, None, ALU.mult), reads=["s1"], writes=["mean"])
            P.op("dve", lambda e: e.tensor_tensor(sq[:], mean[:], mean[:], ALU.mult), reads=["mean"], writes=["sq"])
            P.op("dve", lambda e: e.scalar_tensor_tensor(rstd[:], s2[:], 1.0 / D, sq[:], ALU.mult, ALU.subtract), reads=["s2", "sq"], writes=["rstd"])
            P.op("act", lambda e: e.activation(rstd[:], rstd[:], AF.Sqrt, bias=LN_EPS), reads=["rstd"], writes=["rstd"])
            P.op("dve", lambda e: e.reciprocal(rstd[:], rstd[:]), reads=["rstd"], writes=["rstd"])
            for dc in range(NCH):
                zk = "z%d" % dc
                P.op("dve", lambda e, dc=dc: e.tensor_tensor(z[:, dc, :], z[:, dc, :], mean[:], ALU.subtract), reads=[zk, "mean"], writes=[zk])
                P.op("pool", lambda e, dc=dc: e.tensor_tensor(z[:, dc, :], z[:, dc, :], rstd[:], ALU.mult), reads=[zk, "rstd"], writes=[zk])
                P.op("dve", lambda e, dc=dc: e.tensor_scalar(z[:, dc, :], z[:, dc, :], gb[:, dc:dc + 1], gb[:, NCH + dc:NCH + dc + 1], ALU.mult, ALU.add),
                     reads=[zk, "gb"], writes=[zk])
                if dc % 8 == 7:
                    q = dc // 8
                    P.dma("sp", xo[half][:, q * 4096:(q + 1) * 4096].rearrange("p (c t) -> p c t", t=512), z[:, q * 8:(q + 1) * 8, :],
                          reads=["z%d" % d_ for d_ in range(q * 8, (q + 1) * 8)])
        P.emit(nc)
    return nc


def k2_weights(inp, l):
    perm = mix_row_perm()
    w = inp["w_out"][l][perm]
    wo = np.ascontiguousarray(w.reshape(NCH, 128, 8, 4, 128).transpose(2, 1, 3, 0, 4).reshape(8, 128, 4 * NCH * 128))
    lgb = np.ascontiguousarray(np.concatenate([inp["ln_g"][l].reshape(NCH, 128).T, inp["ln_b"][l].reshape(NCH, 128).T], axis=1).astype(np.float32))
    return wo, lgb


def kernel(**inp):
    inp = {k: np.asarray(v) for k, v in inp.items()}
    x = inp["x"]
    B = x.shape[0]
    NB = SEQ // 512
    pos = inp["positions"].astype(np.int32)
    xblk = [block_x(np.ascontiguousarray(x[b].T), NB) for b in range(B)]
    progs = {"ret": build_k1(SEQ, [0], single=True), "gla": build_k1(SEQ, [2], single=True), "dn": build_k1(SEQ, [1], single=True)}
    nc2 = build_k2(2)
    for l in range(DEPTH):
        mixs = [np.zeros((NB, 128, 8, 512), ml_dtypes.bfloat16) for _ in range(8)]
        for p in range(6):
            kind = PASSES[p][0]
            in_maps = [k1_inputs(xblk[c // 4], pos[c // 4:c // 4 + 1], l, c % 4, inp, passes=[p], single=True) for c in range(8)]
            r1 = run_bass_kernel_spmd(progs[kind], in_maps, core_ids=list(range(8)))
            rc0 = PASS_ROWS[p] // 128
            n = 1 if kind == "dn" else 2
            for c in range(8):
                mixs[c][:, :, rc0:rc0 + n, :] = np.asarray(r1.results[c]["mixT"]).reshape(NB, 128, n, 512)
            del in_maps, r1
        wo, lgb = k2_weights(inp, l)
        in2 = []
        for c in range(8):
            b, sq_ = c // 4, c % 4
            mixc = np.ascontiguousarray(np.concatenate([mixs[b * 4 + hg][sq_ * 2:sq_ * 2 + 2] for hg in range(4)], axis=2).reshape(2, 128, NCH * 512))
            in2.append({"mixc": mixc, "xres": np.ascontiguousarray(xblk[b][sq_ * 2:sq_ * 2 + 2]), "wo": wo, "lgb": lgb, "cst": CST})
        r2 = run_bass_kernel_spmd(nc2, in2, core_ids=list(range(8)))
        xblk = [np.ascontiguousarray(np.concatenate([np.asarray(r2.results[b * 4 + s]["xo"]) for s in range(4)], axis=0)) for b in range(B)]
        del in2, r2, mixs
    out = np.stack([unblock_x(xblk[b]).T for b in range(B)], axis=0).astype(np.float32)
    return out
```

```python
import contextlib
import numpy as np
import ml_dtypes
import concourse.bass as bass
import concourse.mybir as mybir
from concourse.bass_utils import run_bass_kernel_spmd

F32 = mybir.dt.float32
BF16 = mybir.dt.bfloat16
I32 = mybir.dt.int32
AF = mybir.ActivationFunctionType
ALU = mybir.AluOpType

D = 4096
SEQ = 4096
DEPTH = 2
DIN = 14384
EPS = 1e-6
LN_EPS = 1e-5
ALPHA = float((2 * DEPTH) ** 0.25)
NCH = 32
DBG = 0

ENGS = ("pe", "dve", "act", "pool", "sp")
NDSEM = 6


class _Op:
    __slots__ = ("eng", "fn", "idx", "is_dma", "dsem", "dval", "waits", "mark")


class Prog:
    def __init__(self):
        self.ops = {e: [] for e in ENGS}
        self.lastw = {}
        self.readers = {}
        self.ndma = {e: 0 for e in ENGS}
        self.known = {e: {} for e in ENGS}
        self.snap = {e: [] for e in ENGS}

    def _need(self, eng, dep, waits):
        kn = self.known[eng]
        if dep[0] == "c":
            _, e2, idx = dep
            if e2 == eng:
                if e2 == "pe":
                    return
                if idx < len(self.ops[eng]) - 3:
                    return
            key = ("c", e2)
            if kn.get(key, -1) >= idx:
                return
            waits.append(dep)
            kn[key] = idx
            for k, v in self.snap[e2][idx].items():
                if kn.get(k, -1) < v:
                    kn[k] = v
        else:
            _, e2, slot, val = dep
            key = ("d", e2, slot)
            if kn.get(key, -1) >= val:
                return
            waits.append(dep)
            kn[key] = val

    def _add(self, eng, fn, reads, writes, is_dma):
        op = _Op()
        op.eng = eng
        op.fn = fn
        op.idx = len(self.ops[eng])
        op.is_dma = is_dma
        op.mark = False
        op.waits = []
        if is_dma:
            j = self.ndma[eng]
            self.ndma[eng] += 1
            op.dsem = j % NDSEM
            op.dval = 16 * (j // NDSEM + 1)
            if j >= NDSEM:
                self._need(eng, ("d", eng, op.dsem, op.dval - 16), op.waits)
            me = ("d", eng, op.dsem, op.dval)
        else:
            me = ("c", eng, op.idx)
        for r in reads:
            w = self.lastw.get(r)
            if w is not None:
                self._need(eng, w, op.waits)
        for w_ in writes:
            w = self.lastw.get(w_)
            if w is not None:
                self._need(eng, w, op.waits)
            for rd in self.readers.get(w_, ()):
                if rd != me:
                    self._need(eng, rd, op.waits)
        for r in reads:
            self.readers.setdefault(r, []).append(me)
        for w_ in writes:
            self.lastw[w_] = me
            self.readers[w_] = []
        self.ops[eng].append(op)
        self.snap[eng].append(dict(self.known[eng]))
        return op

    def op(self, eng, fn, reads=(), writes=()):
        return self._add(eng, fn, reads, writes, False)

    def dma(self, eng, out, in_, reads=(), writes=()):
        return self._add(eng, lambda e: e.dma_start(out=out, in_=in_), reads, writes, True)

    def emit(self, nc):
        for e in ENGS:
            for op in self.ops[e]:
                for d in op.waits:
                    if d[0] == "c":
                        self.ops[d[1]][d[2]].mark = True
        cnt = {}
        for e in ENGS:
            c = 0
            arr = []
            for op in self.ops[e]:
                if op.mark:
                    c += 1
                arr.append(c)
            cnt[e] = arr
        with contextlib.ExitStack() as st:
            csem = {e: st.enter_context(nc.semaphore("c_" + e)) for e in ENGS}
            dsem = {e: [st.enter_context(nc.semaphore("d_%s%d" % (e, i))) for i in range(NDSEM)]
                    for e in ENGS if self.ndma[e] > 0}
            block = st.enter_context(nc.Block())
            handles = {"pe": block.tensor, "dve": block.vector, "act": block.scalar,
                       "pool": block.gpsimd, "sp": block.sync}

            def mk(e):
                def body(h):
                    for op in self.ops[e]:
                        for d in op.waits:
                            if d[0] == "c":
                                h.wait_ge(csem[d[1]], cnt[d[1]][d[2]])
                            else:
                                h.wait_ge(dsem[d[1]][d[2]], d[3])
                        ins = op.fn(h)
                        if op.is_dma:
                            ins.then_inc(dsem[e][op.dsem], 16)
                        elif op.mark:
                            ins.then_inc(csem[e], 1)
                    if self.ndma[e] > 0:
                        n = self.ndma[e]
                        for s in range(min(NDSEM, n)):
                            last = ((n - 1 - s) // NDSEM) * NDSEM + s
                            h.wait_ge(dsem[e][s], 16 * (last // NDSEM + 1))
                return body

            for e in ENGS:
                if self.ops[e]:
                    handles[e](mk(e))


def _consts():
    i = np.arange(128)
    c = {}
    c["ident"] = np.eye(128, dtype=np.float32)
    c["ones"] = np.ones((128, 128), np.float32)
    perm = np.zeros((128, 128), np.float32)
    perm[(i + 64) % 128, i] = 1.0
    c["perm"] = perm
    c["mincl"] = (i[None, :] >= i[:, None]).astype(np.float32)
    c["mstrict"] = (i[None, :] > i[:, None]).astype(np.float32)
    c["mlow"] = (i[:, None] > i[None, :]).astype(np.float32)
    blk = lambda s: ((i[:, None] // s) == (i[None, :] // s)).astype(np.float32)
    c["bd16"] = blk(16)
    c["off32"] = blk(32) - blk(16)
    c["off64"] = blk(64) - blk(32)
    c["off128"] = 1.0 - blk(64)
    t = np.arange(512)
    c["reset"] = np.broadcast_to(((t % 128) != 0).astype(np.float32)[None, :], (128, 512)).copy()
    names = ["ident", "ones", "perm", "mincl", "mstrict", "mlow", "bd16", "off32", "off64", "off128", "reset"]
    off = {}
    o = 0
    for n in names:
        off[n] = o
        o += c[n].shape[1]
    half = 64
    inv = (10000.0 ** (-np.arange(half, dtype=np.float32) / half)).astype(np.float32)
    invf = np.concatenate([inv, inv]).astype(np.float32)
    sgn = np.concatenate([-np.ones(64), np.ones(64)]).astype(np.float32)
    cols = np.stack([invf, sgn], axis=1).astype(np.float32)
    off["invf"] = o
    off["sgn"] = o + 1
    o += 2
    arr = np.concatenate([c[n] for n in names] + [cols], axis=1).astype(np.float32)
    return arr, off


CST, COFF = _consts()
NCST = CST.shape[1]

LA_COLS = 784
DN_COLS = 514
PASSES = [("ret", LA_COLS), ("dn", DN_COLS), ("gla", LA_COLS), ("dn", DN_COLS), ("dn", DN_COLS), ("dn", DN_COLS)]
PASS_ROWS = [0, 512, 256, 640, 768, 896]
NTAB = 640


def build_k1(T, passes=None, single=False):
    if passes is None:
        passes = list(range(6))
    NB = T // 512
    nc = bass.Bass("TRN2", target_bir_lowering=False)
    xT = nc.dram_tensor("xT", [NB, 128, NCH * 512], F32, kind="ExternalInput").ap()
    pos = nc.dram_tensor("pos", [1, T], I32, kind="ExternalInput").ap()
    cst = nc.dram_tensor("cst", [128, NCST], F32, kind="ExternalInput").ap()
    if single:
        p0 = passes[0]
        Wd = {p0: nc.dram_tensor("w", [128, NCH * PASSES[p0][1]], F32, kind="ExternalInput").ap()}
        tabd = {p0: nc.dram_tensor("tab", [128, NTAB], F32, kind="ExternalInput").ap()}
        nrows = 128 if PASSES[p0][0] == "dn" else 256
        mixT = nc.dram_tensor("mixT", [NB, 128, (nrows // 128) * 512], BF16, kind="ExternalOutput").ap()
    else:
        Wd = {p: nc.dram_tensor("w%d" % p, [128, NCH * PASSES[p][1]], F32, kind="ExternalInput").ap() for p in sorted(set(passes))}
        tabd = {p: nc.dram_tensor("tab%d" % p, [128, NTAB], F32, kind="ExternalInput").ap() for p in sorted(set(passes))}
        mixT = nc.dram_tensor("mixT", [NB, 128, 8 * 512], BF16, kind="ExternalOutput").ap()
    xb = nc.dram_tensor("xb", [NB, 128, NCH * 512], BF16).ap()

    P = Prog()
    with contextlib.ExitStack() as st:
        def sb(name, shape, dt=F32):
            return st.enter_context(nc.sbuf_tensor(name, shape, dt))

        def psum(name, shape, dt=F32):
            return st.enter_context(nc.psum_tensor(name, shape, dt))

        WA = sb("WA", [128, NCH, LA_COLS], BF16)
        xt = [sb("xt%d" % i, [128, NCH, 512], BF16) for i in range(2)]
        cs = sb("cs", [128, NCST])
        cb = sb("cb", [128, 128 * 10], BF16)
        tab = [sb("tabA", [128, NTAB]), sb("tabB", [128, NTAB])]
        PS = [psum("ps%d" % i, [128, 512]) for i in range(7)]
        PSB = psum("psb", [128, 1024], BF16)

        def C(name, w=128):
            o = COFF[name]
            return cs[:, o:o + w]

        def CB(name):
            o = COFF[name]
            return cb[:, o:o + 128]

        f512 = [sb("f512_%d" % i, [128, 512]) for i in range(7)]
        b512 = [sb("b512_%d" % i, [128, 512], BF16) for i in range(3)]
        i512 = sb("i512", [128, 512], I32)
        S = sb("S", [128, 256])
        Sb = sb("Sb", [128, 256], BF16)
        vbf = sb("vbf", [128, 256], BF16)
        gs = sb("gs", [128, 256])
        ktok = sb("ktok", [128, 128], BF16)
        PT = sb("PT", [128, 128], BF16)
        tmpS = sb("tmpS", [128, 256])
        onrm = sb("onrm", [128, 256])
        mixb = sb("mixb", [128, 256], BF16)
        mixTs = [sb("mixTs%d" % i, [128, 1024], BF16) for i in range(2)]
        st6 = sb("st6", [128, 6])
        mv = sb("mv", [128, 2])
        col = sb("col", [128, 64])
        a1f = sb("a1f", [16, 512])
        U = [sb("U%d" % i, [128, 515]) for i in range(3)]
        f128 = [sb("f128_%d" % i, [128, 128]) for i in range(8)]
        dnb = {}
        for nm in ["a", "n", "L32", "N32", "L64", "N64", "L128", "Z", "Y", "p2a", "p2n", "p4a", "p4n",
                   "W1", "V1", "QKm", "qg", "kb", "kd", "vb", "nwT", "vnew"]:
            dnb[nm] = [sb("dn_%s%d" % (nm, t), [128, 128], BF16) for t in range(4)]
        dgs = [sb("dgs%d" % t, [128, 128]) for t in range(4)]
        Af = [sb("Af%d" % t, [128, 128]) for t in range(4)]
        Nf = [sb("Nf%d" % t, [128, 128]) for t in range(4)]
        Zf = [sb("Zf%d" % t, [128, 128]) for t in range(4)]
        Yf = [sb("Yf%d" % t, [128, 128]) for t in range(4)]
        c4 = {nm: sb("c4_" + nm, [128, 4]) for nm in ["beta", "y", "ay", "e", "l", "g", "gc", "eg", "bg", "kdc", "gl", "egl"]}

        P.dma("sp", cs[:], cst[:, :], writes=["cs"])
        P.op("dve", lambda e: e.tensor_copy(cb[:], cs[:, 0:1280]), reads=["cs"], writes=["cb"])
        for blk in range(NB):
            for hh in range(2):
                P.dma("pool", xb[blk][:, hh * 8192:(hh + 1) * 8192], xT[blk][:, hh * 8192:(hh + 1) * 8192], writes=["xb_%d_%d" % (blk, hh)])

        xcount = [0]

        def load_x(blk):
            slot = xcount[0] % 2
            xcount[0] += 1
            for hh in range(2):
                src = xb[blk][:, hh * 8192:(hh + 1) * 8192].rearrange("p (c t) -> p c t", t=512)
                P.dma("sp", xt[slot][:, hh * 16:(hh + 1) * 16, :], src,
                      reads=["xb_%d_%d" % (blk, hh)], writes=["xt%d_%d" % (slot, hh)])
            return slot

        pref = {}

        def get_x(blk):
            slot = pref.pop(blk) if blk in pref else load_x(blk)
            if blk + 1 < NB:
                pref[blk + 1] = load_x(blk + 1)
            return slot

        def xkeys(slot):
            return ["xt%d_0" % slot, "xt%d_1" % slot]

        def load_w(p):
            kind, ncols = PASSES[p]
            Wt, key = WA, "WA"
            for q in range(4):
                src = Wd[p][:, q * 8 * ncols:(q + 1) * 8 * ncols].rearrange("p (c n) -> p c n", n=ncols)
                P.dma("pool", Wt[:, q * 8:(q + 1) * 8, 0:ncols], src, writes=["%s_%d" % (key, q)])
            tslot = 0 if single else p % 2
            P.dma("sp", tab[tslot][:], tabd[p][:, :], writes=["tab%d" % tslot])
            return Wt, ["%s_%d" % (key, q) for q in range(4)], tab[tslot], "tab%d" % tslot

        def proj_fm(ps_ap, Wt, wkeys, c0, ncol, slot, pskey, M=128):
            for c in range(NCH):
                P.op("pe", lambda e, c=c: e.matmul(ps_ap, Wt[:, c, c0:c0 + ncol], xt[slot][:, c, :],
                                                  start=(c == 0), stop=(c == NCH - 1)),
                     reads=wkeys + xkeys(slot), writes=pskey if isinstance(pskey, list) else [pskey])

        def proj_tm(ps_ap, Wt, wkeys, c0, ncol, slot, t, pskey):
            for c in range(NCH):
                P.op("pe", lambda e, c=c: e.matmul(ps_ap, xt[slot][:, c, t * 128:(t + 1) * 128], Wt[:, c, c0:c0 + ncol],
                                                  start=(c == 0), stop=(c == NCH - 1)),
                     reads=wkeys + xkeys(slot), writes=[pskey])

        def finish_tile(p, blk, t, ps_o, okey, ncol, gate_ap, gkey, normg_ap, tkey, mslot, use_ln):
            if use_ln and DBG != 6:
                P.op("dve", lambda e: e.bn_stats(st6[:], ps_o), reads=[okey], writes=["st6"])
                P.op("dve", lambda e: e.bn_aggr(mv[:], st6[:]), reads=["st6"], writes=["mv"])
                P.op("act", lambda e: e.activation(col[:, 0:1], mv[:, 1:2], AF.Sqrt, bias=EPS), reads=["mv"], writes=["col0"])
                P.op("dve", lambda e: e.reciprocal(col[:, 1:2], col[:, 0:1]), reads=["col0"], writes=["col1"])
                P.op("dve", lambda e: e.tensor_scalar(onrm[:, 0:ncol], ps_o, mv[:, 0:1], col[:, 1:2], ALU.subtract, ALU.mult),
                     reads=[okey, "mv", "col1"], writes=["onrm"])
            else:
                P.op("act", lambda e: e.activation(tmpS[:, 0:ncol], ps_o, AF.Square, accum_out=col[:, 2:3]),
                     reads=[okey], writes=["tmpS", "col2"])
                P.op("act", lambda e: e.activation(col[:, 0:1], col[:, 2:3], AF.Sqrt, bias=EPS, scale=1.0 / ncol),
                     reads=["col2"], writes=["col0"])
                P.op("dve", lambda e: e.reciprocal(col[:, 1:2], col[:, 0:1]), reads=["col0"], writes=["col1"])
                P.op("dve", lambda e: e.tensor_scalar(onrm[:, 0:ncol], ps_o, col[:, 1:2], None, ALU.mult),
                     reads=[okey, "col1"], writes=["onrm"])
            if DBG == 7:
                return
            P.op("pool", lambda e: e.tensor_tensor(onrm[:, 0:ncol], onrm[:, 0:ncol], normg_ap, ALU.mult),
                 reads=["onrm", tkey], writes=["onrm"])
            if DBG == 8:
                return
            P.op("dve", lambda e: e.tensor_tensor(mixb[:, 0:ncol], onrm[:, 0:ncol], gate_ap, ALU.mult),
                 reads=["onrm", gkey], writes=["mixb"])
            if DBG == 9:
                return
            for c in range(ncol // 128):
                P.op("pe", lambda e, c=c: e.transpose(PSB[:, 128 + c * 128:128 + (c + 1) * 128], mixb[:, c * 128:(c + 1) * 128], CB("ident")),
                     reads=["mixb", "cb"], writes=["psb_m%d" % c])
                if DBG == 10:
                    continue
                P.op("dve", lambda e, c=c: e.tensor_copy(mixTs[mslot][:, c * 512 + t * 128:c * 512 + (t + 1) * 128], PSB[:, 128 + c * 128:128 + (c + 1) * 128]),
                     reads=["psb_m%d" % c], writes=["mixTs%d" % mslot])

        def store_mix(p, blk, mslot, nchunk):
            r0 = 0 if single else PASS_ROWS[p]
            rc0 = r0 // 128
            P.dma("sp", mixT[blk][:, rc0 * 512:(rc0 + nchunk) * 512], mixTs[mslot][:, 0:nchunk * 512], reads=["mixTs%d" % mslot])

        mcount = [0]

        def la_pass(p):
            kind = PASSES[p][0]
            Wt, wkeys, tb, tkey = load_w(p)
            normg = tb[:, 384:640]
            P.op("dve", lambda e: e.memset(S[:], 0.0), writes=["S"])
            P.op("dve", lambda e: e.memset(Sb[:], 0.0), writes=["Sb"])
            for blk in range(NB):
                slot = get_x(blk)
                mslot = mcount[0] % 2
                mcount[0] += 1
                qps, kps, cps, dps = PS[0], PS[1], PS[2], PS[6]
                proj_fm(qps[:, :], Wt, wkeys, 0, 128, slot, "ps0")
                proj_fm(kps[:, :], Wt, wkeys, 128, 128, slot, "ps1")
                qt, kt = b512[0], b512[1]
                if DBG == 1:
                    continue
                if kind == "ret":
                    ang, t1, t2, C1, S1, qf, kf_ = f512[0], f512[1], f512[2], f512[3], f512[4], f512[5], f512[6]
                    P.dma("sp", i512[:], pos[0:1, blk * 512:(blk + 1) * 512].partition_broadcast(128), writes=["i512"])
                    P.op("dve", lambda e: e.tensor_copy(ang[:], i512[:]), reads=["i512"], writes=["f0"])
                    P.op("dve", lambda e: e.tensor_scalar(ang[:], ang[:], C("invf", 1), None, ALU.mult), reads=["f0", "cs"], writes=["f0"])
                    for (dst, dkey, shift) in ((S1, "f4", 0.0), (C1, "f3", float(np.pi / 2))):
                        P.op("dve", lambda e, shift=shift: e.tensor_scalar(t1[:], ang[:], shift, float(1.0 / (2 * np.pi)), ALU.add, ALU.mult),
                             reads=["f0"], writes=["f1"])
                        P.op("dve", lambda e: e.tensor_copy(i512[:], t1[:]), reads=["f1"], writes=["i512"])
                        P.op("dve", lambda e: e.tensor_copy(t1[:], i512[:]), reads=["i512"], writes=["f1"])
                        P.op("dve", lambda e: e.scalar_tensor_tensor(t1[:], t1[:], float(-2 * np.pi), ang[:], ALU.mult, ALU.add),
                             reads=["f1", "f0"], writes=["f1"])
                        if shift != 0.0:
                            P.op("dve", lambda e, shift=shift: e.tensor_scalar(t1[:], t1[:], shift, None, ALU.add), reads=["f1"], writes=["f1"])
                        P.op("act", lambda e, dst=dst: e.activation(dst[:], t1[:], AF.Sin), reads=["f1"], writes=[dkey])
                    P.op("dve", lambda e: e.tensor_scalar(S1[:], S1[:], C("sgn", 1), None, ALU.mult), reads=["f4", "cs"], writes=["f4"])
                    for (ps_, pk, xf, xk, swp, swk, tabo, outb, ok) in (
                            (qps, "ps0", qf, "f5", cps, "ps2", 0, qt, "b0"), (kps, "ps1", kf_, "f6", dps, "ps6", 128, kt, "b1")):
                        P.op("act", lambda e, xf=xf, ps_=ps_: e.activation(xf[:], ps_[:, :], AF.Identity), reads=[pk], writes=[xk])
                        P.op("pe", lambda e, swp=swp, xf=xf: e.matmul(swp[:, :], C("perm"), xf[:], start=True, stop=True),
                             reads=[xk, "cs"], writes=[swk])
                        P.op("dve", lambda e, xf=xf: e.tensor_tensor(xf[:], xf[:], C1[:], ALU.mult), reads=[xk, "f3"], writes=[xk])
                        P.op("dve", lambda e, swp=swp: e.tensor_tensor(t2[:], swp[:, :], S1[:], ALU.mult), reads=[swk, "f4"], writes=["f2"])
                        P.op("pool", lambda e, xf=xf: e.tensor_tensor(xf[:], xf[:], t2[:], ALU.add), reads=[xk, "f2"], writes=[xk])
                        for t in range(4):
                            P.op("dve" if t % 2 == 0 else "pool", lambda e, xf=xf, outb=outb, tabo=tabo, t=t: e.tensor_tensor(
                                outb[:, t * 128:(t + 1) * 128], xf[:, t * 128:(t + 1) * 128], tb[:, tabo:tabo + 128], ALU.mult),
                                reads=[xk, tkey], writes=[ok])
                    elast = lambda t: tb[:, 256:257]
                    ekey = tkey
                else:
                    y, ay, ee, ll, bcum, eb, enb = f512[0], f512[1], f512[2], f512[3], f512[4], f512[5], f512[6]
                    proj_fm(cps[0:16, :], Wt, wkeys, 768, 16, slot, "ps2")
                    P.op("act", lambda e: e.activation(a1f[:], cps[0:16, :], AF.Identity), reads=["ps2"], writes=["a1f"])
                    P.op("pe", lambda e: e.matmul(dps[:, :], tb[0:16, 0:128], a1f[:], start=True, stop=True),
                         reads=["a1f", tkey], writes=["ps6"])
                    P.op("dve", lambda e: e.tensor_scalar(y[:], dps[:, :], tb[:, 256:257], -1.0, ALU.add, ALU.mult),
                         reads=["ps6", tkey], writes=["f0"])
                    P.op("dve", lambda e: e.scalar_tensor_tensor(ay[:], y[:], -1.0, y[:], ALU.mult, ALU.max), reads=["f0"], writes=["f1"])
                    P.op("act", lambda e: e.activation(ee[:], ay[:], AF.Exp, scale=-1.0), reads=["f1"], writes=["f2"])
                    P.op("act", lambda e: e.activation(ll[:], ee[:], AF.Ln, bias=1.0), reads=["f2"], writes=["f3"])
                    P.op("dve", lambda e: e.scalar_tensor_tensor(ll[:], y[:], 0.0, ll[:], ALU.max, ALU.add), reads=["f0", "f3"], writes=["f3"])
                    P.op("dve", lambda e: e.tensor_scalar(ll[:], ll[:], -1.0 / 16.0, None, ALU.mult), reads=["f3"], writes=["f3"])
                    P.op("dve", lambda e: e.tensor_tensor_scan(bcum[:], C("reset", 512), ll[:], 0.0, ALU.mult, ALU.add),
                         reads=["f3", "cs"], writes=["f4"])
                    P.op("act", lambda e: e.activation(eb[:], bcum[:], AF.Exp), reads=["f4"], writes=["f5"])
                    P.op("act", lambda e: e.activation(enb[:], bcum[:], AF.Exp, scale=-1.0), reads=["f4"], writes=["f6"])
                    P.op("dve", lambda e: e.scalar_tensor_tensor(qt[:], qps[:, :], float(128 ** -0.5), eb[:], ALU.mult, ALU.mult),
                         reads=["ps0", "f5"], writes=["b0"])
                    P.op("dve", lambda e: e.tensor_tensor(kt[:], kps[:, :], enb[:], ALU.mult), reads=["ps1", "f6"], writes=["b1"])
                    elast = lambda t: eb[:, t * 128 + 127:t * 128 + 128]
                    ekey = "f5"
                if DBG == 2:
                    continue
                for t in range(4):
                    ts = slice(t * 128, (t + 1) * 128)
                    P.op("pe", lambda e, ts=ts: e.transpose(PSB[:, 0:128], kt[:, ts], CB("ident")), reads=["b1", "cb"], writes=["psb_k"])
                    P.op("act", lambda e: e.activation(ktok[:], PSB[:, 0:128], AF.Identity), reads=["psb_k"], writes=["ktok"])
                    proj_tm(PS[3][:, :], Wt, wkeys, 256, 512, slot, t, "ps3")
                    P.op("act", lambda e: e.activation(vbf[:], PS[3][:, 0:256], AF.Identity), reads=["ps3"], writes=["vbf"])
                    P.op("act", lambda e: e.activation(gs[:], PS[3][:, 256:512], AF.Silu), reads=["ps3"], writes=["gs"])
                    P.op("pe", lambda e, ts=ts: e.matmul(PS[4][:, 0:128], kt[:, ts], qt[:, ts], start=True, stop=True),
                         reads=["b0", "b1"], writes=["ps4s"])
                    P.op("dve", lambda e: e.tensor_tensor(PT[:], PS[4][:, 0:128], C("mincl"), ALU.mult), reads=["ps4s", "cs"], writes=["PT"])
                    if DBG == 3:
                        continue
                    P.op("pe", lambda e: e.matmul(PS[5][:, 0:256], PT[:], vbf[:], start=True, stop=False), reads=["PT", "vbf"], writes=["ps5"])
                    P.op("pe", lambda e, ts=ts: e.matmul(PS[5][:, 0:256], qt[:, ts], Sb[:], start=False, stop=True),
                         reads=["b0", "Sb"], writes=["ps5"])
                    P.op("pe", lambda e: e.matmul(PS[4][:, 128:384], ktok[:], vbf[:], start=True, stop=True),
                         reads=["ktok", "vbf"], writes=["ps4d"])
                    P.op("dve", lambda e: e.tensor_tensor(tmpS[:], PS[4][:, 128:384], S[:], ALU.add), reads=["ps4d", "S"], writes=["tmpS"])
                    P.op("act", lambda e, t=t: e.activation(S[:], tmpS[:], AF.Identity, scale=elast(t)), reads=["tmpS", ekey], writes=["S"])
                    P.op("pool", lambda e: e.tensor_copy(Sb[:], S[:]), reads=["S"], writes=["Sb"])
                    if DBG == 4:
                        continue
                    finish_tile(p, blk, t, PS[5][:, 0:256], "ps5", 256, gs[:], "gs", normg, tkey, mslot, kind == "ret")
                if DBG == 5:
                    continue
                store_mix(p, blk, mslot, 2)

        def dn_pass(p):
            Wt, wkeys, tb, tkey = load_w(p)
            normg = tb[:, 384:512]
            P.op("dve", lambda e: e.memset(S[:, 0:128], 0.0), writes=["S"])
            P.op("dve", lambda e: e.memset(Sb[:, 0:128], 0.0), writes=["Sb"])
            for s_ in range(3):
                P.op("dve", lambda e, s_=s_: e.memset(U[s_][:, 512:515], 0.0), writes=["U%d" % s_])
            P.op("act", lambda e: e.activation(col[:, 8:9], tb[:, 16:17], AF.Exp), reads=[tkey], writes=["col8"])
            P.op("dve", lambda e: e.tensor_scalar(col[:, 9:10], col[:, 8:9], -1.0, None, ALU.mult), reads=["col8"], writes=["col9"])
            for blk in range(NB):
                slot = get_x(blk)
                mslot = mcount[0] % 2
                mcount[0] += 1
                cvs = [f512[0], f512[1], f512[2]]
                for s_ in range(3):
                    ps_ = PS[s_ % 2]
                    pk = "ps%d" % (s_ % 2)
                    proj_fm(ps_[:, :], Wt, wkeys, s_ * 128, 128, slot, [pk, "pc%d_1" % (s_ % 2)] )
                    Us, uk = U[s_], "U%d" % s_
                    P.op("pool", lambda e, Us=Us: e.tensor_copy(Us[:, 0:3], Us[:, 512:515]), reads=[uk], writes=[uk + "h"])
                    P.op("act", lambda e, Us=Us, ps_=ps_: e.activation(Us[:, 3:515], ps_[:, :], AF.Identity), reads=[pk, uk + "h"], writes=[uk])
                    acc, ak = f512[3 + s_ % 2], "f%d" % (3 + s_ % 2)
                    P.op("dve", lambda e, Us=Us, acc=acc, s_=s_: e.tensor_scalar(acc[:], Us[:, 0:512], tb[:, s_ * 4:s_ * 4 + 1], None, ALU.mult),
                         reads=[uk, uk + "h", tkey], writes=[ak])
                    for i in range(1, 4):
                        P.op("dve", lambda e, Us=Us, acc=acc, s_=s_, i=i: e.scalar_tensor_tensor(
                            acc[:], Us[:, i:i + 512], tb[:, s_ * 4 + i:s_ * 4 + i + 1], acc[:], ALU.mult, ALU.add),
                            reads=[uk, uk + "h", ak, tkey], writes=[ak])
                    P.op("act", lambda e, acc=acc, s_=s_: e.activation(cvs[s_][:], acc[:], AF.Silu), reads=[ak], writes=["f%d" % s_])
                qn, kn = f512[5], f512[6]
                qb, kb_ = b512[0], b512[1]
                for (s_, dst, dk_, scl, bdst, bk) in ((0, qn, "f5", float(128 ** -0.5), qb, "b0"), (1, kn, "f6", 1.0, kb_, "b1")):
                    sq, sqk = f512[3], "f3"
                    P.op("act", lambda e, s_=s_: e.activation(sq[:], cvs[s_][:], AF.Square), reads=["f%d" % s_], writes=[sqk])
                    P.op("pe", lambda e: e.matmul(PS[2][:, :], C("ones"), sq[:], start=True, stop=True), reads=[sqk, "cs"], writes=["ps2"])
                    P.op("act", lambda e: e.activation(f512[4][:], PS[2][:, :], AF.Sqrt, bias=EPS), reads=["ps2"], writes=["f4"])
                    P.op("dve", lambda e: e.reciprocal(f512[4][:], f512[4][:]), reads=["f4"], writes=["f4"])
                    P.op("dve", lambda e, s_=s_, dst=dst, scl=scl: e.scalar_tensor_tensor(dst[:], cvs[s_][:], scl, f512[4][:], ALU.mult, ALU.mult),
                         reads=["f%d" % s_, "f4"], writes=[dk_])
                    P.op("pool", lambda e, dst=dst, bdst=bdst: e.tensor_copy(bdst[:], dst[:]), reads=[dk_], writes=[bk])
                vs = cvs[2]
                if DBG == 11:
                    continue
                for t in range(4):
                    proj_tm(PS[2][:, 0:130], Wt, wkeys, 384, 130, slot, t, "ps2")
                    P.op("act", lambda e, t=t: e.activation(dgs[t][:], PS[2][:, 0:128], AF.Silu), reads=["ps2"], writes=["dgs%d" % t])
                    P.op("act", lambda e, t=t: e.activation(c4["beta"][:, t:t + 1], PS[2][:, 128:129], AF.Sigmoid), reads=["ps2"], writes=["c4beta"])
                    P.op("dve", lambda e, t=t: e.tensor_scalar(c4["y"][:, t:t + 1], PS[2][:, 129:130], tb[:, 17:18], None, ALU.add),
                         reads=["ps2", tkey], writes=["c4y"])
                P.op("dve", lambda e: e.scalar_tensor_tensor(c4["ay"][:], c4["y"][:], -1.0, c4["y"][:], ALU.mult, ALU.max), reads=["c4y"], writes=["c4ay"])
                P.op("act", lambda e: e.activation(c4["e"][:], c4["ay"][:], AF.Exp, scale=-1.0), reads=["c4ay"], writes=["c4e"])
                P.op("act", lambda e: e.activation(c4["l"][:], c4["e"][:], AF.Ln, bias=1.0), reads=["c4e"], writes=["c4l"])
                P.op("dve", lambda e: e.scalar_tensor_tensor(c4["l"][:], c4["y"][:], 0.0, c4["l"][:], ALU.max, ALU.add), reads=["c4y", "c4l"], writes=["c4l"])
                P.op("dve", lambda e: e.tensor_scalar(c4["g"][:], c4["l"][:], col[:, 9:10], None, ALU.mult), reads=["c4l", "col9"], writes=["c4g"])
                if DBG == 12:
                    continue
                for t in range(4):
                    ts = slice(t * 128, (t + 1) * 128)
                    tk = lambda nm: "%s%d" % (nm, t)
                    gb, gbk = f512[3], "f3"
                    P.op("dve", lambda e, t=t: e.tensor_scalar(gb[:, 0:128], C("mincl"), c4["g"][:, t:t + 1], None, ALU.mult),
                         reads=["cs", "c4g"], writes=[gbk])
                    P.op("dve", lambda e, t=t: e.tensor_scalar(gb[:, 128:256], C("ident"), c4["beta"][:, t:t + 1], None, ALU.mult),
                         reads=["cs", "c4beta"], writes=[gbk])
                    P.op("pe", lambda e: e.matmul(PS[3][:, 0:256], C("ones"), gb[:, 0:256], start=True, stop=True), reads=[gbk, "cs"], writes=["ps3"])
                    P.op("pe", lambda e, t=t: e.matmul(PS[3][:, 256:257], C("mincl"), c4["g"][:, t:t + 1], start=True, stop=True),
                         reads=["c4g", "cs"], writes=["ps3c"])
                    gcc = c4["gc"][:, t:t + 1]
                    P.op("act", lambda e, gcc=gcc: e.activation(gcc, PS[3][:, 256:257], AF.Identity), reads=["ps3c"], writes=["c4gc"])
                    P.op("act", lambda e, t=t: e.activation(c4["gl"][:, t:t + 1], PS[3][:, 127:128], AF.Identity), reads=["ps3"], writes=["c4gl"])
                    Dji, Dij = f128[0], f128[1]
                    P.op("dve", lambda e, gcc=gcc: e.tensor_scalar(Dji[:], PS[3][:, 0:128], gcc, 0.0, ALU.subtract, ALU.min),
                         reads=["ps3", "c4gc"], writes=["g0"])
                    P.op("act", lambda e: e.activation(Dji[:], Dji[:], AF.Exp), reads=["g0"], writes=["g0"])
                    P.op("dve", lambda e, gcc=gcc: e.tensor_scalar(Dij[:], PS[3][:, 0:128], gcc, 0.0, ALU.subtract, ALU.max),
                         reads=["ps3", "c4gc"], writes=["g1"])
                    P.op("act", lambda e: e.activation(Dij[:], Dij[:], AF.Exp, scale=-1.0), reads=["g1"], writes=["g1"])
                    P.op("pe", lambda e, ts=ts: e.matmul(PS[4][:, 0:128], kb_[:, ts], kb_[:, ts], start=True, stop=True), reads=["b1"], writes=["ps4a"])
                    P.op("pe", lambda e, ts=ts: e.matmul(PS[4][:, 128:256], kb_[:, ts], qb[:, ts], start=True, stop=True), reads=["b0", "b1"], writes=["ps4b"])
                    t1_, t2_, t3_, t4_ = f128[2], f128[3], f128[4], f128[5]
                    P.op("dve", lambda e: e.tensor_tensor(t1_[:], PS[4][:, 0:128], Dji[:], ALU.mult), reads=["ps4a", "g0"], writes=["g2"])
                    P.op("dve", lambda e: e.tensor_tensor(t2_[:], PS[3][:, 128:256], C("mstrict"), ALU.mult), reads=["ps3", "cs"], writes=["g3"])
                    P.op("pool", lambda e, t=t: e.tensor_tensor(Nf[t][:], t1_[:], t2_[:], ALU.mult), reads=["g2", "g3"], writes=[tk("Nf")])
                    P.op("dve", lambda e: e.tensor_tensor(t3_[:], PS[4][:, 0:128], Dij[:], ALU.mult), reads=["ps4a", "g1"], writes=["g4"])
                    P.op("dve", lambda e, t=t: e.scalar_tensor_tensor(Af[t][:], C("mlow"), c4["beta"][:, t:t + 1], t3_[:], ALU.mult, ALU.mult),
                         reads=["g4", "cs", "c4beta"], writes=[tk("Af")])
                    P.op("dve", lambda e: e.tensor_tensor(t4_[:], PS[4][:, 128:256], Dji[:], ALU.mult), reads=["ps4b", "g0"], writes=["g5"])
                    P.op("pool", lambda e, t=t: e.tensor_tensor(dnb["QKm"][t][:], t4_[:], C("mincl"), ALU.mult), reads=["g5", "cs"], writes=[tk("QKm")])
                    Eg = f128[6]
                    P.op("act", lambda e: e.activation(Eg[:], PS[3][:, 0:128], AF.Exp), reads=["ps3"], writes=["g6"])
                    P.op("dve", lambda e, t=t, ts=ts: e.tensor_tensor(dnb["qg"][t][:], qn[:, ts], Eg[:], ALU.mult), reads=["f5", "g6"], writes=[tk("qg")])
                    P.op("act", lambda e, t=t: e.activation(c4["eg"][:, t:t + 1], c4["gc"][:, t:t + 1], AF.Exp), reads=["c4gc"], writes=["c4eg"])
                    P.op("dve", lambda e, t=t: e.tensor_tensor(c4["bg"][:, t:t + 1], c4["eg"][:, t:t + 1], c4["beta"][:, t:t + 1], ALU.mult),
                         reads=["c4eg", "c4beta"], writes=["c4bg"])
                    P.op("act", lambda e, t=t: e.activation(c4["kdc"][:, t:t + 1], c4["gc"][:, t:t + 1], AF.Exp, scale=-1.0, bias=c4["gl"][:, t:t + 1]),
                         reads=["c4gc", "c4gl"], writes=["c4kdc"])
                    P.op("act", lambda e, t=t: e.activation(c4["egl"][:, t:t + 1], c4["gl"][:, t:t + 1], AF.Exp), reads=["c4gl"], writes=["c4egl"])
                    P.op("pe", lambda e, ts=ts: e.transpose(PS[4][:, 256:384], kn[:, ts], C("ident")), reads=["f6", "cs"], writes=["ps4c"])
                    P.op("pe", lambda e, ts=ts: e.transpose(PS[4][:, 384:512], vs[:, ts], C("ident")), reads=["f2", "cs"], writes=["ps4d"])
                    P.op("dve", lambda e, t=t: e.tensor_scalar(dnb["kb"][t][:], PS[4][:, 256:384], c4["bg"][:, t:t + 1], None, ALU.mult),
                         reads=["ps4c", "c4bg"], writes=[tk("kb")])
                    P.op("act", lambda e, t=t: e.activation(dnb["kd"][t][:], PS[4][:, 256:384], AF.Identity, scale=c4["kdc"][:, t:t + 1]),
                         reads=["ps4c", "c4kdc"], writes=[tk("kd")])
                    P.op("act", lambda e, t=t: e.activation(dnb["vb"][t][:], PS[4][:, 384:512], AF.Identity, scale=c4["beta"][:, t:t + 1]),
                         reads=["ps4d", "c4beta"], writes=[tk("vb")])
                    for (nm, src, m) in (("a", Af, "bd16"), ("n", Nf, "bd16"), ("L32", Af, "off32"), ("N32", Nf, "off32"),
                                         ("L64", Af, "off64"), ("N64", Nf, "off64"), ("L128", Af, "off128")):
                        P.op("pool", lambda e, nm=nm, src=src, m=m, t=t: e.tensor_tensor(dnb[nm][t][:], src[t][:], C(m), ALU.mult),
                             reads=[tk("Af") if src is Af else tk("Nf"), "cs"], writes=[tk(nm)])
                    for (nm, src, Mf) in (("Z", Af, Zf), ("Y", Nf, Yf)):
                        sk = tk("Af") if src is Af else tk("Nf")
                        P.op("pool", lambda e, src=src, Mf=Mf, t=t: e.tensor_tensor(Mf[t][:], src[t][:], C("bd16"), ALU.mult),
                             reads=[sk, "cs"], writes=["%sf%d" % (nm, t)])
                        P.op("pool", lambda e, Mf=Mf, t=t: e.tensor_tensor(Mf[t][:], C("ident"), Mf[t][:], ALU.subtract),
                             reads=["%sf%d" % (nm, t), "cs"], writes=["%sf%d" % (nm, t)])
                        P.op("pool", lambda e, Mf=Mf, nm=nm, t=t: e.tensor_copy(dnb[nm][t][:], Mf[t][:]), reads=["%sf%d" % (nm, t)], writes=[tk(nm)])

                if DBG == 13:
                    continue
                rnd = [0]

                PSBf = PSB[:, :].bitcast(F32)

                def slotp(t, h, r=None):
                    if DBG != 38:
                        bank = ((PS[5], PS[6]), (PS[0], PS[1]), (PS[2], PS[3]), (PS[4], None))[t][h]
                        return PSBf[:, 0:128] if bank is None else bank[:, 0:128]
                    if t % 2 == 0:
                        return (PS[5] if h == 0 else PS[6])[:, 0:128]
                    return (PS[0] if h == 0 else PS[1])[:, 0:128]

                def pkey(t, h, r=None):
                    if DBG != 38:
                        extra = {(2, 0): ["ps2"], (2, 1): ["ps3", "ps3c", "ps3o", "ps3d"], (3, 0): ["ps4a", "ps4b", "ps4c", "ps4d"],
                                 (3, 1): ["psb_k", "psb_m0", "psb_m1"]}.get((t, h), [])
                        return ["pc%d_%d" % (h, t)] + extra
                    return ["pc%d_%d" % (h, t % 2)]

                def mm(t, h, lhs, rhs, lk, rk):
                    if DBG == 21 and lk[0] in "YZ":
                        return
                    sp_, pk_ = slotp(t, h), pkey(t, h)
                    wk_ = pk_
                    P.op("pe", lambda e: e.matmul(sp_, lhs, rhs, start=True, stop=True), reads=[lk, rk], writes=wk_)

                def ev_copy(eng, t, h, dst, dk_):
                    sp_, pk_ = slotp(t, h), pkey(t, h)
                    if eng == "act" and DBG != 30:
                        P.op("act", lambda e: e.activation(dst, sp_, AF.Identity), reads=pk_, writes=[dk_])
                    else:
                        P.op("dve", lambda e: e.tensor_copy(dst, sp_), reads=pk_, writes=[dk_])

                def ev_comb(t, h, nm, op):
                    if DBG == 20:
                        return
                    if DBG in (19, 21):
                        ev_copy("dve", t, h, dnb[nm][t][:], "%s%d" % (nm, t))
                        return
                    sc = 1.0 if op == "add" else -1.0
                    Mf = Zf[t] if nm == "Z" else Yf[t]
                    fk = "%sf%d" % (nm, t)
                    sp_, pk_ = slotp(t, h), pkey(t, h)
                    P.op("dve", lambda e: e.tensor_tensor(Mf[:], Mf[:], sp_, ALU.add if op == "add" else ALU.subtract),
                         reads=pk_ + [fk], writes=[fk])
                    P.op("dve" if DBG >= 18 else "pool", lambda e: e.tensor_copy(dnb[nm][t][:], Mf[:]), reads=[fk], writes=["%s%d" % (nm, t)])

                def chain_for(tl):
                    pa, pn = ["a"] * 4, ["n"] * 4
                    for lvl in range(3):
                        na, nn = ("p2a", "p2n") if lvl % 2 == 0 else ("p4a", "p4n")
                        rnd[0] += 1
                        for t in tl:
                            tk = lambda nm: "%s%d" % (nm, t)
                            mm(t, 0, dnb[pn[t]][t][:], dnb[pa[t]][t][:], tk(pn[t]), tk(pa[t]))
                            mm(t, 1, dnb[pa[t]][t][:], dnb[pn[t]][t][:], tk(pa[t]), tk(pn[t]))
                            ev_copy("act", t, 0, dnb[na][t][:], tk(na))
                            ev_copy("dve", t, 1, dnb[nn][t][:], tk(nn))
                        if DBG == 15:
                            break
                        rnd[0] += 1
                        for t in tl:
                            tk = lambda nm: "%s%d" % (nm, t)
                            if DBG == 22:
                                mm(t, 0, dnb["n"][t][:], dnb["a"][t][:], tk("n"), tk("a"))
                                mm(t, 1, dnb["a"][t][:], dnb["n"][t][:], tk("a"), tk("n"))
                                continue
                            if DBG == 23:
                                mm(t, 0, dnb["n"][t][:], dnb[na][t][:], tk("n"), tk(na))
                                mm(t, 1, dnb["a"][t][:], dnb[nn][t][:], tk("a"), tk(nn))
                                continue
                            mm(t, 0, dnb["Y"][t][:], dnb[na][t][:], tk("Y"), tk(na))
                            mm(t, 1, dnb["Z"][t][:], dnb[nn][t][:], tk("Z"), tk(nn))
                            ev_comb(t, 0, "Z", "add")
                            ev_comb(t, 1, "Y", "add")
                            pa[t], pn[t] = na, nn
                        if DBG in (18, 19, 20, 21, 22, 23):
                            break
                    if DBG in (15, 16, 18, 19, 20, 21, 22, 23, 30, 31):
                        return
                    for (Ln, Nn) in (("L32", "N32"), ("L64", "N64")):
                        rnd[0] += 1
                        for t in tl:
                            tk = lambda nm: "%s%d" % (nm, t)
                            mm(t, 0, dnb[Nn][t][:], dnb["Z"][t][:], tk(Nn), tk("Z"))
                            mm(t, 1, dnb[Ln][t][:], dnb["Y"][t][:], tk(Ln), tk("Y"))
                            ev_copy("act", t, 0, dnb["W1"][t][:], tk("W1"))
                            ev_copy("dve", t, 1, dnb["V1"][t][:], tk("V1"))
                        rnd[0] += 1
                        for t in tl:
                            tk = lambda nm: "%s%d" % (nm, t)
                            mm(t, 0, dnb["Y"][t][:], dnb["W1"][t][:], tk("Y"), tk("W1"))
                            mm(t, 1, dnb["Z"][t][:], dnb["V1"][t][:], tk("Z"), tk("V1"))
                            ev_comb(t, 0, "Z", "sub")
                            ev_comb(t, 1, "Y", "sub")
                    rnd[0] += 1
                    for t in tl:
                        tk = lambda nm: "%s%d" % (nm, t)
                        mm(t, 1, dnb["L128"][t][:], dnb["Y"][t][:], tk("L128"), tk("Y"))
                        ev_copy("act", t, 1, dnb["V1"][t][:], tk("V1"))
                    rnd[0] += 1
                    for t in tl:
                        tk = lambda nm: "%s%d" % (nm, t)
                        mm(t, 1, dnb["Z"][t][:], dnb["V1"][t][:], tk("Z"), tk("V1"))
                        ev_comb(t, 1, "Y", "sub")
                        mm(t, 0, dnb["kb"][t][:], dnb["Y"][t][:], tk("kb"), tk("Y"))
                        P.op("act", lambda e, t=t, sp_=slotp(t, 0): e.activation(dnb["nwT"][t][:], sp_, AF.Identity, scale=-1.0),
                             reads=pkey(t, 0), writes=[tk("nwT")])

                Yc = ["Y"] * 4
                if DBG == 33:
                    chain_for([0, 1, 2, 3])
                elif DBG == 35:
                    for tsel in range(4):
                        chain_for([tsel])
                elif DBG != 38:
                    chain_for([0, 1, 2, 3])
                else:
                    chain_for([0, 1])
                    chain_for([2, 3])
                if DBG == 14:
                    continue
                for t in range(4):
                    tk = lambda nm: "%s%d" % (nm, t)
                    Yt = dnb[Yc[t]][t]
                    P.op("pe", lambda e, Yt=Yt, t=t: e.matmul(PS[3][:, 0:128], Yt[:], dnb["vb"][t][:], start=True, stop=False),
                         reads=[tk(Yc[t]), tk("vb")], writes=["ps3"])
                    P.op("pe", lambda e, t=t: e.matmul(PS[3][:, 0:128], dnb["nwT"][t][:], Sb[:, 0:128], start=False, stop=True),
                         reads=[tk("nwT"), "Sb"], writes=["ps3"])
                    P.op("act", lambda e, t=t: e.activation(dnb["vnew"][t][:], PS[3][:, 0:128], AF.Identity), reads=["ps3"], writes=[tk("vnew")])
                    P.op("pe", lambda e, t=t: e.matmul(PS[3][:, 128:256], dnb["qg"][t][:], Sb[:, 0:128], start=True, stop=False),
                         reads=[tk("qg"), "Sb"], writes=["ps3o"])
                    P.op("pe", lambda e, t=t: e.matmul(PS[3][:, 128:256], dnb["QKm"][t][:], dnb["vnew"][t][:], start=False, stop=True),
                         reads=[tk("QKm"), tk("vnew")], writes=["ps3o"])
                    P.op("pe", lambda e, t=t: e.matmul(PS[3][:, 256:384], dnb["kd"][t][:], dnb["vnew"][t][:], start=True, stop=True),
                         reads=[tk("kd"), tk("vnew")], writes=["ps3d"])
                    P.op("dve", lambda e, t=t: e.scalar_tensor_tensor(S[:, 0:128], S[:, 0:128], c4["egl"][:, t:t + 1], PS[3][:, 256:384], ALU.mult, ALU.add),
                         reads=["S", "c4egl", "ps3d"], writes=["S"])
                    P.op("pool", lambda e: e.tensor_copy(Sb[:, 0:128], S[:, 0:128]), reads=["S"], writes=["Sb"])
                    finish_tile(p, blk, t, PS[3][:, 128:256], "ps3o", 128, dgs[t][:], "dgs%d" % t, normg, tkey, mslot, False)
                store_mix(p, blk, mslot, 1)

        for p in passes:
            if PASSES[p][0] == "dn":
                dn_pass(p)
            else:
                la_pass(p)
        P.emit(nc)
    return nc


_ST = np.concatenate([[0], np.cumsum([512, 512, 1024, 1024, 512, 512, 1024, 1024, 16, 6144, 2048, 16, 16])]).astype(np.int64)


def k1_inputs(xT_b, pos_b, l, hg, inp, passes=None, single=False):
    w_in = inp["w_in"][l]
    m = {"xT": xT_b, "pos": np.ascontiguousarray(pos_b), "cst": CST}
    rep = lambda v, n=128: np.broadcast_to(np.asarray(v, np.float32)[None, :], (n, len(v)))
    for p, (kind, ncols) in enumerate(PASSES):
        if passes is not None and p not in passes:
            continue
        tab = np.zeros((128, NTAB), np.float32)
        if kind == "ret":
            h = hg
            cols = np.concatenate([np.arange(_ST[0] + h * 128, _ST[0] + (h + 1) * 128), np.arange(_ST[1] + h * 128, _ST[1] + (h + 1) * 128),
                                   np.arange(_ST[2] + h * 256, _ST[2] + (h + 1) * 256), np.arange(_ST[3] + h * 256, _ST[3] + (h + 1) * 256),
                                   np.arange(_ST[8], _ST[8] + 16)])
            lg = np.log(np.float32(1.0) - np.power(np.float32(2.0), np.float32(-5.0 - h)))
            i = np.arange(128, dtype=np.float32)
            rq = np.exp((i + 1.0) * lg) * (128 ** -0.5)
            rk = np.exp(-(i + 1.0) * lg)
            tab[:, 0:128] = rep(rq)
            tab[:, 128:256] = rep(rk)
            tab[:, 256] = np.exp(128.0 * lg)
            tab[:, 384:640] = rep(inp["ret_norm_g"][l][h * 256:(h + 1) * 256])
        elif kind == "gla":
            h = hg
            cols = np.concatenate([np.arange(_ST[4] + h * 128, _ST[4] + (h + 1) * 128), np.arange(_ST[5] + h * 128, _ST[5] + (h + 1) * 128),
                                   np.arange(_ST[6] + h * 256, _ST[6] + (h + 1) * 256), np.arange(_ST[7] + h * 256, _ST[7] + (h + 1) * 256),
                                   np.arange(_ST[8], _ST[8] + 16)])
            tab[0:16, 0:128] = inp["gla_w_a2"][l][:, h * 128:(h + 1) * 128]
            tab[:, 256] = inp["gla_b_a"][l][h * 128:(h + 1) * 128]
            tab[:, 384:640] = rep(inp["gla_norm_g"][l][h * 256:(h + 1) * 256])
        else:
            idx = [1, 3, 4, 5].index(p)
            h = hg * 4 + idx
            b0 = _ST[9]
            cols = np.concatenate([np.arange(b0 + h * 128, b0 + (h + 1) * 128), np.arange(b0 + 2048 + h * 128, b0 + 2048 + (h + 1) * 128),
                                   np.arange(b0 + 4096 + h * 128, b0 + 4096 + (h + 1) * 128),
                                   np.arange(_ST[10] + h * 128, _ST[10] + (h + 1) * 128), [_ST[11] + h], [_ST[12] + h]])
            cw = inp["dn_conv_w"][l]
            for s_ in range(3):
                tab[:, s_ * 4:(s_ + 1) * 4] = cw[:, s_ * 2048 + h * 128:s_ * 2048 + (h + 1) * 128].T
            tab[:, 16] = inp["dn_a_log"][l][h]
            tab[:, 17] = inp["dn_dt_bias"][l][h]
            tab[:, 384:512] = rep(inp["dn_norm_g"][l][h * 128:(h + 1) * 128])
        wp = w_in[:, cols]
        m["w" if single else "w%d" % p] = np.ascontiguousarray(wp.reshape(NCH, 128, ncols).transpose(1, 0, 2).reshape(128, NCH * ncols))
        m["tab" if single else "tab%d" % p] = tab
    return m


def mix_row_perm():
    perm = []
    for hg in range(4):
        rows = np.zeros(1024, np.int64)
        rows[0:256] = np.arange(hg * 256, (hg + 1) * 256)
        rows[256:512] = 1024 + np.arange(hg * 256, (hg + 1) * 256)
        for i in range(4):
            h = hg * 4 + i
            rows[512 + i * 128:512 + (i + 1) * 128] = 2048 + np.arange(h * 128, (h + 1) * 128)
        perm.append(rows)
    return np.concatenate(perm)


def block_x(xT, nb):
    return np.ascontiguousarray(xT.reshape(NCH, 128, nb, 512).transpose(2, 1, 0, 3).reshape(nb, 128, NCH * 512))


def unblock_x(xb):
    nb = xb.shape[0]
    return np.ascontiguousarray(xb.reshape(nb, 128, NCH, 512).transpose(2, 1, 0, 3).reshape(D, nb * 512))


def build_k2(NH=2):
    nc = bass.Bass("TRN2", target_bir_lowering=False)
    mixc = nc.dram_tensor("mixc", [NH, 128, NCH * 512], BF16, kind="ExternalInput").ap()
    xres = nc.dram_tensor("xres", [NH, 128, NCH * 512], F32, kind="ExternalInput").ap()
    wo = nc.dram_tensor("wo", [8, 128, 4 * NCH * 128], F32, kind="ExternalInput").ap()
    lgb = nc.dram_tensor("lgb", [128, 2 * NCH], F32, kind="ExternalInput").ap()
    cst = nc.dram_tensor("cst", [128, NCST], F32, kind="ExternalInput").ap()
    xo = nc.dram_tensor("xo", [NH, 128, NCH * 512], F32, kind="ExternalOutput").ap()
    P = Prog()
    with contextlib.ExitStack() as st:
        def sb(name, shape, dt=F32):
            return st.enter_context(nc.sbuf_tensor(name, shape, dt))
        cs = sb("cs", [128, NCST])
        gb = sb("gb", [128, 2 * NCH])
        mx = sb("mx", [128, NCH, 512], BF16)
        z = sb("z", [128, NCH, 512])
        wt = [sb("wt%d" % i, [128, 4, NCH, 128], BF16) for i in range(2)]
        sq = sb("sq", [128, 512])
        s1 = sb("s1", [128, 512])
        s2 = sb("s2", [128, 512])
        mean = sb("mean", [128, 512])
        rstd = sb("rstd", [128, 512])
        PS = [st.enter_context(nc.psum_tensor("ps%d" % i, [128, 512], F32)) for i in range(4)]
        ones = cs[:, COFF["ones"]:COFF["ones"] + 128]
        P.dma("sp", cs[:], cst[:, :], writes=["cs"])
        P.dma("sp", gb[:], lgb[:, :], writes=["gb"])
        wcount = 0
        for half in range(NH):
            for q in range(2):
                P.dma("sp", mx[:, q * 16:(q + 1) * 16, :], mixc[half][:, q * 8192:(q + 1) * 8192].rearrange("p (c t) -> p c t", t=512),
                      writes=["mx%d" % q])
            for q in range(4):
                P.dma("sp", z[:, q * 8:(q + 1) * 8, :], xres[half][:, q * 4096:(q + 1) * 4096].rearrange("p (c t) -> p c t", t=512),
                      writes=["z%d" % d_ for d_ in range(q * 8, (q + 1) * 8)])
            for dc in range(NCH):
                dcg, j = dc // 4, dc % 4
                w = wt[(wcount // 4) % 2]
                wk = "wt%d" % ((wcount // 4) % 2)
                if j == 0:
                    for hh in range(2):
                        P.dma("pool", w[:, hh * 2:(hh + 1) * 2, :, :],
                              wo[dcg][:, hh * 8192:(hh + 1) * 8192].rearrange("p (j c n) -> p j c n", j=2, n=128), writes=[wk + "_%d" % hh])
                wcount += 1
                ps = PS[dc % 2]
                pk = "ps%d" % (dc % 2)
                for ec in range(NCH):
                    P.op("pe", lambda e, ec=ec, w=w, ps=ps, j=j: e.matmul(ps[:, :], w[:, j, ec, :], mx[:, ec, :], start=(ec == 0), stop=(ec == NCH - 1)),
                         reads=[wk + "_%d" % (j // 2), "mx0", "mx1"], writes=[pk])
                zk = "z%d" % dc
                P.op("dve", lambda e, dc=dc, ps=ps: e.scalar_tensor_tensor(z[:, dc, :], z[:, dc, :], ALPHA, ps[:, :], ALU.mult, ALU.add),
                     reads=[zk, pk], writes=[zk])
                P.op("act", lambda e, dc=dc: e.activation(sq[:], z[:, dc, :], AF.Square), reads=[zk], writes=["sq"])
                P.op("pe", lambda e, dc=dc: e.matmul(PS[2][:, :], ones, z[:, dc, :], start=True, stop=True), reads=[zk, "cs"], writes=["ps2"])
                P.op("pe", lambda e: e.matmul(PS[3][:, :], ones, sq[:], start=True, stop=True), reads=["sq", "cs"], writes=["ps3"])
                if dc == 0:
                    P.op("dve", lambda e: e.tensor_copy(s1[:], PS[2][:, :]), reads=["ps2"], writes=["s1"])
                    P.op("dve", lambda e: e.tensor_copy(s2[:], PS[3][:, :]), reads=["ps3"], writes=["s2"])
                else:
                    P.op("dve", lambda e: e.tensor_tensor(s1[:], s1[:], PS[2][:, :], ALU.add), reads=["ps2", "s1"], writes=["s1"])
                    P.op("dve", lambda e: e.tensor_tensor(s2[:], s2[:], PS[3][:, :], ALU.add), reads=["ps3", "s2"], writes=["s2"])
            P.op("dve", lambda e: e.tensor_scalar(mean[:], s1[:], 1.0 / D, None, ALU.mult), reads=["s1"], writes=["mean"])
            P.op("dve", lambda e: e.tensor_tensor(sq[:], mean[:], mean[:], ALU.mult), reads=["mean"], writes=["sq"])
            P.op("dve", lambda e: e.scalar_tensor_tensor(rstd[:], s2[:], 1.0 / D, sq[:], ALU.mult, ALU.subtract), reads=["s2", "sq"], writes=["rstd"])
            P.op("act", lambda e: e.activation(rstd[:], rstd[:], AF.Sqrt, bias=LN_EPS), reads=["rstd"], writes=["rstd"])
            P.op("dve", lambda e: e.reciprocal(rstd[:], rstd[:]), reads=["rstd"], writes=["rstd"])
            for dc in range(NCH):
                zk = "z%d" % dc
                P.op("dve", lambda e, dc=dc: e.tensor_tensor(z[:, dc, :], z[:, dc, :], mean[:], ALU.subtract), reads=[zk, "mean"], writes=[zk])
                P.op("pool", lambda e, dc=dc: e.tensor_tensor(z[:, dc, :], z[:, dc, :], rstd[:], ALU.mult), reads=[zk, "rstd"], writes=[zk])
                P.op("dve", lambda e, dc=dc: e.tensor_scalar(z[:, dc, :], z[:, dc, :], gb[:, dc:dc + 1], gb[:, NCH + dc:NCH + dc + 1], ALU.mult, ALU.add),
                     reads=[zk, "gb"], writes=[zk])
                if dc % 8 == 7:
                    q = dc // 8
                    P.dma("sp", xo[half][:, q * 4096:(q + 1) * 4096].rearrange("p (c t) -> p c t", t=512), z[:, q * 8:(q + 1) * 8, :],
                          reads=["z%d" % d_ for d_ in range(q * 8, (q + 1) * 8)])
        P.emit(nc)
    return nc


def k2_weights(inp, l):
    perm = mix_row_perm()
    w = inp["w_out"][l][perm]
    wo = np.ascontiguousarray(w.reshape(NCH, 128, 8, 4, 128).transpose(2, 1, 3, 0, 4).reshape(8, 128, 4 * NCH * 128))
    lgb = np.ascontiguousarray(np.concatenate([inp["ln_g"][l].reshape(NCH, 128).T, inp["ln_b"][l].reshape(NCH, 128).T], axis=1).astype(np.float32))
    return wo, lgb


def kernel(**inp):
    inp = {k: np.asarray(v) for k, v in inp.items()}
    x = inp["x"]
    B = x.shape[0]
    NB = SEQ // 512
    pos = inp["positions"].astype(np.int32)
    xblk = [block_x(np.ascontiguousarray(x[b].T), NB) for b in range(B)]
    progs = {"ret": build_k1(SEQ, [0], single=True), "gla": build_k1(SEQ, [2], single=True), "dn": build_k1(SEQ, [1], single=True)}
    nc2 = build_k2(2)
    for l in range(DEPTH):
        mixs = [np.zeros((NB, 128, 8, 512), ml_dtypes.bfloat16) for _ in range(8)]
        for p in range(6):
            kind = PASSES[p][0]
            in_maps = [k1_inputs(xblk[c // 4], pos[c // 4:c // 4 + 1], l, c % 4, inp, passes=[p], single=True) for c in range(8)]
            r1 = run_bass_kernel_spmd(progs[kind], in_maps, core_ids=list(range(8)))
            rc0 = PASS_ROWS[p] // 128
            n = 1 if kind == "dn" else 2
            for c in range(8):
                mixs[c][:, :, rc0:rc0 + n, :] = np.asarray(r1.results[c]["mixT"]).reshape(NB, 128, n, 512)
            del in_maps, r1
        wo, lgb = k2_weights(inp, l)
        in2 = []
        for c in range(8):
            b, sq_ = c // 4, c % 4
            mixc = np.ascontiguousarray(np.concatenate([mixs[b * 4 + hg][sq_ * 2:sq_ * 2 + 2] for hg in range(4)], axis=2).reshape(2, 128, NCH * 512))
            in2.append({"mixc": mixc, "xres": np.ascontiguousarray(xblk[b][sq_ * 2:sq_ * 2 + 2]), "wo": wo, "lgb": lgb, "cst": CST})
        r2 = run_bass_kernel_spmd(nc2, in2, core_ids=list(range(8)))
        xblk = [np.ascontiguousarray(np.concatenate([np.asarray(r2.results[b * 4 + s]["xo"]) for s in range(4)], axis=0)) for b in range(B)]
        del in2, r2, mixs
    out = np.stack([unblock_x(xblk[b]).T for b in range(B)], axis=0).astype(np.float32)
    return out
```
